# Optimizing a Trainium2 kernel written in Bass

```python
import jax, jax.numpy as jnp
from jax import lax
import numpy as np

D_MODEL = 1024
BATCH = 8
SEQ = 2048
DEPTH = 4

CTX_LEN = 256
GRID_W = 64
N_MIXERS = 3
ALPHA = (2 * DEPTH) ** 0.25
BETA = (8 * DEPTH) ** -0.25
LN_EPS = 1e-6
RMS_EPS = 1e-6
N_MOD = 6
FFN_HIDDEN = -(-8 * D_MODEL // (3 * 256)) * 256
POOL_WINDOWS = (2, 4, 8, 16)
N_POOL_GROUPS = len(POOL_WINDOWS)
POOL_GROUP_DIM = D_MODEL // N_POOL_GROUPS
HGRN_HEAD_DIM = 128
HGRN_HEADS = D_MODEL // HGRN_HEAD_DIM
HGRN_CHUNK = 64
NA_HEAD_DIM = 64
NA_HEADS = D_MODEL // NA_HEAD_DIM
NA_ROWS = 8
NA_COLS = 16
NA_QBLOCK = 16
NA_KBAND = 32
N_POOL_LAYERS = len(range(0, DEPTH, N_MIXERS))
N_HGRN_LAYERS = len(range(1, DEPTH, N_MIXERS))
N_NA_LAYERS = len(range(2, DEPTH, N_MIXERS))

kernel_name = 'hybrid_pool_hgrn2_natten_dit_trunk'


def layer_norm(x, g, b):
    xf = x.astype(jnp.float32)
    mu = jnp.mean(xf, axis=-1, keepdims=True)
    var = jnp.mean(jnp.square(xf - mu), axis=-1, keepdims=True)
    return ((xf - mu) * lax.rsqrt(var + LN_EPS) * g + b).astype(x.dtype)


def swiglu(h, w_in, w_out):
    gt, up = jnp.split(h @ w_in, 2, axis=-1)
    return (jax.nn.silu(gt) * up) @ w_out


def centred_window_mean(h, w):
    L = h.shape[1]
    lo = w // 2
    hi = w - 1 - lo
    t = np.arange(L)
    start = np.clip(t - lo, 0, L)
    end = np.clip(t + hi + 1, 0, L)
    hf = h.astype(jnp.float32)
    cs = jnp.concatenate([jnp.zeros_like(hf[:, :1]), jnp.cumsum(hf, axis=1)], axis=1)
    cnt = jnp.asarray((end - start).astype(np.float32))
    return ((cs[:, end] - cs[:, start]) / cnt[None, :, None]).astype(h.dtype)


def multiscale_pool(h, w_group, scale):
    B, L, D = h.shape
    outs = []
    for g, w in enumerate(POOL_WINDOWS):
        hg = h[..., g * POOL_GROUP_DIM:(g + 1) * POOL_GROUP_DIM]
        outs.append(centred_window_mean(hg, w) - hg)
    p = jnp.stack(outs, axis=2)
    y = jnp.einsum('blgc,gcd->blgd', p, w_group).reshape(B, L, D)
    return y * scale


def pool_mixer(hx, hc, w_group, scale, need_ctx):
    ox = multiscale_pool(hx, w_group, scale)
    oc = multiscale_pool(hc, w_group, scale) if need_ctx else None
    return ox, oc


def chunk_gated_scan(q, k, v, log_f, s0):
    B, H, L, DK = q.shape
    DV = v.shape[-1]
    C = HGRN_CHUNK
    n = L // C

    def chunks(a):
        return jnp.moveaxis(a.reshape(B, H, n, C, a.shape[-1]), 2, 0)

    lower = jnp.asarray(np.tril(np.ones((C, C), dtype=bool)))

    def step(s, inp):
        q_c, k_c, v_c, g_c = inp
        b = jnp.cumsum(g_c, axis=2)
        rel = jnp.where(lower[:, :, None], b[:, :, :, None, :] - b[:, :, None, :, :], -jnp.inf)
        a = jnp.einsum('bhtd,bhsd,bhtsd->bhts', q_c, k_c, jnp.exp(rel))
        o = jnp.einsum('bhts,bhse->bhte', a, v_c) + jnp.einsum('bhtd,bhde->bhte', q_c * jnp.exp(b), s)
        b_end = b[:, :, -1:, :]
        s_new = jnp.exp(b_end)[:, :, 0, :, None] * s + jnp.einsum('bhsd,bhse->bhde', k_c * jnp.exp(b_end - b), v_c)
        return s_new, o

    s_fin, o = lax.scan(step, s0, (chunks(q), chunks(k), chunks(v), chunks(log_f)))
    o = jnp.moveaxis(o, 0, 2).reshape(B, H, L, DV)
    return o, s_fin


def hgrn2_mixer(hx, hc, w_in, lb_fwd, lb_bwd, norm_w, w_out, need_ctx):
    def heads(a):
        B_, L_, _ = a.shape
        return a.reshape(B_, L_, HGRN_HEADS, HGRN_HEAD_DIM).transpose(0, 2, 1, 3).astype(jnp.float32)

    def project(h):
        q, v, zf, zb, g = jnp.split(h @ w_in, 5, axis=-1)
        return heads(jax.nn.silu(q)), heads(v), heads(zf), heads(zb), g

    def gate(z, lb):
        lb = lb.reshape(HGRN_HEADS, 1, HGRN_HEAD_DIM)
        f = lb + (1.0 - lb) * jax.nn.sigmoid(z)
        return 1.0 - f, jnp.log(f)

    def flip(a):
        return jnp.flip(a, axis=2)

    qx, vx, zfx, zbx, gx = project(hx)
    qc, vc, zfc, zbc, gc = project(hc)
    s0 = jnp.zeros((hc.shape[0], HGRN_HEADS, HGRN_HEAD_DIM, HGRN_HEAD_DIM), jnp.float32)
    kcf, lcf = gate(zfc, lb_fwd)
    kxf, lxf = gate(zfx, lb_fwd)
    oc_f, sc_f = chunk_gated_scan(qc, kcf, vc, lcf, s0)
    ox_f, _ = chunk_gated_scan(qx, kxf, vx, lxf, sc_f)
    kcb, lcb = gate(zbc, lb_bwd)
    kxb, lxb = gate(zbx, lb_bwd)
    oc_b, sc_b = chunk_gated_scan(flip(qc), flip(kcb), flip(vc), flip(lcb), s0)
    ox_b, _ = chunk_gated_scan(flip(qx), flip(kxb), flip(vx), flip(lxb), sc_b)

    def readout(o, g, dtype):
        B_, H_, L_, E_ = o.shape
        o = o * lax.rsqrt(jnp.mean(o * o, axis=-1, keepdims=True) + RMS_EPS)
        o = o.transpose(0, 2, 1, 3).reshape(B_, L_, H_ * E_).astype(dtype)
        return (o * norm_w * jax.nn.silu(g)) @ w_out

    ox = readout(ox_f + flip(ox_b), gx, hx.dtype)
    oc = readout(oc_f + flip(oc_b), gc, hc.dtype) if need_ctx else None
    return ox, oc


def na_mixer(hx, hc, w_qkv, rpb, w_out, need_ctx):
    B, L, D = hx.shape
    rows = L // GRID_W
    kr = min(NA_ROWS, rows)
    scale = NA_HEAD_DIM ** -0.5

    def qkv(h):
        B_, L_, _ = h.shape
        z = (h @ w_qkv).reshape(B_, L_, 3, NA_HEADS, NA_HEAD_DIM)
        return [jnp.moveaxis(z[:, :, m], 1, 2) for m in range(3)]

    qx, kx, vx = qkv(hx)
    qc, kc, vc = qkv(hc)

    def grid(a):
        return a.reshape(B, NA_HEADS, rows, GRID_W, NA_HEAD_DIM)

    qg, kg, vg = grid(qx), grid(kx), grid(vx)
    n_cb = GRID_W // NA_QBLOCK
    qcol = np.arange(GRID_W).reshape(n_cb, NA_QBLOCK)
    band_start = np.clip(np.arange(n_cb) * NA_QBLOCK - NA_COLS // 2, 0, GRID_W - NA_KBAND)
    kcol = band_start[:, None] + np.arange(NA_KBAND)
    win_start = np.clip(qcol - NA_COLS // 2, 0, GRID_W - NA_COLS)
    col_ok = (kcol[:, None, :] >= win_start[:, :, None]) & (kcol[:, None, :] < win_start[:, :, None] + NA_COLS)
    dcol_idx = np.clip(kcol[:, None, :] - qcol[:, :, None] + NA_COLS - 1, 0, 2 * NA_COLS - 2)
    col_ok = jnp.asarray(col_ok)
    n_loc = kr * NA_KBAND

    def row_block(r):
        rs = jnp.clip(r - kr // 2, 0, rows - kr)
        q_r = lax.dynamic_index_in_dim(qg, r, axis=2, keepdims=False)
        k_r = lax.dynamic_slice_in_dim(kg, rs, kr, axis=2)[:, :, :, kcol]
        v_r = lax.dynamic_slice_in_dim(vg, rs, kr, axis=2)[:, :, :, kcol]
        q_b = q_r.reshape(B, NA_HEADS, n_cb, NA_QBLOCK, NA_HEAD_DIM)
        s_loc = jnp.einsum('bhjqd,bhajkd->bhjqak', q_b, k_r).astype(jnp.float32) * scale
        drow = rs + jnp.arange(kr) - r + NA_ROWS - 1
        bias = rpb[:, drow][:, :, dcol_idx].transpose(0, 2, 3, 1, 4)
        s_loc = jnp.where(col_ok[:, :, None, :], s_loc + bias.astype(jnp.float32), -jnp.inf)
        s_loc = s_loc.reshape(B, NA_HEADS, n_cb, NA_QBLOCK, n_loc)
        s_ctx = jnp.einsum('bhjqd,bhmd->bhjqm', q_b, kc).astype(jnp.float32) * scale
        p = jax.nn.softmax(jnp.concatenate([s_loc, s_ctx], axis=-1), axis=-1).astype(vx.dtype)
        p_loc = p[..., :n_loc].reshape(B, NA_HEADS, n_cb, NA_QBLOCK, kr, NA_KBAND)
        o = jnp.einsum('bhjqak,bhajke->bhjqe', p_loc, v_r) + jnp.einsum('bhjqm,bhme->bhjqe', p[..., n_loc:], vc)
        return o.reshape(B, NA_HEADS, GRID_W, NA_HEAD_DIM)

    o = lax.map(row_block, jnp.arange(rows))
    ox = o.transpose(1, 0, 3, 2, 4).reshape(B, L, D) @ w_out
    oc = None
    if need_ctx:
        s = jnp.einsum('bhqd,bhkd->bhqk', qc, kc).astype(jnp.float32) * scale
        pc = jax.nn.softmax(s, axis=-1).astype(vc.dtype)
        occ = jnp.einsum('bhqk,bhkd->bhqd', pc, vc)
        oc = occ.transpose(0, 2, 1, 3).reshape(hc.shape[0], hc.shape[1], D) @ w_out
    return ox, oc


def setup_inputs(seed: int = 0) -> dict:
    key = jax.random.key(seed)
    ks = jax.random.split(key, 20)
    D = D_MODEL
    F = FFN_HIDDEN
    GD = POOL_GROUP_DIM

    def nrm(k, shape, s):
        return jax.random.normal(k, shape, jnp.float32) * s

    return {
        'x': nrm(ks[0], (BATCH, SEQ, D), 1.0),
        'c': nrm(ks[1], (BATCH, D), 1.0),
        'ctx': nrm(ks[2], (BATCH, CTX_LEN, D), 1.0),
        'c_ctx': nrm(ks[3], (D,), 1.0),
        'mod_w': nrm(ks[4], (DEPTH, D, N_MOD * D), 0.5 * D ** -0.5),
        'mod_b': nrm(ks[5], (DEPTH, N_MOD * D), 0.02),
        'ln_g': 1.0 + nrm(ks[6], (DEPTH, 2, D), 0.02),
        'ln_b': nrm(ks[7], (DEPTH, 2, D), 0.02),
        'ffn_w_in': nrm(ks[8], (DEPTH, D, 2 * F), D ** -0.5),
        'ffn_w_out': nrm(ks[9], (DEPTH, F, D), BETA * F ** -0.5),
        'pool_w': nrm(ks[10], (N_POOL_LAYERS, N_POOL_GROUPS, GD, GD), BETA * GD ** -0.5),
        'pool_scale': 1.0 + nrm(ks[11], (N_POOL_LAYERS, D), 0.02),
        'hgrn_w_in': nrm(ks[12], (N_HGRN_LAYERS, D, 5 * D), D ** -0.5),
        'hgrn_lb_logits': 1.0 + nrm(ks[13], (2, DEPTH, D), 0.02),
        'hgrn_norm_w': 1.0 + nrm(ks[14], (N_HGRN_LAYERS, D), 0.02),
        'hgrn_w_out': nrm(ks[15], (N_HGRN_LAYERS, D, D), BETA * D ** -0.5),
        'na_w_qkv': nrm(ks[16], (N_NA_LAYERS, D, 3 * D), D ** -0.5),
        'na_rpb': nrm(ks[17], (N_NA_LAYERS, NA_HEADS, 2 * NA_ROWS - 1, 2 * NA_COLS - 1), 0.02),
        'na_w_out': nrm(ks[18], (N_NA_LAYERS, D, D), BETA * D ** -0.5),
    }


def reference(x, c, ctx, c_ctx, mod_w, mod_b, ln_g, ln_b, ffn_w_in, ffn_w_out,
              pool_w, pool_scale, hgrn_w_in, hgrn_lb_logits, hgrn_norm_w, hgrn_w_out,
              na_w_qkv, na_rpb, na_w_out):
    p_lb = jax.nn.softmax(hgrn_lb_logits.astype(jnp.float32), axis=1)
    lower_bounds = jnp.cumsum(p_lb, axis=1) - p_lb[:, :1]
    s_c = jax.nn.silu(c)
    s_cc = jax.nn.silu(c_ctx)
    for i in range(DEPTH):
        last = i == DEPTH - 1
        kind = i % N_MIXERS
        j = i // N_MIXERS
        mod_x = s_c @ mod_w[i] + mod_b[i]
        mod_c = s_cc @ mod_w[i] + mod_b[i]
        sh1, sc1, g1, sh2, sc2, g2 = jnp.split(mod_x[:, None, :], N_MOD, axis=-1)
        csh1, csc1, cg1, csh2, csc2, cg2 = jnp.split(mod_c, N_MOD, axis=-1)
        hx = x * (1.0 + sc1) + sh1
        hc = ctx * (1.0 + csc1) + csh1 if (kind != 0 or not last) else None
        if kind == 0:
            ox, oc = pool_mixer(hx, hc, pool_w[j], pool_scale[j], not last)
        elif kind == 1:
            ox, oc = hgrn2_mixer(hx, hc, hgrn_w_in[j], lower_bounds[0, i], lower_bounds[1, i],
                                 hgrn_norm_w[j], hgrn_w_out[j], not last)
        else:
            ox, oc = na_mixer(hx, hc, na_w_qkv[j], na_rpb[j], na_w_out[j], not last)
        x = layer_norm(ALPHA * x + g1 * ox, ln_g[i, 0], ln_b[i, 0])
        x = layer_norm(ALPHA * x + g2 * swiglu(x * (1.0 + sc2) + sh2, ffn_w_in[i], ffn_w_out[i]),
                       ln_g[i, 1], ln_b[i, 1])
        if not last:
            ctx = layer_norm(ALPHA * ctx + cg1 * oc, ln_g[i, 0], ln_b[i, 0])
            ctx = layer_norm(ALPHA * ctx + cg2 * swiglu(ctx * (1.0 + csc2) + csh2, ffn_w_in[i], ffn_w_out[i]),
                             ln_g[i, 1], ln_b[i, 1])
    return x
```

```python
import contextlib
import numpy as np
import ml_dtypes
import concourse.bass as bass
import concourse.mybir as mybir
from concourse.bass_utils import run_bass_kernel_spmd

AF = mybir.ActivationFunctionType
ALU = mybir.AluOpType
F32 = mybir.dt.float32
BF16 = mybir.dt.bfloat16

D = 1024
KC = 8
NCTX = 256
SEQ = 2048
T = NCTX + SEQ
DEPTH = 4
FH = 2816
JC = FH // 128
ALPHA = (2 * DEPTH) ** 0.25
LN_EPS = 1e-6
RMS_EPS = 1e-6
EPS_LN = LN_EPS / (ALPHA * ALPHA)
POOL_WINDOWS = (2, 4, 8, 16)
NSLOT = 6
SLOTW = 2048
NPAT = 21


class Buf:
    __slots__ = ("w", "r", "name")

    def __init__(self, name=""):
        self.w = None
        self.r = {}
        self.name = name


class _Eng:
    def __init__(self, name, eng, sem):
        self.name = name
        self.eng = eng
        self.sem = sem
        self.count = 0
        self.known = {}


class FW:
    def __init__(self, nc, stack):
        self.nc = nc
        self.stack = stack
        self.dry = False
        self.E = {}
        for name, eng in (("pe", nc.tensor), ("act", nc.scalar), ("dve", nc.vector),
                          ("pool", nc.gpsimd), ("sp", nc.sync)):
            sem = stack.enter_context(nc.semaphore("sem_" + name))
            self.E[name] = _Eng(name, eng, sem)
        self.nwaits = 0
        self.nins = 0
        self.chans = []

    def new_chan(self, name):
        sem = self.stack.enter_context(self.nc.semaphore("ch_" + name))
        ch = [sem, 0]
        self.chans.append(ch)
        return ch

    def _wait_for(self, E, evs):
        need = {}
        for ev in evs:
            if ev is None:
                continue
            sem, val, clock, src = ev
            if src == "pe" and E.name == "pe":
                continue
            k = id(sem)
            if E.known.get(k, 0) >= val:
                continue
            if k not in need or need[k][1] < val:
                need[k] = (sem, val, clock)
        for k, (sem, val, clock) in need.items():
            if E.known.get(k, 0) >= val:
                continue
            E.eng.wait_ge(sem, val)
            self.nwaits += 1
            E.known[k] = val
            for kk, vv in clock.items():
                if E.known.get(kk, 0) < vv:
                    E.known[kk] = vv

    @staticmethod
    def _deps(reads, writes):
        evs = []
        for b in reads:
            evs.append(b.w)
        for b in writes:
            evs.append(b.w)
            evs.extend(b.r.values())
        return evs

    @staticmethod
    def _record(ev, reads, writes):
        k = id(ev[0])
        for b in reads:
            b.r[k] = ev
        for b in writes:
            b.w = ev
            b.r = {}

    def op(self, en, fn, reads=(), writes=()):
        if self.dry:
            return
        E = self.E[en]
        self._wait_for(E, self._deps(reads, writes))
        ins = fn(E.eng)
        E.count += 1
        ins.then_inc(E.sem, 1)
        self.nins += 1
        ev = (E.sem, E.count, dict(E.known), en)
        self._record(ev, reads, writes)

    def dma(self, en, chan, out, in_, reads=(), writes=()):
        if self.dry:
            return
        E = self.E[en]
        self._wait_for(E, self._deps(reads, writes))
        ins = E.eng.dma_start(out=out, in_=in_)
        chan[1] += 16
        ins.then_inc(chan[0], 16)
        self.nins += 1
        ev = (chan[0], chan[1], dict(E.known), "dma")
        self._record(ev, reads, writes)

    def barrier(self, engines=("pe", "act", "dve", "sp")):
        if self.dry:
            return
        evs = []
        for n in ("pe", "act", "dve", "sp"):
            E = self.E[n]
            if E.count:
                evs.append((E.sem, E.count, {}, "x"))
        for ch in self.chans:
            if ch[1] and not ch[2:]:
                evs.append((ch[0], ch[1], {}, "dma"))
        for n in engines:
            self._wait_for(self.E[n], evs)


class Ring:
    def __init__(self, fw, tile):
        self.fw = fw
        self.tile = tile
        self.bufs = [Buf("slot%d" % i) for i in range(NSLOT)]
        self.chs = [fw.new_chan("slot%d" % i) for i in range(NSLOT)]
        for ch in self.chs:
            ch.append("ring")
        self.reqs = []
        self.pos = 0
        self.issued = 0

    def reset(self):
        self.pos = 0
        self.issued = 0

    def get(self, loads):
        i = self.pos
        self.pos += 1
        if self.fw.dry:
            self.reqs.append(loads)
            return self.tile[:, 0, :], self.bufs[0]
        hi = min(len(self.reqs), i + NSLOT - 2)
        while self.issued < hi:
            r = self.issued
            s = r % NSLOT
            for dst_fn, src in self.reqs[r]:
                self.fw.dma("pool", self.chs[s], dst_fn(self.tile[:, s, :]), src, writes=[self.bufs[s]])
            self.issued += 1
        s = i % NSLOT
        return self.tile[:, s, :], self.bufs[s]


class PsumPool:
    def __init__(self, tile):
        self.tile = tile
        self.bufs = [Buf("ps%d" % i) for i in range(8)]
        self.i = 0
        self.nrot = 7

    def next(self):
        i = self.i
        self.i = (i + 1) % self.nrot
        return self.tile[:, i, :], self.bufs[i]

    def modbank(self):
        return self.tile[:, 7, :], self.bufs[7]


def token_blocks(with_ctx):
    blks = [(0, NCTX, 1)] if with_ctx else []
    for i in range(4):
        blks.append((NCTX + 512 * i, 512, 0))
    return blks


def build(nlayers=DEPTH, dbg=False):
    nc = bass.Bass("TRN2", target_bir_lowering=False)

    def din(name, shape, dt=F32):
        return nc.dram_tensor(name, list(shape), dt, kind="ExternalInput").ap()

    x_d = din("x", [SEQ, D])
    ctx_d = din("ctx", [NCTX, D])
    cvec_d = din("cvec", [128, KC, 2])
    modw_d = din("mod_w", [DEPTH, D, 6 * D])
    modb_d = din("mod_b2", [128, DEPTH, 48, 2])
    lnp_d = din("lnp", [128, DEPTH, 2, 2, KC])
    ffnin_d = din("ffn_w_in", [DEPTH, D, 2 * FH])
    ffnout_d = din("ffn_w_out", [DEPTH, FH, D])
    poolw_d = din("pool_w", [2, 4, 256, 256])
    poolsc_d = din("pool_sc", [128, 2, KC])
    ident_d = din("ident", [128, 128])
    edge_d = din("edgefac", [128, 4, 16])
    hgin_d = din("hgrn_w_in", [D, 5 * D])
    hgout_d = din("hgrn_w_out", [D, D])
    hgnw_d = din("hgrn_nw", [128, KC])
    hglb_d = din("hgrn_lbl", [128, 2, DEPTH, KC])
    mask128_d = din("mask128", [128, 2, 128])
    reset_d = din("reset01", [128, 512])
    naqkv_d = din("na_w_qkv", [D, 3 * D])
    naout_d = din("na_w_out", [D, D])
    nabias_d = din("na_bias", [16, 128, NPAT, 128])
    namask_d = din("na_mask", [128, NPAT, 128], BF16)
    out_d = nc.dram_tensor("out", [SEQ, D], F32, kind="ExternalOutput").ap()
    dbg_d = nc.dram_tensor("dbg", [128, KC, T], F32, kind="ExternalOutput").ap() if dbg else None

    with contextlib.ExitStack() as st:
        fw = FW(nc, st)

        uniq = [0]

        def sb(name, shape, dt=F32, stack=st):
            uniq[0] += 1
            return stack.enter_context(nc.sbuf_tensor("s%d_%s" % (uniq[0], name), list(shape), dt))

        xT = sb("xT", [128, KC, T])
        hT = sb("hT", [128, KC, T], BF16)
        ring_t = sb("ring", [128, NSLOT, SLOTW], BF16)
        modv = sb("modv", [128, DEPTH, 6, KC, 2])
        galpha = sb("galpha", [128, DEPTH, 2, KC, 2])
        lnp = sb("lnp", [128, DEPTH, 2, 2, KC])
        ident = sb("ident", [128, 128])
        identb = sb("identb", [128, 128], BF16)
        onesb = sb("onesb", [128, 128], BF16)
        edgef = sb("edgef", [128, 4, 16])
        poolsc = sb("poolsc", [128, 2, KC])
        cvec = sb("cvec", [128, KC, 2])
        sTb = sb("sTb", [128, KC, 2], BF16)
        modb = sb("modb", [128, DEPTH, 48, 2])
        hgnw = sb("hgnw", [128, KC])
        hglb = sb("hglb", [128, 2, DEPTH, KC])
        lbv = sb("lbv", [128, 4, 2, KC])
        mask128 = sb("mask128", [128, 2, 128])
        rowmask = sb("rowmask", [128, 2])
        reset01 = sb("reset01", [128, 512])
        ps_t = st.enter_context(nc.psum_tensor("ps", [128, 8, 512], F32))
        PS = PsumPool(ps_t)
        ring = Ring(fw, ring_t)

        B_x = [[Buf("x%d_%d" % (c, b)) for b in range(5)] for c in range(KC)]
        B_h = [[Buf("h%d_%d" % (c, b)) for b in range(5)] for c in range(KC)]
        B_const = Buf("const")
        B_modvs = [Buf("modv%d" % i) for i in range(DEPTH)]
        ch_in = fw.new_chan("in")
        ch_outs = [fw.new_chan("out0"), fw.new_chan("out1")]
        ch_dbg = fw.new_chan("dbg")
        ch_mw = [fw.new_chan("mw0"), fw.new_chan("mw1")]
        ch_xin = [fw.new_chan("xin0"), fw.new_chan("xin1")]
        ch_ebs = [fw.new_chan("eb0"), fw.new_chan("eb1")]

        def blk_index(t0):
            return 0 if t0 < NCTX else 1 + (t0 - NCTX) // 512

        def xbufs(t0, chunks=range(KC)):
            return [B_x[c][blk_index(t0)] for c in chunks]

        def hbufs(t0, chunks=range(KC)):
            return [B_h[c][blk_index(t0)] for c in chunks]

        all_x = [b for row in B_x for b in row]
        all_h = [b for row in B_h for b in row]

        def gen():
            ring.reset()
            for dst, src in ((cvec, cvec_d), (lnp, lnp_d), (ident, ident_d), (edgef, edge_d), (poolsc, poolsc_d),
                             (hgnw, hgnw_d), (hglb, hglb_d), (mask128, mask128_d), (reset01, reset_d)):
                fw.dma("sp", ch_in, dst[:], src, writes=[B_const])
            fw.op("dve", lambda e: e.memset(onesb[:], 1.0), writes=[B_const])
            fw.op("dve", lambda e: e.memset(rowmask[:], 0.0), writes=[B_const])
            fw.op("dve", lambda e: e.memset(rowmask[0:64, 0:1], 1.0), writes=[B_const])
            fw.op("dve", lambda e: e.memset(rowmask[64:128, 1:2], 1.0), writes=[B_const])
            fw.op("act", lambda e: e.activation(out=identb[:], in_=ident[:], func=AF.Copy), reads=[B_const], writes=[B_const])
            fw.dma("sp", ch_in, modb[:], modb_d, writes=[B_const])
            fw.op("act", lambda e: e.activation(out=sTb[:], in_=cvec[:], func=AF.Silu), reads=[B_const], writes=[B_const])

            def mod_compute(l, part, nparts):
                if l >= nlayers:
                    return
                psb, Bp = PS.modbank()
                pv = psb[:, 0:96].rearrange("p (j v) -> p j v", v=2)
                per = 24 // nparts
                for sl in range(part * per, (part + 1) * per):
                    slot, Bs = ring.get([(lambda s_: s_.rearrange("p (k c) -> p k c", k=KC),
                                          modw_d[l, :, sl * 256:(sl + 1) * 256].rearrange("(kc p) c -> p kc c", p=128))])
                    wv = slot.rearrange("p (k c) -> p k c", k=KC)
                    for q in range(2):
                        oc = sl * 2 + q
                        for kc in range(KC):
                            fw.op("pe", lambda e, q=q, kc=kc, oc=oc: e.matmul(
                                pv[:, oc, :], lhsT=wv[:, kc, q * 128:(q + 1) * 128], rhs=sTb[:, kc, :],
                                start=(kc == 0), stop=(kc == KC - 1)),
                                reads=[Bs, B_const], writes=[Bp])
                if part != nparts - 1:
                    return
                Bm = B_modvs[l]
                fw.op("dve", lambda e: e.tensor_tensor(
                    out=modv[:, l].rearrange("p m c v -> p (m c) v"), in0=pv, in1=modb[:, l], op=ALU.add),
                    reads=[Bp, B_const], writes=[Bm])
                for m in (1, 4):
                    fw.op("dve", lambda e, m=m: e.tensor_scalar_add(out=modv[:, l, m], in0=modv[:, l, m], scalar1=1.0),
                          reads=[Bm], writes=[Bm])
                for w_, m in ((0, 2), (1, 5)):
                    fw.op("dve", lambda e, m=m, w_=w_: e.tensor_scalar_mul(
                        out=galpha[:, l, w_], in0=modv[:, l, m], scalar1=1.0 / ALPHA),
                        reads=[Bm], writes=[Bm])
                if l % 3 == 0:
                    j = l // 3
                    fw.op("dve", lambda e, j=j: e.tensor_tensor(
                        out=galpha[:, l, 0], in0=galpha[:, l, 0],
                        in1=poolsc[:, j].unsqueeze(2).broadcast_to((128, KC, 2)), op=ALU.mult),
                        reads=[Bm, B_const], writes=[Bm])

            mod_compute(0, 0, 1)
            with contextlib.ExitStack() as ph:
                xin = sb("xin", [128, 2, D], stack=ph)
                B_xin = [Buf("xin0"), Buf("xin1")]
                for i in range(T // 128):
                    s = i % 2
                    src = ctx_d[i * 128:(i + 1) * 128, :] if i < 2 else x_d[(i - 2) * 128:(i - 1) * 128, :]
                    fw.dma("sp", ch_xin[s], xin[:, s], src, writes=[B_xin[s]])
                    for half in range(2):
                        psb, Bp = PS.next()
                        for q in range(4):
                            c = half * 4 + q
                            fw.op("pe", lambda e, s=s, c=c, q=q, psb=psb: e.transpose(
                                psb[:, q * 128:(q + 1) * 128], xin[:, s, c * 128:(c + 1) * 128], ident[:]),
                                reads=[B_xin[s], B_const], writes=[Bp])
                        eng = "act" if half == 0 else "dve"
                        dst = xT[:, half * 4:half * 4 + 4, i * 128:(i + 1) * 128]
                        srcv = psb.rearrange("p (q t) -> p q t", q=4)
                        if eng == "act":
                            fw.op("act", lambda e, dst=dst, srcv=srcv: e.activation(out=dst, in_=srcv, func=AF.Copy),
                                  reads=[Bp], writes=xbufs(i * 128, range(half * 4, half * 4 + 4)))
                        else:
                            fw.op("dve", lambda e, dst=dst, srcv=srcv: e.tensor_copy(out=dst, in_=srcv),
                                  reads=[Bp], writes=xbufs(i * 128, range(half * 4, half * 4 + 4)))
                fw.barrier()

            def modulate(l, which, blks):
                for (t0, n_, v) in blks:
                    for c in range(KC):
                        fw.op("act", lambda e, c=c, t0=t0, n_=n_, v=v: e.activation(
                            out=hT[:, c, t0:t0 + n_], in_=xT[:, c, t0:t0 + n_], func=AF.Identity,
                            scale=modv[:, l, 3 * which + 1, c, v:v + 1], bias=modv[:, l, 3 * which, c, v:v + 1]),
                            reads=[B_x[c][blk_index(t0)], B_modvs[l]], writes=[B_h[c][blk_index(t0)]])

            def layer_norm(l, which, blks, lnt, next_mod):
                ybf, ysq, st_t = lnt
                for (t0, n_, v) in blks:
                    bi = blk_index(t0)
                    xb = [B_x[c][bi] for c in range(KC)]
                    hb = [B_h[c][bi] for c in range(KC)]
                    B_y, B_q, B_s = ybf[1], ysq[1], st_t[1]
                    fw.op("act", lambda e, t0=t0, n_=n_: e.activation(out=ybf[0][:, :, 0:n_], in_=xT[:, :, t0:t0 + n_], func=AF.Copy),
                          reads=xb, writes=[B_y])
                    fw.op("act", lambda e, t0=t0, n_=n_: e.activation(out=ysq[0][:, :, 0:n_], in_=xT[:, :, t0:t0 + n_], func=AF.Square),
                          reads=xb, writes=[B_q])
                    ps1, Bp1 = PS.next()
                    ps2, Bp2 = PS.next()
                    for c in range(KC):
                        fw.op("pe", lambda e, c=c, n_=n_, ps1=ps1: e.matmul(ps1[:, 0:n_], lhsT=onesb[:], rhs=ybf[0][:, c, 0:n_],
                                                                          start=(c == 0), stop=(c == KC - 1)),
                              reads=[B_y, B_const], writes=[Bp1])
                    for c in range(KC):
                        fw.op("pe", lambda e, c=c, n_=n_, ps2=ps2: e.matmul(ps2[:, 0:n_], lhsT=onesb[:], rhs=ysq[0][:, c, 0:n_],
                                                                          start=(c == 0), stop=(c == KC - 1)),
                              reads=[B_q, B_const], writes=[Bp2])
                    S = st_t[0]
                    mean, msq, rstd, mr = S[:, 0, 0:n_], S[:, 1, 0:n_], S[:, 2, 0:n_], S[:, 3, 0:n_]
                    fw.op("dve", lambda e: e.tensor_scalar_mul(out=mean, in0=ps1[:, 0:n_], scalar1=1.0 / D), reads=[Bp1], writes=[B_s])
                    fw.op("dve", lambda e: e.tensor_tensor(out=msq, in0=mean, in1=mean, op=ALU.mult), reads=[B_s], writes=[B_s])
                    fw.op("dve", lambda e: e.scalar_tensor_tensor(out=msq, in0=ps2[:, 0:n_], scalar=1.0 / D, in1=msq,
                                                                  op0=ALU.mult, op1=ALU.subtract), reads=[Bp2, B_s], writes=[B_s])
                    fw.op("act", lambda e: e.activation(out=rstd, in_=msq, func=AF.Sqrt, bias=EPS_LN), reads=[B_s], writes=[B_s])
                    fw.op("dve", lambda e: e.reciprocal(out=rstd, in_=rstd), reads=[B_s], writes=[B_s])
                    fw.op("dve", lambda e: e.tensor_tensor(out=mr, in0=mean, in1=rstd, op=ALU.mult), reads=[B_s], writes=[B_s])
                    xv = xT[:, :, t0:t0 + n_]
                    fw.op("dve", lambda e: e.tensor_tensor(out=xv, in0=xv, in1=rstd.unsqueeze(1).broadcast_to((128, KC, n_)), op=ALU.mult),
                          reads=[B_s] + xb, writes=xb)
                    fw.op("dve", lambda e: e.tensor_tensor(out=xv, in0=xv, in1=mr.unsqueeze(1).broadcast_to((128, KC, n_)), op=ALU.subtract),
                          reads=[B_s] + xb, writes=xb)
                    for c in range(KC):
                        fw.op("dve", lambda e, c=c: e.tensor_scalar(
                            out=xT[:, c, t0:t0 + n_], in0=xT[:, c, t0:t0 + n_],
                            scalar1=lnp[:, l, which, 0, c:c + 1], scalar2=lnp[:, l, which, 1, c:c + 1],
                            op0=ALU.mult, op1=ALU.add), reads=[xb[c], B_const], writes=[xb[c]])
                        if next_mod is not None:
                            nl, nw = next_mod
                            fw.op("act", lambda e, c=c: e.activation(
                                out=hT[:, c, t0:t0 + n_], in_=xT[:, c, t0:t0 + n_], func=AF.Identity,
                                scale=modv[:, nl, 3 * nw + 1, c, v:v + 1], bias=modv[:, nl, 3 * nw, c, v:v + 1]),
                                reads=[xb[c], B_modvs[nl]], writes=[hb[c]])

            def add_branch(psv, l, w_, c, t0, n_, v):
                return lambda e: e.scalar_tensor_tensor(
                    out=xT[:, c, t0:t0 + n_], in0=psv, scalar=galpha[:, l, w_, c, v:v + 1], in1=xT[:, c, t0:t0 + n_],
                    op0=ALU.mult, op1=ALU.add)

            def ffn(l, blks, act_t, sg_t):
                act, B_act = act_t
                slices = [list(range(0, 6)), list(range(6, 12)), list(range(12, 17)), list(range(17, 22))]
                nsg = 0
                for si_, sl in enumerate(slices):
                    for jj, j in enumerate(sl):
                        slot, Bs = ring.get([
                            (lambda s: s[:, 0:1024].rearrange("p (k c) -> p k c", k=KC),
                             ffnin_d[l, :, j * 128:(j + 1) * 128].rearrange("(k p) c -> p k c", p=128)),
                            (lambda s: s[:, 1024:2048].rearrange("p (k c) -> p k c", k=KC),
                             ffnin_d[l, :, FH + j * 128:FH + (j + 1) * 128].rearrange("(k p) c -> p k c", p=128)),
                        ])
                        wg = slot[:, 0:1024].rearrange("p (k c) -> p k c", k=KC)
                        wu = slot[:, 1024:2048].rearrange("p (k c) -> p k c", k=KC)
                        for (t0, n_, v) in blks:
                            bi = blk_index(t0)
                            psg, Bg = PS.next()
                            psu, Bu = PS.next()
                            for k in range(KC):
                                fw.op("pe", lambda e, k=k, psg=psg: e.matmul(psg[:, 0:n_], lhsT=wg[:, k, :], rhs=hT[:, k, t0:t0 + n_],
                                                                          start=(k == 0), stop=(k == KC - 1)),
                                      reads=[Bs, B_h[k][bi]], writes=[Bg])
                            for k in range(KC):
                                fw.op("pe", lambda e, k=k, psu=psu: e.matmul(psu[:, 0:n_], lhsT=wu[:, k, :], rhs=hT[:, k, t0:t0 + n_],
                                                                          start=(k == 0), stop=(k == KC - 1)),
                                      reads=[Bs, B_h[k][bi]], writes=[Bu])
                            sg, Bsg = sg_t[nsg % 2]
                            nsg += 1
                            fw.op("act", lambda e, sg=sg, psg=psg: e.activation(out=sg[:, 0:n_], in_=psg[:, 0:n_], func=AF.Silu),
                                  reads=[Bg], writes=[Bsg])
                            fw.op("dve", lambda e, sg=sg, psu=psu, jj=jj: e.tensor_tensor(
                                out=act[:, jj, t0:t0 + n_], in0=sg[:, 0:n_], in1=psu[:, 0:n_], op=ALU.mult),
                                reads=[Bsg, Bu], writes=[B_act[jj][bi]])
                    wo = []
                    for p0 in range(0, len(sl), 2):
                        js = sl[p0:p0 + 2]
                        slot, Bs = ring.get([
                            (lambda s, q=q: s[:, q * 1024:(q + 1) * 1024], ffnout_d[l, j * 128:(j + 1) * 128, :])
                            for q, j in enumerate(js)])
                        for q in range(len(js)):
                            wo.append((slot[:, q * 1024:(q + 1) * 1024], Bs))
                    for (t0, n_, v) in blks:
                        bi = blk_index(t0)
                        for m in range(KC):
                            pso, Bo = PS.next()
                            for jj in range(len(sl)):
                                wv, Bs = wo[jj]
                                fw.op("pe", lambda e, jj=jj, wv=wv, pso=pso, m=m: e.matmul(
                                    pso[:, 0:n_], lhsT=wv[:, m * 128:(m + 1) * 128], rhs=act[:, jj, t0:t0 + n_],
                                    start=(jj == 0), stop=(jj == len(sl) - 1)),
                                    reads=[Bs, B_act[jj][bi]], writes=[Bo])
                            fw.op("dve", add_branch(pso[:, 0:n_], l, 1, m, t0, n_, v),
                                  reads=[Bo, B_modvs[l], B_x[m][bi]], writes=[B_x[m][bi]])
                    mod_compute(l + 1, si_, 4)

            def pool_mixer(l, blks, ph):
                j = l // 3
                pT = hT
                B_p = B_h
                pad = sb("pad", [128, 2, SEQ + 16], stack=ph)
                tmp = sb("ptmp", [128, 2, SEQ + 16], stack=ph)
                B_pad = [Buf(), Buf()]
                B_tmp = [Buf(), Buf()]
                segs = [(NCTX, SEQ)] + ([(0, NCTX)] if blks[0][2] == 1 else [])
                it = 0
                for c in range(KC):
                    wi = c // 2
                    w = POOL_WINDOWS[wi]
                    for (s0, L) in segs:
                        s = it % 2
                        it += 1
                        sb_ = [B_h[c][blk_index(t)] for t in range(s0, s0 + L, 512)] if L > NCTX else [B_h[c][0]]
                        pb_ = [B_p[c][blk_index(t)] for t in range(s0, s0 + L, 512)] if L > NCTX else [B_p[c][0]]
                        P_ = pad[:, s]
                        Q_ = tmp[:, s]
                        fw.op("dve", lambda e, P_=P_, L=L: e.memset(P_[:, 0:8], 0.0), writes=[B_pad[s]])
                        fw.op("dve", lambda e, P_=P_, L=L: e.memset(P_[:, 8 + L:16 + L], 0.0), writes=[B_pad[s]])
                        fw.op("dve", lambda e, P_=P_, L=L, c=c, s0=s0: e.tensor_copy(out=P_[:, 8:8 + L], in_=hT[:, c, s0:s0 + L]),
                              reads=sb_, writes=[B_pad[s]])
                        src, dst = P_, Q_
                        Bsrc, Bdst = B_pad[s], B_tmp[s]
                        length = L + 16
                        step = 1
                        while step < w:
                            nl_ = length - step
                            fw.op("dve", lambda e, src=src, dst=dst, nl_=nl_, step=step: e.tensor_tensor(
                                out=dst[:, 0:nl_], in0=src[:, 0:nl_], in1=src[:, step:step + nl_], op=ALU.add),
                                reads=[Bsrc], writes=[Bdst])
                            src, dst = dst, src
                            Bsrc, Bdst = Bdst, Bsrc
                            length = nl_
                            step *= 2
                        off = 8 - w // 2
                        fw.op("dve", lambda e, src=src, dst=dst, off=off, L=L, w=w: e.tensor_scalar_mul(
                            out=dst[:, 0:L], in0=src[:, off:off + L], scalar1=1.0 / w), reads=[Bsrc], writes=[Bdst])
                        fw.op("dve", lambda e, dst=dst, wi=wi: e.tensor_tensor(out=dst[:, 0:8], in0=dst[:, 0:8], in1=edgef[:, wi, 0:8], op=ALU.mult),
                              reads=[B_const], writes=[Bdst])
                        fw.op("dve", lambda e, dst=dst, wi=wi, L=L: e.tensor_tensor(out=dst[:, L - 8:L], in0=dst[:, L - 8:L], in1=edgef[:, wi, 8:16], op=ALU.mult),
                              reads=[B_const], writes=[Bdst])
                        fw.op("dve", lambda e, dst=dst, c=c, s0=s0, L=L: e.tensor_tensor(
                            out=pT[:, c, s0:s0 + L], in0=dst[:, 0:L], in1=hT[:, c, s0:s0 + L], op=ALU.subtract),
                            reads=[Bdst] + sb_, writes=pb_)
                slot, Bs = ring.get([
                    (lambda s: s.rearrange("p (g k c) -> p g k c", g=4, k=2),
                     poolw_d[j].rearrange("g (k p) c -> p g k c", p=128))])
                wgp = slot.rearrange("p (g k c) -> p g k c", g=4, k=2)
                for (t0, n_, v) in blks:
                    bi = blk_index(t0)
                    for g in range(4):
                        for mo in range(2):
                            pso, Bo = PS.next()
                            for k in range(2):
                                fw.op("pe", lambda e, g=g, mo=mo, k=k, pso=pso: e.matmul(
                                    pso[:, 0:n_], lhsT=wgp[:, g, k, mo * 128:(mo + 1) * 128], rhs=pT[:, 2 * g + k, t0:t0 + n_],
                                    start=(k == 0), stop=(k == 1)), reads=[Bs, B_p[2 * g + k][bi]], writes=[Bo])
                            c = 2 * g + mo
                            fw.op("dve", add_branch(pso[:, 0:n_], l, 0, c, t0, n_, v),
                                  reads=[Bo, B_modvs[l], B_x[c][bi]], writes=[B_x[c][bi]])

            def hgrn_mixer(l, blks, ph):
                B_lb = Buf()
                E_ = sb("lbE", [128, 2, DEPTH, KC], stack=ph)
                ssum = sb("lbS", [128, 2, KC], stack=ph)
                fw.op("act", lambda e: e.activation(out=E_[:], in_=hglb[:], func=AF.Exp), reads=[B_const], writes=[B_lb])
                fw.op("dve", lambda e: e.tensor_tensor(out=ssum[:], in0=E_[:, :, 0], in1=E_[:, :, 1], op=ALU.add), reads=[B_lb], writes=[B_lb])
                for k in (2, 3):
                    fw.op("dve", lambda e, k=k: e.tensor_tensor(out=ssum[:], in0=ssum[:], in1=E_[:, :, k], op=ALU.add), reads=[B_lb], writes=[B_lb])
                fw.op("dve", lambda e: e.reciprocal(out=ssum[:], in_=ssum[:]), reads=[B_lb], writes=[B_lb])
                fw.op("dve", lambda e: e.tensor_tensor(out=E_[:], in0=E_[:], in1=ssum[:].unsqueeze(2).broadcast_to((128, 2, DEPTH, KC)), op=ALU.mult),
                      reads=[B_lb], writes=[B_lb])
                fw.op("dve", lambda e: e.tensor_copy(out=lbv[:, 0], in_=E_[:, :, 0]), reads=[B_lb], writes=[B_lb])
                for k in range(1, l + 1):
                    fw.op("dve", lambda e, k=k: e.tensor_tensor(out=lbv[:, 0], in0=lbv[:, 0], in1=E_[:, :, k], op=ALU.add), reads=[B_lb], writes=[B_lb])
                fw.op("dve", lambda e: e.tensor_tensor(out=lbv[:, 0], in0=lbv[:, 0], in1=E_[:, :, 0], op=ALU.subtract), reads=[B_lb], writes=[B_lb])
                fw.op("dve", lambda e: e.tensor_scalar(out=lbv[:, 1], in0=lbv[:, 0], scalar1=-1.0, scalar2=1.0, op0=ALU.mult, op1=ALU.add),
                      reads=[B_lb], writes=[B_lb])
                fw.op("dve", lambda e: e.tensor_scalar_add(out=lbv[:, 2], in0=lbv[:, 0], scalar1=-1.0), reads=[B_lb], writes=[B_lb])

                def t(name, shape, dt=F32):
                    return sb(name, shape, dt, stack=ph)
                NT = T // 128
                qT = t("qT", [128, T], BF16)
                vtok = t("vtok", [128, NT, 128], BF16)
                oacc = t("oacc", [128, T])
                B_q = [Buf() for _ in range(5)]
                B_sg = [Buf() for _ in range(5)]
                B_vt = [Buf() for _ in range(5)]
                B_oa = [Buf() for _ in range(5)]
                vblk, B_vblk = t("vblk", [128, 512], BF16), Buf()
                SUB = 256
                Dd = [[None, None], [None, None]]
                for di in range(2):
                    carry_ = (t("carry%d" % di, [128, 128]), Buf())
                    for si in range(2):
                        d = {}
                        tg = "%d%d" % (di, si)
                        for nm in ("sig", "lgf", "bb", "tmpf", "E1"):
                            d[nm] = (t(nm + tg, [128, SUB]), Buf())
                        for nm in ("qt", "kt", "kh"):
                            d[nm] = (t(nm + tg, [128, SUB], BF16), Buf())
                        d["khtok"] = (t("khtok" + tg, [128, 2, 2, 128], BF16), Buf())
                        d["ATm"] = (t("ATm" + tg, [128, 2, 128], BF16), Buf())
                        d["ebe"] = (t("ebe" + tg, [128, 2, 4]), Buf())
                        d["SS"] = (t("SS" + tg, [128, 5, 128]), Buf())
                        d["Sbf"] = (t("Sbf" + tg, [128, 4, 128], BF16), Buf())
                        d["carry"] = carry_
                        d["pss"] = (ps_t[:, 4 + 2 * di + si, :], PS.bufs[4 + 2 * di + si])
                        Dd[di][si] = d
                rstd, B_rstd = Dd[0][0]["sig"]
                ontmp, B_ontmp = Dd[0][0]["lgf"]
                osq, B_osq = Dd[0][0]["qt"]
                onb, B_onb = Dd[0][0]["kt"]
                sgb, B_sgb = Dd[0][0]["kh"]
                PS.nrot = 4
                PS.i = 0

                def wview(s, h):
                    return s[:, h * 1024:(h + 1) * 1024].rearrange("p (k c) -> p k c", k=KC)

                def wsrc(col0):
                    return hgin_d[:, col0:col0 + 128].rearrange("(k p) c -> p k c", p=128)

                def mm8(ps, wv, t0, n_, Bs, bi, Bp):
                    for k in range(KC):
                        fw.op("pe", lambda e, k=k: e.matmul(ps[:, 0:n_], lhsT=wv[:, k, :], rhs=hT[:, k, t0:t0 + n_],
                                                            start=(k == 0), stop=(k == KC - 1)),
                              reads=[Bs, B_h[k][bi]], writes=[Bp])

                for m in range(KC):
                    c0 = m * 128
                    sl1, Bs1 = ring.get([(lambda s: wview(s, 0), wsrc(0 * D + c0)), (lambda s: wview(s, 1), wsrc(1 * D + c0))])
                    sl2, Bs2 = ring.get([(lambda s: wview(s, 0), wsrc(2 * D + c0)), (lambda s: wview(s, 1), wsrc(3 * D + c0))])
                    sl3, Bs3 = ring.get([(lambda s: wview(s, 0), wsrc(4 * D + c0)),
                                         (lambda s: s[:, 1024:2048], hgout_d[c0:c0 + 128, :])])
                    wq, wv_ = wview(sl1, 0), wview(sl1, 1)
                    wz = [wview(sl2, 0), wview(sl2, 1)]
                    wg = wview(sl3, 0)
                    wo = sl3[:, 1024:2048]
                    for (t0, n_, v) in blks:
                        bi = blk_index(t0)
                        nt_ = n_ // 128
                        g0 = t0 // 128
                        ps, Bp = PS.next()
                        mm8(ps, wq, t0, n_, Bs1, bi, Bp)
                        fw.op("act", lambda e: e.activation(out=qT[:, t0:t0 + n_], in_=ps[:, 0:n_], func=AF.Silu), reads=[Bp], writes=[B_q[bi]])
                        ps, Bp = PS.next()
                        mm8(ps, wv_, t0, n_, Bs1, bi, Bp)
                        fw.op("dve", lambda e: e.tensor_copy(out=vblk[:, 0:n_], in_=ps[:, 0:n_]), reads=[Bp], writes=[B_vblk])
                        ps2, Bp2 = PS.next()
                        psb = ps2.bitcast(BF16)
                        for j in range(nt_):
                            fw.op("pe", lambda e, j=j: e.transpose(psb[:, j * 128:(j + 1) * 128], vblk[:, j * 128:(j + 1) * 128], identb[:]),
                                  reads=[B_vblk, B_const], writes=[Bp2])
                        fw.op("act", lambda e: e.activation(out=vtok[:, g0:g0 + nt_, :],
                                                            in_=psb[:, 0:nt_ * 128].rearrange("p (c e) -> p c e", e=128), func=AF.Copy),
                              reads=[Bp2], writes=[B_vt[bi]])
                    for di in range(2):
                        fw.op("dve", lambda e, di=di: e.memset(Dd[di][0]["carry"][0][:], 0.0), writes=[Dd[di][0]["carry"][1]])
                    touched = {}

                    def front(item):
                        di, si_, (t0, n_, v) = item
                        d = Dd[di][si_]
                        bi = blk_index(t0)
                        nch = n_ // 64
                        nt_ = n_ // 128
                        g0 = t0 // 128
                        mid = 31 if di == 0 else 32
                        endc = 63 if di == 0 else 0
                        sig, B_sig = d["sig"]
                        lgf, B_lgf = d["lgf"]
                        bb, B_bb = d["bb"]
                        tmpf, B_tmpf = d["tmpf"]
                        E1, B_E1 = d["E1"]
                        qt_, B_qt = d["qt"]
                        kt_, B_kt = d["kt"]
                        kh_, B_kh = d["kh"]
                        khtok, B_khtok = d["khtok"]
                        ATm, B_AT = d["ATm"]
                        ebe, B_ebe = d["ebe"]
                        ps, Bp = PS.next()
                        mm8(ps, wz[di], t0, n_, Bs2, bi, Bp)
                        fw.op("act", lambda e: e.activation(out=sig[:, 0:n_], in_=ps[:, 0:n_], func=AF.Sigmoid, scale=-1.0), reads=[Bp], writes=[B_sig])
                        fw.op("act", lambda e: e.activation(out=lgf[:, 0:n_], in_=sig[:, 0:n_], func=AF.Ln,
                                                            scale=lbv[:, 2, di, m:m + 1], bias=1.0), reads=[B_sig, B_lb], writes=[B_lgf])
                        fw.op("dve", lambda e: e.tensor_tensor_scan(out=bb[:, 0:n_], data0=reset01[:, 0:n_], data1=lgf[:, 0:n_],
                                                                    initial=0.0, op0=ALU.mult, op1=ALU.add),
                              reads=[B_lgf, B_const], writes=[B_bb])
                        if di == 0:
                            bbv, B_bbv = bb, B_bb
                            E2, B_E2 = lgf, B_lgf
                        else:
                            fw.op("dve", lambda e: e.tensor_tensor(out=tmpf[:, 0:n_], in0=lgf[:, 0:n_], in1=bb[:, 0:n_], op=ALU.subtract),
                                  reads=[B_lgf, B_bb], writes=[B_tmpf])
                            bb3 = bb[:, 0:n_].rearrange("p (c s) -> p c s", s=64)
                            fw.op("dve", lambda e: e.tensor_tensor(
                                out=lgf[:, 0:n_].rearrange("p (c s) -> p c s", s=64),
                                in0=tmpf[:, 0:n_].rearrange("p (c s) -> p c s", s=64),
                                in1=bb3[:, :, 63:64].broadcast_to((128, nch, 64)), op=ALU.add),
                                reads=[B_tmpf, B_bb], writes=[B_lgf])
                            bbv, B_bbv = lgf, B_lgf
                            E2, B_E2 = bb, B_bb
                        bbv3 = bbv[:, 0:n_].rearrange("p (c s) -> p c s", s=64)
                        tm3 = tmpf[:, 0:n_].rearrange("p (c s) -> p c s", s=64)
                        fw.op("act", lambda e: e.activation(out=ebe[:, 0, 0:nch], in_=bbv3[:, :, endc], func=AF.Exp), reads=[B_bbv], writes=[B_ebe])
                        fw.op("act", lambda e: e.activation(out=ebe[:, 1, 0:nch], in_=bbv3[:, :, mid], func=AF.Exp), reads=[B_bbv], writes=[B_ebe])
                        fw.op("dve", lambda e: e.tensor_tensor(out=tm3, in0=bbv3, in1=bbv3[:, :, mid:mid + 1].broadcast_to((128, nch, 64)),
                                                               op=ALU.subtract), reads=[B_bbv], writes=[B_tmpf])
                        fw.op("act", lambda e: e.activation(out=E1[:, 0:n_], in_=tmpf[:, 0:n_], func=AF.Exp), reads=[B_tmpf], writes=[B_E1])
                        fw.op("act", lambda e: e.activation(out=E2[:, 0:n_], in_=tmpf[:, 0:n_], func=AF.Exp, scale=-1.0), reads=[B_tmpf], writes=[B_E2])
                        fw.op("dve", lambda e: e.tensor_tensor(out=qt_[:, 0:n_], in0=qT[:, t0:t0 + n_], in1=E1[:, 0:n_], op=ALU.mult),
                              reads=[B_q[bi], B_E1], writes=[B_qt])
                        fw.op("dve", lambda e: e.scalar_tensor_tensor(out=kt_[:, 0:n_], in0=sig[:, 0:n_], scalar=lbv[:, 1, di, m:m + 1],
                                                                      in1=E2[:, 0:n_], op0=ALU.mult, op1=ALU.mult),
                              reads=[B_sig, B_E2, B_lb], writes=[B_kt])
                        fw.op("dve", lambda e: e.tensor_tensor(out=tm3, in0=bbv3, in1=bbv3[:, :, endc:endc + 1].broadcast_to((128, nch, 64)),
                                                               op=ALU.subtract), reads=[B_bbv], writes=[B_tmpf])
                        fw.op("act", lambda e: e.activation(out=E1[:, 0:n_], in_=tmpf[:, 0:n_], func=AF.Exp, scale=-1.0), reads=[B_tmpf], writes=[B_E1])
                        fw.op("dve", lambda e: e.scalar_tensor_tensor(out=kh_[:, 0:n_], in0=sig[:, 0:n_], scalar=lbv[:, 1, di, m:m + 1],
                                                                      in1=E1[:, 0:n_], op0=ALU.mult, op1=ALU.mult),
                              reads=[B_sig, B_E1, B_lb], writes=[B_kh])
                        ps2, Bp2 = PS.next()
                        psb = ps2.bitcast(BF16)
                        for j in range(nt_):
                            fw.op("pe", lambda e, j=j: e.transpose(psb[:, j * 128:(j + 1) * 128], kh_[:, j * 128:(j + 1) * 128], identb[:]),
                                  reads=[B_kh, B_const], writes=[Bp2])
                        for hf in range(2):
                            fw.op("act", lambda e, hf=hf: e.activation(out=khtok[:, hf, 0:nt_, :],
                                                                       in_=psb[:, 0:nt_ * 128].rearrange("p (c e) -> p c e", e=128),
                                                                       func=AF.Identity, scale=rowmask[:, hf:hf + 1]),
                                  reads=[Bp2, B_const], writes=[B_khtok])
                        psa, Ba = PS.next()
                        for j in range(nt_):
                            fw.op("pe", lambda e, j=j: e.matmul(psa[:, j * 128:(j + 1) * 128], lhsT=kt_[:, j * 128:(j + 1) * 128],
                                                                rhs=qt_[:, j * 128:(j + 1) * 128], start=True, stop=True),
                                  reads=[B_kt, B_qt], writes=[Ba])
                        fw.op("dve", lambda e: e.tensor_tensor(
                            out=ATm[:, 0:nt_, :], in0=psa[:, 0:n_].rearrange("p (j t) -> p j t", t=128),
                            in1=mask128[:, di, :].unsqueeze(1).broadcast_to((128, nt_, 128)), op=ALU.mult),
                            reads=[Ba, B_const], writes=[B_AT])
                        for ch in range(nch):
                            j, hf = ch // 2, ch % 2
                            pss, Bss = d["pss"]
                            fw.op("pe", lambda e, ch=ch, j=j, hf=hf, pss=pss: e.matmul(
                                pss[:, ch * 128:(ch + 1) * 128], lhsT=khtok[:, hf, j, :],
                                rhs=vtok[:, g0 + j, :], start=True, stop=True),
                                reads=[B_khtok, B_vt[bi]], writes=[Bss])

                    def back(item):
                        di, si_, (t0, n_, v) = item
                        d = Dd[di][si_]
                        bi = blk_index(t0)
                        nch = n_ // 64
                        g0 = t0 // 128
                        qt_, B_qt = d["qt"]
                        ATm, B_AT = d["ATm"]
                        ebe, B_ebe = d["ebe"]
                        SS, B_SS = d["SS"]
                        Sbf, B_Sbf = d["Sbf"]
                        carry, B_carry = d["carry"]
                        chs = list(range(nch)) if di == 0 else list(range(nch - 1, -1, -1))
                        for idx, ch in enumerate(chs):
                            jin, jout = (ch, ch + 1) if di == 0 else (ch + 1, ch)
                            pss, Bss = d["pss"]
                            if idx == 0:
                                fw.op("dve", lambda e, jin=jin: e.tensor_copy(out=SS[:, jin, :], in_=carry[:]), reads=[B_carry], writes=[B_SS])
                            fw.op("dve", lambda e, ch=ch, jin=jin, jout=jout, pss=pss: e.scalar_tensor_tensor(
                                out=SS[:, jout, :], in0=SS[:, jin, :], scalar=ebe[:, 0, ch:ch + 1],
                                in1=pss[:, ch * 128:(ch + 1) * 128], op0=ALU.mult, op1=ALU.add),
                                reads=[B_SS, B_ebe, Bss], writes=[B_SS])
                        jlast = nch if di == 0 else 0
                        fw.op("dve", lambda e: e.tensor_copy(out=carry[:], in_=SS[:, jlast, :]), reads=[B_SS], writes=[B_carry])
                        off = 0 if di == 0 else 1
                        fw.op("dve", lambda e: e.tensor_tensor(
                            out=Sbf[:, 0:nch, :], in0=SS[:, off:off + nch, :],
                            in1=ebe[:, 1, 0:nch].unsqueeze(2).broadcast_to((128, nch, 128)), op=ALU.mult),
                            reads=[B_SS, B_ebe], writes=[B_Sbf])
                        pso, Bo = PS.next()
                        for ch in range(nch):
                            j, hf = ch // 2, ch % 2
                            c_lo, c_hi = ch * 64, (ch + 1) * 64
                            fw.op("pe", lambda e: e.matmul(pso[:, c_lo:c_hi], lhsT=vtok[:, g0 + j, :], rhs=ATm[:, j, hf * 64:(hf + 1) * 64],
                                                           start=True, stop=False), reads=[B_vt[bi], B_AT], writes=[Bo])
                            fw.op("pe", lambda e: e.matmul(pso[:, c_lo:c_hi], lhsT=Sbf[:, ch, :], rhs=qt_[:, c_lo:c_hi],
                                                           start=False, stop=True), reads=[B_Sbf, B_qt], writes=[Bo])
                        if t0 not in touched:
                            touched[t0] = True
                            fw.op("act", lambda e: e.activation(out=oacc[:, t0:t0 + n_], in_=pso[:, 0:n_], func=AF.Copy), reads=[Bo], writes=[B_oa[bi]])
                        else:
                            fw.op("dve", lambda e: e.tensor_tensor(out=oacc[:, t0:t0 + n_], in0=oacc[:, t0:t0 + n_], in1=pso[:, 0:n_], op=ALU.add),
                                  reads=[Bo, B_oa[bi]], writes=[B_oa[bi]])

                    subs = []
                    for (t0, n_, v) in blks:
                        for q_ in range(n_ // SUB):
                            subs.append((t0 + q_ * SUB, SUB, v))
                    nctx_sub = NCTX // SUB if blks[0][2] == 1 else 0
                    fw_order = subs
                    bw_order = subs[:nctx_sub][::-1] + subs[nctx_sub:][::-1]
                    seq = []
                    for i_ in range(len(subs)):
                        seq.append((0, i_ % 2, fw_order[i_]))
                        seq.append((1, i_ % 2, bw_order[i_]))
                    SKEW = 3
                    for k_ in range(min(SKEW, len(seq))):
                        front(seq[k_])
                    for k_ in range(len(seq)):
                        if k_ + SKEW < len(seq):
                            front(seq[k_ + SKEW])
                        back(seq[k_])
                    for (t0, n_, v) in subs:
                        bi = blk_index(t0)
                        fw.op("act", lambda e: e.activation(out=osq[:, 0:n_], in_=oacc[:, t0:t0 + n_], func=AF.Square), reads=[B_oa[bi]], writes=[B_osq])
                        ps, Bp = PS.next()
                        fw.op("pe", lambda e: e.matmul(ps[:, 0:n_], lhsT=onesb[:], rhs=osq[:, 0:n_], start=True, stop=True), reads=[B_osq, B_const], writes=[Bp])
                        fw.op("act", lambda e: e.activation(out=rstd[:, 0:n_], in_=ps[:, 0:n_], func=AF.Sqrt, scale=1.0 / 128, bias=RMS_EPS), reads=[Bp], writes=[B_rstd])
                        fw.op("dve", lambda e: e.reciprocal(out=rstd[:, 0:n_], in_=rstd[:, 0:n_]), reads=[B_rstd], writes=[B_rstd])
                        fw.op("dve", lambda e: e.scalar_tensor_tensor(out=ontmp[:, 0:n_], in0=oacc[:, t0:t0 + n_], scalar=hgnw[:, m:m + 1], in1=rstd[:, 0:n_],
                                                                      op0=ALU.mult, op1=ALU.mult), reads=[B_oa[bi], B_rstd, B_const], writes=[B_ontmp])
                        ps, Bp = PS.next()
                        mm8(ps, wg, t0, n_, Bs3, bi, Bp)
                        fw.op("act", lambda e: e.activation(out=sgb[:, 0:n_], in_=ps[:, 0:n_], func=AF.Silu), reads=[Bp], writes=[B_sgb])
                        fw.op("dve", lambda e: e.tensor_tensor(out=onb[:, 0:n_], in0=ontmp[:, 0:n_], in1=sgb[:, 0:n_], op=ALU.mult),
                              reads=[B_ontmp, B_sgb], writes=[B_onb])
                        for mo in range(KC):
                            ps, Bp = PS.next()
                            fw.op("pe", lambda e: e.matmul(ps[:, 0:n_], lhsT=wo[:, mo * 128:(mo + 1) * 128], rhs=onb[:, 0:n_], start=True, stop=True),
                                  reads=[Bs3, B_onb], writes=[Bp])
                            fw.op("dve", add_branch(ps[:, 0:n_], l, 0, mo, t0, n_, v), reads=[Bp, B_modvs[l], B_x[mo][bi]], writes=[B_x[mo][bi]])
                PS.nrot = 7
                PS.i = 0

            def na_mixer(l, blks, ph):
                def t(name, shape, dt=F32):
                    return sb(name, shape, dt, stack=ph)
                NT = T // 128
                qT = t("naq", [128, T], BF16)
                kT = t("nak", [128, T], BF16)
                vblk, B_vblk = t("navb", [128, 512], BF16), Buf()
                vtok = t("navt", [128, NT, 2, 65], BF16)
                otok = t("naot", [128, NT, 128], BF16)
                oT = t("naoT", [128, T], BF16)
                mask, B_mask = t("namask", [128, NPAT, 128], BF16), Buf()
                EB = [(t("naEB%d" % i, [128, NPAT, 128]), Buf(), ch_ebs[i]) for i in range(2)]
                PT = [(t("naPT%d" % i, [128, 7 * 128], BF16), Buf()) for i in range(4)]
                rden = [(t("narden%d" % i, [128, 1]), Buf()) for i in range(4)]
                B_q = [Buf() for _ in range(NT)]
                B_k = [Buf() for _ in range(NT)]
                B_vt = [Buf() for _ in range(NT)]
                B_ot = [Buf() for _ in range(NT)]
                B_oT = [Buf() for _ in range(5)]
                fw.dma("sp", ch_in, mask[:], namask_d, writes=[B_mask])
                fw.op("dve", lambda e: e.memset(vtok[:, :, :, 64:65], 1.0), writes=B_vt)
                units = []
                for m in range(16):
                    if m < 2:
                        units.append((2 + m, [2, 3, 4, 5], 5 + 4 * m))
                    elif m >= 14:
                        units.append((2 + m, [14, 15, 16, 17], 5 + 4 * (m - 12)))
                    else:
                        units.append((2 + m, [m + i for i in range(5)], 0))
                units.append((0, [], 0))
                units.append((1, [], 0))

                def wview(s, h):
                    return s[:, h * 1024:(h + 1) * 1024].rearrange("p (k c) -> p k c", k=KC)

                def wsrc(col0):
                    return naqkv_d[:, col0:col0 + 128].rearrange("(k p) c -> p k c", p=128)

                def mm8(ps, wv, t0, n_, Bs, bi, Bp):
                    for k in range(KC):
                        fw.op("pe", lambda e, k=k: e.matmul(ps[:, 0:n_], lhsT=wv[:, k, :], rhs=hT[:, k, t0:t0 + n_],
                                                            start=(k == 0), stop=(k == KC - 1)),
                              reads=[Bs, B_h[k][bi]], writes=[Bp])
                nu = 0
                for mp in range(KC):
                    c0 = mp * 128
                    sl1, Bs1 = ring.get([(lambda s: wview(s, 0), wsrc(c0)), (lambda s: wview(s, 1), wsrc(D + c0))])
                    sl2, Bs2 = ring.get([(lambda s: wview(s, 0), wsrc(2 * D + c0)),
                                         (lambda s: s[:, 1024:2048], naout_d[c0:c0 + 128, :])])
                    wq, wk, wv_ = wview(sl1, 0), wview(sl1, 1), wview(sl2, 0)
                    wo = sl2[:, 1024:2048]
                    for (t0, n_, v) in blks:
                        bi = blk_index(t0)
                        nt_ = n_ // 128
                        g0 = t0 // 128
                        tb_ = list(range(g0, g0 + nt_))
                        ps, Bp = PS.next()
                        mm8(ps, wq, t0, n_, Bs1, bi, Bp)
                        fw.op("act", lambda e: e.activation(out=qT[:, t0:t0 + n_], in_=ps[:, 0:n_], func=AF.Copy), reads=[Bp], writes=[B_q[i] for i in tb_])
                        ps, Bp = PS.next()
                        mm8(ps, wk, t0, n_, Bs1, bi, Bp)
                        fw.op("dve", lambda e: e.tensor_copy(out=kT[:, t0:t0 + n_], in_=ps[:, 0:n_]), reads=[Bp], writes=[B_k[i] for i in tb_])
                        ps, Bp = PS.next()
                        mm8(ps, wv_, t0, n_, Bs2, bi, Bp)
                        fw.op("act", lambda e: e.activation(out=vblk[:, 0:n_], in_=ps[:, 0:n_], func=AF.Copy), reads=[Bp], writes=[B_vblk])
                        ps2, Bp2 = PS.next()
                        psb = ps2.bitcast(BF16)
                        for j in range(nt_):
                            fw.op("pe", lambda e, j=j: e.transpose(psb[:, j * 128:(j + 1) * 128], vblk[:, j * 128:(j + 1) * 128], identb[:]),
                                  reads=[B_vblk, B_const], writes=[Bp2])
                        for hh in range(2):
                            fw.op("dve", lambda e, hh=hh: e.tensor_copy(
                                out=vtok[:, g0:g0 + nt_, hh, 0:64],
                                in_=psb[:, 0:nt_ * 128].rearrange("p (j h e) -> p j h e", h=2, e=64)[:, :, hh, :]),
                                reads=[Bp2], writes=[B_vt[i] for i in tb_])
                    for hh in range(2):
                        h = 2 * mp + hh
                        pb = hh * 64
                        eb, B_eb, ch_eb = EB[h % 2]
                        fw.dma("sp", ch_eb, eb[:], nabias_d[h], writes=[B_eb])
                        fw.op("act", lambda e: e.activation(out=eb[:], in_=eb[:], func=AF.Exp), reads=[B_eb], writes=[B_eb])
                        fw.op("dve", lambda e: e.tensor_tensor(out=eb[:], in0=eb[:], in1=mask[:], op=ALU.mult), reads=[B_eb, B_mask], writes=[B_eb])
                        def front(u):
                            qt, ktiles, p0 = units[u]
                            nk = len(ktiles)
                            tiles = ktiles + [0, 1]
                            ntl = len(tiles)
                            pt, B_pt = PT[u % len(PT)]
                            banks = []
                            for j, kt_i in enumerate(tiles):
                                if j % 4 == 0:
                                    banks.append(PS.next())
                                psx, Bx = banks[-1]
                                jj = j % 4
                                fw.op("pe", lambda e, psx=psx, jj=jj, kt_i=kt_i: e.matmul(
                                    psx[:, jj * 128:(jj + 1) * 128], lhsT=kT[pb:pb + 64, kt_i * 128:(kt_i + 1) * 128],
                                    rhs=qT[pb:pb + 64, qt * 128:(qt + 1) * 128], start=True, stop=True),
                                    reads=[B_k[kt_i], B_q[qt]], writes=[Bx])
                            for bidx, (psx, Bx) in enumerate(banks):
                                w_ = min(4, ntl - 4 * bidx) * 128
                                fw.op("act", lambda e, psx=psx, w_=w_, bidx=bidx: e.activation(
                                    out=pt[:, bidx * 512:bidx * 512 + w_], in_=psx[:, 0:w_], func=AF.Exp, scale=0.125),
                                    reads=[Bx], writes=[B_pt])
                            if nk:
                                fw.op("dve", lambda e: e.tensor_tensor(
                                    out=pt[:, 0:nk * 128], in0=pt[:, 0:nk * 128],
                                    in1=eb[:, p0:p0 + nk, :].rearrange("p a b -> p (a b)"), op=ALU.mult),
                                    reads=[B_pt, B_eb], writes=[B_pt])

                        def back(u):
                            qt, ktiles, p0 = units[u]
                            tiles = ktiles + [0, 1]
                            ntl = len(tiles)
                            pt, B_pt = PT[u % len(PT)]
                            rd, B_rd = rden[u % len(rden)]
                            pso, Bo = PS.next()
                            for j, kt_i in enumerate(tiles):
                                fw.op("pe", lambda e, j=j, kt_i=kt_i: e.matmul(
                                    pso[:, 0:65], lhsT=pt[:, j * 128:(j + 1) * 128], rhs=vtok[:, kt_i, hh, :],
                                    start=(j == 0), stop=(j == ntl - 1)), reads=[B_pt, B_vt[kt_i]], writes=[Bo])
                            fw.op("dve", lambda e: e.reciprocal(out=rd[:], in_=pso[:, 64:65]), reads=[Bo], writes=[B_rd])
                            fw.op("act", lambda e: e.activation(out=otok[:, qt, pb:pb + 64], in_=pso[:, 0:64], func=AF.Identity, scale=rd[:, 0:1]),
                                  reads=[Bo, B_rd], writes=[B_ot[qt]])

                        NSK = 2
                        for u in range(min(NSK, len(units))):
                            front(u)
                        for u in range(len(units)):
                            if u + NSK < len(units):
                                front(u + NSK)
                            back(u)
                    for (t0, n_, v) in blks:
                        bi = blk_index(t0)
                        nt_ = n_ // 128
                        g0 = t0 // 128
                        ps2, Bp2 = PS.next()
                        psb = ps2.bitcast(BF16)
                        for j in range(nt_):
                            fw.op("pe", lambda e, j=j: e.transpose(psb[:, j * 128:(j + 1) * 128], otok[:, g0 + j, :], identb[:]),
                                  reads=[B_ot[g0 + j], B_const], writes=[Bp2])
                        fw.op("act", lambda e: e.activation(out=oT[:, t0:t0 + n_], in_=psb[:, 0:n_], func=AF.Copy), reads=[Bp2], writes=[B_oT[bi]])
                        for mo in range(KC):
                            ps, Bp = PS.next()
                            fw.op("pe", lambda e: e.matmul(ps[:, 0:n_], lhsT=wo[:, mo * 128:(mo + 1) * 128], rhs=oT[:, t0:t0 + n_], start=True, stop=True),
                                  reads=[Bs2, B_oT[bi]], writes=[Bp])
                            fw.op("dve", add_branch(ps[:, 0:n_], l, 0, mo, t0, n_, v), reads=[Bp, B_modvs[l], B_x[mo][bi]], writes=[B_x[mo][bi]])

            for l in range(nlayers):
                last = l == DEPTH - 1
                kind = l % 3
                blks = token_blocks(not last)
                if l == 0:
                    modulate(0, 0, blks)
                with contextlib.ExitStack() as ph:
                    if kind == 0:
                        pool_mixer(l, blks, ph)
                    elif kind == 1:
                        hgrn_mixer(l, blks, ph)
                    else:
                        na_mixer(l, blks, ph)
                    fw.barrier()
                with contextlib.ExitStack() as ph:
                    ybf = (sb("ybf", [128, KC, 512], BF16, stack=ph), Buf())
                    ysq = (sb("ysq", [128, KC, 512], BF16, stack=ph), Buf())
                    stt = (sb("lnst", [128, 4, 512], stack=ph), Buf())
                    act = (sb("act", [128, 6, T], BF16, stack=ph), [[Buf() for _ in range(5)] for _ in range(6)])
                    sg = [(sb("sg%d" % i, [128, 512], BF16, stack=ph), Buf()) for i in range(2)]
                    layer_norm(l, 0, blks, (ybf, ysq, stt), (l, 1))
                    ffn(l, blks, act, sg)
                    nxt = None if last else (l + 1, 0)
                    layer_norm(l, 1, blks, (ybf, ysq, stt), nxt)
                    fw.barrier()
            if dbg:
                fw.dma("sp", ch_dbg, dbg_d, xT[:], reads=all_x)
            with contextlib.ExitStack() as ph:
                xo = sb("xo", [128, 2, D], stack=ph)
                B_xo = [Buf(), Buf()]
                for i in range(SEQ // 128):
                    s = i % 2
                    tt = NCTX + i * 128
                    for half in range(2):
                        psb, Bp = PS.next()
                        for q in range(4):
                            c = half * 4 + q
                            fw.op("pe", lambda e, c=c, q=q, psb=psb, tt=tt: e.transpose(
                                psb[:, q * 128:(q + 1) * 128], xT[:, c, tt:tt + 128], ident[:]),
                                reads=[B_x[c][blk_index(tt)], B_const], writes=[Bp])
                        dst = xo[:, s, half * 512:(half + 1) * 512]
                        if half == 0:
                            fw.op("act", lambda e, dst=dst, psb=psb: e.activation(out=dst, in_=psb, func=AF.Copy), reads=[Bp], writes=[B_xo[s]])
                        else:
                            fw.op("dve", lambda e, dst=dst, psb=psb: e.tensor_copy(out=dst, in_=psb), reads=[Bp], writes=[B_xo[s]])
                    fw.dma("sp", ch_outs[s], out_d[i * 128:(i + 1) * 128, :], xo[:, s], reads=[B_xo[s]])
                if not fw.dry:
                    fw._wait_for(fw.E["sp"], [(c_[0], c_[1], {}, "dma") for c_ in ch_outs + [ch_dbg] if c_[1]])

        fw.dry = True
        gen()
        fw.dry = False
        PS.i = 0
        PS.nrot = 7
        gen()
    return nc


def na_patterns():
    pats = [(8, 4 + 2 * i) for i in range(5)]
    for qr0, krs in ((0, (0, 2, 4, 6)), (2, (0, 2, 4, 6)), (28, (24, 26, 28, 30)), (30, (24, 26, 28, 30))):
        for kr0 in krs:
            pats.append((qr0, kr0))
    kk = np.arange(128)
    kro, kc = (kk // 64)[:, None], (kk % 64)[:, None]
    qro, qc = (kk // 64)[None, :], (kk % 64)[None, :]
    drow = np.zeros((NPAT, 128, 128), np.int64)
    dcol = np.zeros((NPAT, 128, 128), np.int64)
    valid = np.zeros((NPAT, 128, 128), bool)
    for p, (qr0, kr0) in enumerate(pats):
        kr = kr0 + kro
        qr = qr0 + qro
        rs = np.clip(qr - 4, 0, 24)
        ws = np.clip(qc - 8, 0, 48)
        valid[p] = (kr >= rs) & (kr < rs + 8) & (kc >= ws) & (kc < ws + 16)
        drow[p] = np.clip(kr - qr + 7, 0, 14)
        dcol[p] = np.clip(kc - qc + 15, 0, 30)
    return drow, dcol, valid


def host_inputs(inputs, nlayers=DEPTH):
    f32 = np.float32
    drow, dcol, valid = na_patterns()
    rpb = np.asarray(inputs["na_rpb"], f32)[0]
    na_bias = rpb[:, drow, dcol].transpose(0, 2, 1, 3)
    na_mask = valid.transpose(1, 0, 2).astype(ml_dtypes.bfloat16)
    x = np.asarray(inputs["x"], f32)
    c = np.asarray(inputs["c"], f32)
    ctx = np.asarray(inputs["ctx"], f32)
    c_ctx = np.asarray(inputs["c_ctx"], f32)
    mod_b = np.asarray(inputs["mod_b"], f32)
    modb2 = np.repeat(mod_b.reshape(DEPTH, 48, 128).transpose(2, 0, 1)[..., None], 2, axis=-1)
    lnp = np.stack([np.asarray(inputs["ln_g"], f32), np.asarray(inputs["ln_b"], f32)], axis=2)
    lnp = lnp.reshape(DEPTH, 2, 2, KC, 128).transpose(4, 0, 1, 2, 3)
    poolsc = np.asarray(inputs["pool_scale"], f32).reshape(2, KC, 128).transpose(2, 0, 1)
    edge = np.ones((4, 16), f32)
    for wi, w in enumerate(POOL_WINDOWS):
        lo = w // 2
        hi = w - 1 - lo
        for t in range(8):
            cnt = min(t + hi + 1, 10 ** 6) - max(t - lo, 0)
            edge[wi, t] = f32(w) / f32(cnt)
            tr = 7 - t
            cnt = min(tr + lo + 1, w) if tr < hi else w
            edge[wi, 8 + t] = f32(w) / f32(cnt)
    edge = np.broadcast_to(edge[None], (128, 4, 16))
    common = {
        "mod_w": np.ascontiguousarray(inputs["mod_w"], f32),
        "mod_b2": np.ascontiguousarray(modb2),
        "lnp": np.ascontiguousarray(lnp),
        "ffn_w_in": np.ascontiguousarray(inputs["ffn_w_in"], f32),
        "ffn_w_out": np.ascontiguousarray(inputs["ffn_w_out"], f32),
        "pool_w": np.ascontiguousarray(inputs["pool_w"], f32),
        "pool_sc": np.ascontiguousarray(poolsc),
        "hgrn_w_in": np.ascontiguousarray(np.asarray(inputs["hgrn_w_in"], f32)[0]),
        "hgrn_w_out": np.ascontiguousarray(np.asarray(inputs["hgrn_w_out"], f32)[0]),
        "hgrn_nw": np.ascontiguousarray(np.asarray(inputs["hgrn_norm_w"], f32)[0].reshape(KC, 128).T),
        "hgrn_lbl": np.ascontiguousarray(np.asarray(inputs["hgrn_lb_logits"], f32).reshape(2, DEPTH, KC, 128).transpose(3, 0, 1, 2)),
        "mask128": np.ascontiguousarray(np.stack([np.kron(np.eye(2, dtype=f32), np.triu(np.ones((64, 64), f32))),
                                                  np.kron(np.eye(2, dtype=f32), np.tril(np.ones((64, 64), f32)))], axis=1)),
        "reset01": np.ascontiguousarray(np.broadcast_to((np.arange(512) % 64 != 0).astype(f32)[None], (128, 512))),
        "na_w_qkv": np.ascontiguousarray(np.asarray(inputs["na_w_qkv"], f32)[0]),
        "na_w_out": np.ascontiguousarray(np.asarray(inputs["na_w_out"], f32)[0]),
        "na_bias": np.ascontiguousarray(na_bias),
        "na_mask": np.ascontiguousarray(na_mask),
        "ident": np.eye(128, dtype=f32),
        "edgefac": np.ascontiguousarray(edge),
    }
    maps = []
    for b in range(x.shape[0]):
        cv = np.stack([c[b], c_ctx], axis=-1).reshape(KC, 128, 2).transpose(1, 0, 2)
        m = dict(common)
        m["x"] = np.ascontiguousarray(x[b])
        m["ctx"] = np.ascontiguousarray(ctx[b])
        m["cvec"] = np.ascontiguousarray(cv)
        maps.append(m)
    return maps


_NC_CACHE = {}


def kernel(**inputs):
    maps = host_inputs(inputs)
    if "nc" not in _NC_CACHE:
        _NC_CACHE["nc"] = build()
    nc = _NC_CACHE["nc"]
    res = run_bass_kernel_spmd(nc, maps, core_ids=list(range(len(maps))))
    return np.stack([r["out"] for r in res.results], axis=0).astype(np.float32)
```

```python
import contextlib
import numpy as np
import ml_dtypes
import concourse.bass as bass
import concourse.mybir as mybir
from concourse.bass_utils import run_bass_kernel_spmd

AF = mybir.ActivationFunctionType
ALU = mybir.AluOpType
F32 = mybir.dt.float32
BF16 = mybir.dt.bfloat16

D = 1024
KC = 8
NCTX = 256
SEQ = 2048
T = NCTX + SEQ
DEPTH = 4
FH = 2816
JC = FH // 128
ALPHA = (2 * DEPTH) ** 0.25
LN_EPS = 1e-6
RMS_EPS = 1e-6
EPS_LN = LN_EPS / (ALPHA * ALPHA)
POOL_WINDOWS = (2, 4, 8, 16)
NSLOT = 6
SLOTW = 2048
NPAT = 21


class Buf:
    __slots__ = ("w", "r", "name")

    def __init__(self, name=""):
        self.w = None
        self.r = {}
        self.name = name


class _Eng:
    def __init__(self, name, eng, sem):
        self.name = name
        self.eng = eng
        self.sem = sem
        self.count = 0
        self.known = {}


class FW:
    def __init__(self, nc, stack):
        self.nc = nc
        self.stack = stack
        self.dry = False
        self.E = {}
        for name, eng in (("pe", nc.tensor), ("act", nc.scalar), ("dve", nc.vector),
                          ("pool", nc.gpsimd), ("sp", nc.sync)):
            sem = stack.enter_context(nc.semaphore("sem_" + name))
            self.E[name] = _Eng(name, eng, sem)
        self.nwaits = 0
        self.nins = 0
        self.chans = []

    def new_chan(self, name):
        sem = self.stack.enter_context(self.nc.semaphore("ch_" + name))
        ch = [sem, 0]
        self.chans.append(ch)
        return ch

    def _wait_for(self, E, evs):
        need = {}
        for ev in evs:
            if ev is None:
                continue
            sem, val, clock, src = ev
            if src == "pe" and E.name == "pe":
                continue
            k = id(sem)
            if E.known.get(k, 0) >= val:
                continue
            if k not in need or need[k][1] < val:
                need[k] = (sem, val, clock)
        for k, (sem, val, clock) in need.items():
            if E.known.get(k, 0) >= val:
                continue
            E.eng.wait_ge(sem, val)
            self.nwaits += 1
            E.known[k] = val
            for kk, vv in clock.items():
                if E.known.get(kk, 0) < vv:
                    E.known[kk] = vv

    @staticmethod
    def _deps(reads, writes):
        evs = []
        for b in reads:
            evs.append(b.w)
        for b in writes:
            evs.append(b.w)
            evs.extend(b.r.values())
        return evs

    @staticmethod
    def _record(ev, reads, writes):
        k = id(ev[0])
        for b in reads:
            b.r[k] = ev
        for b in writes:
            b.w = ev
            b.r = {}

    def op(self, en, fn, reads=(), writes=()):
        if self.dry:
            return
        E = self.E[en]
        self._wait_for(E, self._deps(reads, writes))
        ins = fn(E.eng)
        E.count += 1
        ins.then_inc(E.sem, 1)
        self.nins += 1
        ev = (E.sem, E.count, dict(E.known), en)
        self._record(ev, reads, writes)

    def dma(self, en, chan, out, in_, reads=(), writes=()):
        if self.dry:
            return
        E = self.E[en]
        self._wait_for(E, self._deps(reads, writes))
        ins = E.eng.dma_start(out=out, in_=in_)
        chan[1] += 16
        ins.then_inc(chan[0], 16)
        self.nins += 1
        ev = (chan[0], chan[1], dict(E.known), "dma")
        self._record(ev, reads, writes)

    def barrier(self, engines=("pe", "act", "dve", "sp")):
        if self.dry:
            return
        evs = []
        for n in ("pe", "act", "dve", "sp"):
            E = self.E[n]
            if E.count:
                evs.append((E.sem, E.count, {}, "x"))
        for ch in self.chans:
            if ch[1] and not ch[2:]:
                evs.append((ch[0], ch[1], {}, "dma"))
        for n in engines:
            self._wait_for(self.E[n], evs)


class Ring:
    def __init__(self, fw, tile):
        self.fw = fw
        self.tile = tile
        self.bufs = [Buf("slot%d" % i) for i in range(NSLOT)]
        self.chs = [fw.new_chan("slot%d" % i) for i in range(NSLOT)]
        for ch in self.chs:
            ch.append("ring")
        self.reqs = []
        self.pos = 0
        self.issued = 0

    def reset(self):
        self.pos = 0
        self.issued = 0

    def get(self, loads):
        i = self.pos
        self.pos += 1
        if self.fw.dry:
            self.reqs.append(loads)
            return self.tile[:, 0, :], self.bufs[0]
        hi = min(len(self.reqs), i + NSLOT - 2)
        while self.issued < hi:
            r = self.issued
            s = r % NSLOT
            for dst_fn, src in self.reqs[r]:
                self.fw.dma("pool", self.chs[s], dst_fn(self.tile[:, s, :]), src, writes=[self.bufs[s]])
            self.issued += 1
        s = i % NSLOT
        return self.tile[:, s, :], self.bufs[s]


class PsumPool:
    def __init__(self, tile):
        self.tile = tile
        self.bufs = [Buf("ps%d" % i) for i in range(8)]
        self.i = 0
        self.nrot = 7

    def next(self):
        i = self.i
        self.i = (i + 1) % self.nrot
        return self.tile[:, i, :], self.bufs[i]

    def modbank(self):
        return self.tile[:, 7, :], self.bufs[7]


def token_blocks(with_ctx):
    blks = [(0, NCTX, 1)] if with_ctx else []
    for i in range(4):
        blks.append((NCTX + 512 * i, 512, 0))
    return blks


def build(nlayers=DEPTH, dbg=False):
    nc = bass.Bass("TRN2", target_bir_lowering=False)

    def din(name, shape, dt=F32):
        return nc.dram_tensor(name, list(shape), dt, kind="ExternalInput").ap()

    x_d = din("x", [SEQ, D])
    ctx_d = din("ctx", [NCTX, D])
    cvec_d = din("cvec", [128, KC, 2])
    modw_d = din("mod_w", [DEPTH, D, 6 * D])
    modb_d = din("mod_b2", [128, DEPTH, 48, 2])
    lnp_d = din("lnp", [128, DEPTH, 2, 2, KC])
    ffnin_d = din("ffn_w_in", [DEPTH, D, 2 * FH])
    ffnout_d = din("ffn_w_out", [DEPTH, FH, D])
    poolw_d = din("pool_w", [2, 4, 256, 256])
    poolsc_d = din("pool_sc", [128, 2, KC])
    ident_d = din("ident", [128, 128])
    edge_d = din("edgefac", [128, 4, 16])
    hgin_d = din("hgrn_w_in", [D, 5 * D])
    hgout_d = din("hgrn_w_out", [D, D])
    hgnw_d = din("hgrn_nw", [128, KC])
    hglb_d = din("hgrn_lbl", [128, 2, DEPTH, KC])
    mask128_d = din("mask128", [128, 2, 128])
    reset_d = din("reset01", [128, 512])
    naqkv_d = din("na_w_qkv", [D, 3 * D])
    naout_d = din("na_w_out", [D, D])
    nabias_d = din("na_bias", [16, 128, NPAT, 128])
    namask_d = din("na_mask", [128, NPAT, 128], BF16)
    out_d = nc.dram_tensor("out", [SEQ, D], F32, kind="ExternalOutput").ap()
    dbg_d = nc.dram_tensor("dbg", [128, KC, T], F32, kind="ExternalOutput").ap() if dbg else None

    with contextlib.ExitStack() as st:
        fw = FW(nc, st)

        uniq = [0]

        def sb(name, shape, dt=F32, stack=st):
            uniq[0] += 1
            return stack.enter_context(nc.sbuf_tensor("s%d_%s" % (uniq[0], name), list(shape), dt))

        xT = sb("xT", [128, KC, T])
        hT = sb("hT", [128, KC, T], BF16)
        ring_t = sb("ring", [128, NSLOT, SLOTW], BF16)
        modv = sb("modv", [128, DEPTH, 6, KC, 2])
        galpha = sb("galpha", [128, DEPTH, 2, KC, 2])
        lnp = sb("lnp", [128, DEPTH, 2, 2, KC])
        ident = sb("ident", [128, 128])
        identb = sb("identb", [128, 128], BF16)
        onesb = sb("onesb", [128, 128], BF16)
        edgef = sb("edgef", [128, 4, 16])
        poolsc = sb("poolsc", [128, 2, KC])
        cvec = sb("cvec", [128, KC, 2])
        sTb = sb("sTb", [128, KC, 2], BF16)
        modb = sb("modb", [128, DEPTH, 48, 2])
        hgnw = sb("hgnw", [128, KC])
        hglb = sb("hglb", [128, 2, DEPTH, KC])
        lbv = sb("lbv", [128, 4, 2, KC])
        mask128 = sb("mask128", [128, 2, 128])
        reset01 = sb("reset01", [128, 512])
        ps_t = st.enter_context(nc.psum_tensor("ps", [128, 8, 512], F32))
        PS = PsumPool(ps_t)
        ring = Ring(fw, ring_t)

        B_x = [[Buf("x%d_%d" % (c, b)) for b in range(5)] for c in range(KC)]
        B_h = [[Buf("h%d_%d" % (c, b)) for b in range(5)] for c in range(KC)]
        B_const = Buf("const")
        B_modvs = [Buf("modv%d" % i) for i in range(DEPTH)]
        ch_in = fw.new_chan("in")
        ch_outs = [fw.new_chan("out0"), fw.new_chan("out1")]
        ch_dbg = fw.new_chan("dbg")
        ch_mw = [fw.new_chan("mw0"), fw.new_chan("mw1")]
        ch_xin = [fw.new_chan("xin0"), fw.new_chan("xin1")]
        ch_ebs = [fw.new_chan("eb0"), fw.new_chan("eb1")]

        def blk_index(t0):
            return 0 if t0 < NCTX else 1 + (t0 - NCTX) // 512

        def xbufs(t0, chunks=range(KC)):
            return [B_x[c][blk_index(t0)] for c in chunks]

        def hbufs(t0, chunks=range(KC)):
            return [B_h[c][blk_index(t0)] for c in chunks]

        all_x = [b for row in B_x for b in row]
        all_h = [b for row in B_h for b in row]

        def gen():
            ring.reset()
            for dst, src in ((cvec, cvec_d), (lnp, lnp_d), (ident, ident_d), (edgef, edge_d), (poolsc, poolsc_d),
                             (hgnw, hgnw_d), (hglb, hglb_d), (mask128, mask128_d), (reset01, reset_d)):
                fw.dma("sp", ch_in, dst[:], src, writes=[B_const])
            fw.op("dve", lambda e: e.memset(onesb[:], 1.0), writes=[B_const])
            fw.op("act", lambda e: e.activation(out=identb[:], in_=ident[:], func=AF.Copy), reads=[B_const], writes=[B_const])
            fw.dma("sp", ch_in, modb[:], modb_d, writes=[B_const])
            fw.op("act", lambda e: e.activation(out=sTb[:], in_=cvec[:], func=AF.Silu), reads=[B_const], writes=[B_const])

            def mod_compute(l, part, nparts):
                if l >= nlayers:
                    return
                psb, Bp = PS.modbank()
                pv = psb[:, 0:96].rearrange("p (j v) -> p j v", v=2)
                per = 24 // nparts
                for sl in range(part * per, (part + 1) * per):
                    slot, Bs = ring.get([(lambda s_: s_.rearrange("p (k c) -> p k c", k=KC),
                                          modw_d[l, :, sl * 256:(sl + 1) * 256].rearrange("(kc p) c -> p kc c", p=128))])
                    wv = slot.rearrange("p (k c) -> p k c", k=KC)
                    for q in range(2):
                        oc = sl * 2 + q
                        for kc in range(KC):
                            fw.op("pe", lambda e, q=q, kc=kc, oc=oc: e.matmul(
                                pv[:, oc, :], lhsT=wv[:, kc, q * 128:(q + 1) * 128], rhs=sTb[:, kc, :],
                                start=(kc == 0), stop=(kc == KC - 1)),
                                reads=[Bs, B_const], writes=[Bp])
                if part != nparts - 1:
                    return
                Bm = B_modvs[l]
                fw.op("dve", lambda e: e.tensor_tensor(
                    out=modv[:, l].rearrange("p m c v -> p (m c) v"), in0=pv, in1=modb[:, l], op=ALU.add),
                    reads=[Bp, B_const], writes=[Bm])
                for m in (1, 4):
                    fw.op("dve", lambda e, m=m: e.tensor_scalar_add(out=modv[:, l, m], in0=modv[:, l, m], scalar1=1.0),
                          reads=[Bm], writes=[Bm])
                for w_, m in ((0, 2), (1, 5)):
                    fw.op("dve", lambda e, m=m, w_=w_: e.tensor_scalar_mul(
                        out=galpha[:, l, w_], in0=modv[:, l, m], scalar1=1.0 / ALPHA),
                        reads=[Bm], writes=[Bm])
                if l % 3 == 0:
                    j = l // 3
                    fw.op("dve", lambda e, j=j: e.tensor_tensor(
                        out=galpha[:, l, 0], in0=galpha[:, l, 0],
                        in1=poolsc[:, j].unsqueeze(2).broadcast_to((128, KC, 2)), op=ALU.mult),
                        reads=[Bm, B_const], writes=[Bm])

            mod_compute(0, 0, 1)
            with contextlib.ExitStack() as ph:
                xin = sb("xin", [128, 2, D], stack=ph)
                B_xin = [Buf("xin0"), Buf("xin1")]
                for i in range(T // 128):
                    s = i % 2
                    src = ctx_d[i * 128:(i + 1) * 128, :] if i < 2 else x_d[(i - 2) * 128:(i - 1) * 128, :]
                    fw.dma("sp", ch_xin[s], xin[:, s], src, writes=[B_xin[s]])
                    for half in range(2):
                        psb, Bp = PS.next()
                        for q in range(4):
                            c = half * 4 + q
                            fw.op("pe", lambda e, s=s, c=c, q=q, psb=psb: e.transpose(
                                psb[:, q * 128:(q + 1) * 128], xin[:, s, c * 128:(c + 1) * 128], ident[:]),
                                reads=[B_xin[s], B_const], writes=[Bp])
                        eng = "act" if half == 0 else "dve"
                        dst = xT[:, half * 4:half * 4 + 4, i * 128:(i + 1) * 128]
                        srcv = psb.rearrange("p (q t) -> p q t", q=4)
                        if eng == "act":
                            fw.op("act", lambda e, dst=dst, srcv=srcv: e.activation(out=dst, in_=srcv, func=AF.Copy),
                                  reads=[Bp], writes=xbufs(i * 128, range(half * 4, half * 4 + 4)))
                        else:
                            fw.op("dve", lambda e, dst=dst, srcv=srcv: e.tensor_copy(out=dst, in_=srcv),
                                  reads=[Bp], writes=xbufs(i * 128, range(half * 4, half * 4 + 4)))
                fw.barrier()

            def modulate(l, which, blks):
                for (t0, n_, v) in blks:
                    for c in range(KC):
                        fw.op("act", lambda e, c=c, t0=t0, n_=n_, v=v: e.activation(
                            out=hT[:, c, t0:t0 + n_], in_=xT[:, c, t0:t0 + n_], func=AF.Identity,
                            scale=modv[:, l, 3 * which + 1, c, v:v + 1], bias=modv[:, l, 3 * which, c, v:v + 1]),
                            reads=[B_x[c][blk_index(t0)], B_modvs[l]], writes=[B_h[c][blk_index(t0)]])

            def layer_norm(l, which, blks, lnt, next_mod):
                ybf, ysq, st_t = lnt
                for (t0, n_, v) in blks:
                    bi = blk_index(t0)
                    xb = [B_x[c][bi] for c in range(KC)]
                    hb = [B_h[c][bi] for c in range(KC)]
                    B_y, B_q, B_s = ybf[1], ysq[1], st_t[1]
                    fw.op("act", lambda e, t0=t0, n_=n_: e.activation(out=ybf[0][:, :, 0:n_], in_=xT[:, :, t0:t0 + n_], func=AF.Copy),
                          reads=xb, writes=[B_y])
                    fw.op("act", lambda e, t0=t0, n_=n_: e.activation(out=ysq[0][:, :, 0:n_], in_=xT[:, :, t0:t0 + n_], func=AF.Square),
                          reads=xb, writes=[B_q])
                    ps1, Bp1 = PS.next()
                    ps2, Bp2 = PS.next()
                    for c in range(KC):
                        fw.op("pe", lambda e, c=c, n_=n_, ps1=ps1: e.matmul(ps1[:, 0:n_], lhsT=onesb[:], rhs=ybf[0][:, c, 0:n_],
                                                                          start=(c == 0), stop=(c == KC - 1)),
                              reads=[B_y, B_const], writes=[Bp1])
                    for c in range(KC):
                        fw.op("pe", lambda e, c=c, n_=n_, ps2=ps2: e.matmul(ps2[:, 0:n_], lhsT=onesb[:], rhs=ysq[0][:, c, 0:n_],
                                                                          start=(c == 0), stop=(c == KC - 1)),
                              reads=[B_q, B_const], writes=[Bp2])
                    S = st_t[0]
                    mean, msq, rstd, mr = S[:, 0, 0:n_], S[:, 1, 0:n_], S[:, 2, 0:n_], S[:, 3, 0:n_]
                    fw.op("dve", lambda e: e.tensor_scalar_mul(out=mean, in0=ps1[:, 0:n_], scalar1=1.0 / D), reads=[Bp1], writes=[B_s])
                    fw.op("dve", lambda e: e.tensor_tensor(out=msq, in0=mean, in1=mean, op=ALU.mult), reads=[B_s], writes=[B_s])
                    fw.op("dve", lambda e: e.scalar_tensor_tensor(out=msq, in0=ps2[:, 0:n_], scalar=1.0 / D, in1=msq,
                                                                  op0=ALU.mult, op1=ALU.subtract), reads=[Bp2, B_s], writes=[B_s])
                    fw.op("act", lambda e: e.activation(out=rstd, in_=msq, func=AF.Ln, bias=EPS_LN), reads=[B_s], writes=[B_s])
                    fw.op("act", lambda e: e.activation(out=rstd, in_=rstd, func=AF.Exp, scale=-0.5), reads=[B_s], writes=[B_s])
                    fw.op("dve", lambda e: e.tensor_tensor(out=mr, in0=mean, in1=rstd, op=ALU.mult), reads=[B_s], writes=[B_s])
                    xv = xT[:, :, t0:t0 + n_]
                    fw.op("dve", lambda e: e.tensor_tensor(out=xv, in0=xv, in1=rstd.unsqueeze(1).broadcast_to((128, KC, n_)), op=ALU.mult),
                          reads=[B_s] + xb, writes=xb)
                    fw.op("dve", lambda e: e.tensor_tensor(out=xv, in0=xv, in1=mr.unsqueeze(1).broadcast_to((128, KC, n_)), op=ALU.subtract),
                          reads=[B_s] + xb, writes=xb)
                    for c in range(KC):
                        fw.op("dve", lambda e, c=c: e.tensor_scalar(
                            out=xT[:, c, t0:t0 + n_], in0=xT[:, c, t0:t0 + n_],
                            scalar1=lnp[:, l, which, 0, c:c + 1], scalar2=lnp[:, l, which, 1, c:c + 1],
                            op0=ALU.mult, op1=ALU.add), reads=[xb[c], B_const], writes=[xb[c]])
                        if next_mod is not None:
                            nl, nw = next_mod
                            fw.op("act", lambda e, c=c: e.activation(
                                out=hT[:, c, t0:t0 + n_], in_=xT[:, c, t0:t0 + n_], func=AF.Identity,
                                scale=modv[:, nl, 3 * nw + 1, c, v:v + 1], bias=modv[:, nl, 3 * nw, c, v:v + 1]),
                                reads=[xb[c], B_modvs[nl]], writes=[hb[c]])

            def add_branch(psv, l, w_, c, t0, n_, v):
                return lambda e: e.scalar_tensor_tensor(
                    out=xT[:, c, t0:t0 + n_], in0=psv, scalar=galpha[:, l, w_, c, v:v + 1], in1=xT[:, c, t0:t0 + n_],
                    op0=ALU.mult, op1=ALU.add)

            def ffn(l, blks, act_t, sg_t):
                act, B_act = act_t
                slices = [list(range(0, 6)), list(range(6, 12)), list(range(12, 17)), list(range(17, 22))]
                nsg = 0
                for si_, sl in enumerate(slices):
                    for jj, j in enumerate(sl):
                        slot, Bs = ring.get([
                            (lambda s: s[:, 0:1024].rearrange("p (k c) -> p k c", k=KC),
                             ffnin_d[l, :, j * 128:(j + 1) * 128].rearrange("(k p) c -> p k c", p=128)),
                            (lambda s: s[:, 1024:2048].rearrange("p (k c) -> p k c", k=KC),
                             ffnin_d[l, :, FH + j * 128:FH + (j + 1) * 128].rearrange("(k p) c -> p k c", p=128)),
                        ])
                        wg = slot[:, 0:1024].rearrange("p (k c) -> p k c", k=KC)
                        wu = slot[:, 1024:2048].rearrange("p (k c) -> p k c", k=KC)
                        for (t0, n_, v) in blks:
                            bi = blk_index(t0)
                            psg, Bg = PS.next()
                            psu, Bu = PS.next()
                            for k in range(KC):
                                fw.op("pe", lambda e, k=k, psg=psg: e.matmul(psg[:, 0:n_], lhsT=wg[:, k, :], rhs=hT[:, k, t0:t0 + n_],
                                                                          start=(k == 0), stop=(k == KC - 1)),
                                      reads=[Bs, B_h[k][bi]], writes=[Bg])
                            for k in range(KC):
                                fw.op("pe", lambda e, k=k, psu=psu: e.matmul(psu[:, 0:n_], lhsT=wu[:, k, :], rhs=hT[:, k, t0:t0 + n_],
                                                                          start=(k == 0), stop=(k == KC - 1)),
                                      reads=[Bs, B_h[k][bi]], writes=[Bu])
                            sg, Bsg = sg_t[nsg % 2]
                            nsg += 1
                            fw.op("act", lambda e, sg=sg, psg=psg: e.activation(out=sg[:, 0:n_], in_=psg[:, 0:n_], func=AF.Silu),
                                  reads=[Bg], writes=[Bsg])
                            fw.op("dve", lambda e, sg=sg, psu=psu, jj=jj: e.tensor_tensor(
                                out=act[:, jj, t0:t0 + n_], in0=sg[:, 0:n_], in1=psu[:, 0:n_], op=ALU.mult),
                                reads=[Bsg, Bu], writes=[B_act[jj][bi]])
                    wo = []
                    for p0 in range(0, len(sl), 2):
                        js = sl[p0:p0 + 2]
                        slot, Bs = ring.get([
                            (lambda s, q=q: s[:, q * 1024:(q + 1) * 1024], ffnout_d[l, j * 128:(j + 1) * 128, :])
                            for q, j in enumerate(js)])
                        for q in range(len(js)):
                            wo.append((slot[:, q * 1024:(q + 1) * 1024], Bs))
                    for (t0, n_, v) in blks:
                        bi = blk_index(t0)
                        for m in range(KC):
                            pso, Bo = PS.next()
                            for jj in range(len(sl)):
                                wv, Bs = wo[jj]
                                fw.op("pe", lambda e, jj=jj, wv=wv, pso=pso, m=m: e.matmul(
                                    pso[:, 0:n_], lhsT=wv[:, m * 128:(m + 1) * 128], rhs=act[:, jj, t0:t0 + n_],
                                    start=(jj == 0), stop=(jj == len(sl) - 1)),
                                    reads=[Bs, B_act[jj][bi]], writes=[Bo])
                            fw.op("dve", add_branch(pso[:, 0:n_], l, 1, m, t0, n_, v),
                                  reads=[Bo, B_modvs[l], B_x[m][bi]], writes=[B_x[m][bi]])
                    mod_compute(l + 1, si_, 4)

            def pool_mixer(l, blks, ph):
                j = l // 3
                pT = hT
                B_p = B_h
                pad = sb("pad", [128, 2, SEQ + 16], stack=ph)
                tmp = sb("ptmp", [128, 2, SEQ + 16], stack=ph)
                B_pad = [Buf(), Buf()]
                B_tmp = [Buf(), Buf()]
                segs = [(NCTX, SEQ)] + ([(0, NCTX)] if blks[0][2] == 1 else [])
                it = 0
                for c in range(KC):
                    wi = c // 2
                    w = POOL_WINDOWS[wi]
                    for (s0, L) in segs:
                        s = it % 2
                        it += 1
                        sb_ = [B_h[c][blk_index(t)] for t in range(s0, s0 + L, 512)] if L > NCTX else [B_h[c][0]]
                        pb_ = [B_p[c][blk_index(t)] for t in range(s0, s0 + L, 512)] if L > NCTX else [B_p[c][0]]
                        P_ = pad[:, s]
                        Q_ = tmp[:, s]
                        fw.op("dve", lambda e, P_=P_, L=L: e.memset(P_[:, 0:8], 0.0), writes=[B_pad[s]])
                        fw.op("dve", lambda e, P_=P_, L=L: e.memset(P_[:, 8 + L:16 + L], 0.0), writes=[B_pad[s]])
                        fw.op("act", lambda e, P_=P_, L=L, c=c, s0=s0: e.activation(out=P_[:, 8:8 + L], in_=hT[:, c, s0:s0 + L], func=AF.Copy),
                              reads=sb_, writes=[B_pad[s]])
                        src, dst = P_, Q_
                        Bsrc, Bdst = B_pad[s], B_tmp[s]
                        length = L + 16
                        step = 1
                        while step < w:
                            nl_ = length - step
                            fw.op("dve", lambda e, src=src, dst=dst, nl_=nl_, step=step: e.tensor_tensor(
                                out=dst[:, 0:nl_], in0=src[:, 0:nl_], in1=src[:, step:step + nl_], op=ALU.add),
                                reads=[Bsrc], writes=[Bdst])
                            src, dst = dst, src
                            Bsrc, Bdst = Bdst, Bsrc
                            length = nl_
                            step *= 2
                        off = 8 - w // 2
                        fw.op("act", lambda e, src=src, dst=dst, off=off, L=L, w=w: e.activation(
                            out=dst[:, 0:L], in_=src[:, off:off + L], func=AF.Copy, scale=1.0 / w), reads=[Bsrc], writes=[Bdst])
                        fw.op("dve", lambda e, dst=dst, wi=wi: e.tensor_tensor(out=dst[:, 0:8], in0=dst[:, 0:8], in1=edgef[:, wi, 0:8], op=ALU.mult),
                              reads=[B_const], writes=[Bdst])
                        fw.op("dve", lambda e, dst=dst, wi=wi, L=L: e.tensor_tensor(out=dst[:, L - 8:L], in0=dst[:, L - 8:L], in1=edgef[:, wi, 8:16], op=ALU.mult),
                              reads=[B_const], writes=[Bdst])
                        fw.op("dve", lambda e, dst=dst, c=c, s0=s0, L=L: e.tensor_tensor(
                            out=pT[:, c, s0:s0 + L], in0=dst[:, 0:L], in1=hT[:, c, s0:s0 + L], op=ALU.subtract),
                            reads=[Bdst] + sb_, writes=pb_)
                slot, Bs = ring.get([
                    (lambda s: s.rearrange("p (g k c) -> p g k c", g=4, k=2),
                     poolw_d[j].rearrange("g (k p) c -> p g k c", p=128))])
                wgp = slot.rearrange("p (g k c) -> p g k c", g=4, k=2)
                for (t0, n_, v) in blks:
                    bi = blk_index(t0)
                    for g in range(4):
                        for mo in range(2):
                            pso, Bo = PS.next()
                            for k in range(2):
                                fw.op("pe", lambda e, g=g, mo=mo, k=k, pso=pso: e.matmul(
                                    pso[:, 0:n_], lhsT=wgp[:, g, k, mo * 128:(mo + 1) * 128], rhs=pT[:, 2 * g + k, t0:t0 + n_],
                                    start=(k == 0), stop=(k == 1)), reads=[Bs, B_p[2 * g + k][bi]], writes=[Bo])
                            c = 2 * g + mo
                            fw.op("dve", add_branch(pso[:, 0:n_], l, 0, c, t0, n_, v),
                                  reads=[Bo, B_modvs[l], B_x[c][bi]], writes=[B_x[c][bi]])

            def hgrn_mixer(l, blks, ph):
                B_lb = Buf()
                E_ = sb("lbE", [128, 2, DEPTH, KC], stack=ph)
                ssum = sb("lbS", [128, 2, KC], stack=ph)
                fw.op("act", lambda e: e.activation(out=E_[:], in_=hglb[:], func=AF.Exp), reads=[B_const], writes=[B_lb])
                fw.op("dve", lambda e: e.tensor_tensor(out=ssum[:], in0=E_[:, :, 0], in1=E_[:, :, 1], op=ALU.add), reads=[B_lb], writes=[B_lb])
                for k in (2, 3):
                    fw.op("dve", lambda e, k=k: e.tensor_tensor(out=ssum[:], in0=ssum[:], in1=E_[:, :, k], op=ALU.add), reads=[B_lb], writes=[B_lb])
                fw.op("dve", lambda e: e.reciprocal(out=ssum[:], in_=ssum[:]), reads=[B_lb], writes=[B_lb])
                fw.op("dve", lambda e: e.tensor_tensor(out=E_[:], in0=E_[:], in1=ssum[:].unsqueeze(2).broadcast_to((128, 2, DEPTH, KC)), op=ALU.mult),
                      reads=[B_lb], writes=[B_lb])
                fw.op("dve", lambda e: e.tensor_copy(out=lbv[:, 0], in_=E_[:, :, 0]), reads=[B_lb], writes=[B_lb])
                for k in range(1, l + 1):
                    fw.op("dve", lambda e, k=k: e.tensor_tensor(out=lbv[:, 0], in0=lbv[:, 0], in1=E_[:, :, k], op=ALU.add), reads=[B_lb], writes=[B_lb])
                fw.op("dve", lambda e: e.tensor_tensor(out=lbv[:, 0], in0=lbv[:, 0], in1=E_[:, :, 0], op=ALU.subtract), reads=[B_lb], writes=[B_lb])
                fw.op("dve", lambda e: e.tensor_scalar(out=lbv[:, 1], in0=lbv[:, 0], scalar1=-1.0, scalar2=1.0, op0=ALU.mult, op1=ALU.add),
                      reads=[B_lb], writes=[B_lb])
                fw.op("dve", lambda e: e.tensor_scalar_add(out=lbv[:, 2], in0=lbv[:, 0], scalar1=-1.0), reads=[B_lb], writes=[B_lb])

                def t(name, shape, dt=F32):
                    return sb(name, shape, dt, stack=ph)
                NT = T // 128
                qT = t("qT", [128, T], BF16)
                vtok = t("vtok", [128, NT, 128], BF16)
                oacc = t("oacc", [128, T])
                B_q = [Buf() for _ in range(5)]
                B_sg = [Buf() for _ in range(5)]
                B_vt = [Buf() for _ in range(5)]
                B_oa = [Buf() for _ in range(5)]
                vblk, B_vblk = t("vblk", [128, 512], BF16), Buf()
                Dd = []
                for di in range(2):
                    d = {}
                    for nm in ("sig", "lgf", "bb", "tmpf", "E1"):
                        d[nm] = (t("%s%d" % (nm, di), [128, 512]), Buf())
                    for nm in ("qt", "kt", "kh"):
                        d[nm] = (t("%s%d" % (nm, di), [128, 512], BF16), Buf())
                    d["khtok"] = (t("khtok%d" % di, [128, 4, 128], BF16), Buf())
                    d["ATm"] = (t("ATm%d" % di, [128, 4, 128], BF16), Buf())
                    d["ebe"] = (t("ebe%d" % di, [128, 2, 8]), Buf())
                    d["SS"] = (t("SS%d" % di, [128, 9, 128]), Buf())
                    d["Sbf"] = (t("Sbf%d" % di, [128, 8, 128], BF16), Buf())
                    d["carry"] = (t("carry%d" % di, [128, 128]), Buf())
                    d["pss"] = [(ps_t[:, 4 + 2 * di + i, :], PS.bufs[4 + 2 * di + i]) for i in range(2)]
                    Dd.append(d)
                rstd, B_rstd = Dd[0]["sig"]
                ontmp, B_ontmp = Dd[0]["lgf"]
                osq, B_osq = Dd[0]["qt"]
                onb, B_onb = Dd[0]["kt"]
                sgb, B_sgb = Dd[0]["kh"]
                PS.nrot = 4
                PS.i = 0

                def wview(s, h):
                    return s[:, h * 1024:(h + 1) * 1024].rearrange("p (k c) -> p k c", k=KC)

                def wsrc(col0):
                    return hgin_d[:, col0:col0 + 128].rearrange("(k p) c -> p k c", p=128)

                def mm8(ps, wv, t0, n_, Bs, bi, Bp):
                    for k in range(KC):
                        fw.op("pe", lambda e, k=k: e.matmul(ps[:, 0:n_], lhsT=wv[:, k, :], rhs=hT[:, k, t0:t0 + n_],
                                                            start=(k == 0), stop=(k == KC - 1)),
                              reads=[Bs, B_h[k][bi]], writes=[Bp])

                for m in range(KC):
                    c0 = m * 128
                    sl1, Bs1 = ring.get([(lambda s: wview(s, 0), wsrc(0 * D + c0)), (lambda s: wview(s, 1), wsrc(1 * D + c0))])
                    sl2, Bs2 = ring.get([(lambda s: wview(s, 0), wsrc(2 * D + c0)), (lambda s: wview(s, 1), wsrc(3 * D + c0))])
                    sl3, Bs3 = ring.get([(lambda s: wview(s, 0), wsrc(4 * D + c0)),
                                         (lambda s: s[:, 1024:2048], hgout_d[c0:c0 + 128, :])])
                    wq, wv_ = wview(sl1, 0), wview(sl1, 1)
                    wz = [wview(sl2, 0), wview(sl2, 1)]
                    wg = wview(sl3, 0)
                    wo = sl3[:, 1024:2048]
                    for (t0, n_, v) in blks:
                        bi = blk_index(t0)
                        nt_ = n_ // 128
                        g0 = t0 // 128
                        ps, Bp = PS.next()
                        mm8(ps, wq, t0, n_, Bs1, bi, Bp)
                        fw.op("act", lambda e: e.activation(out=qT[:, t0:t0 + n_], in_=ps[:, 0:n_], func=AF.Silu), reads=[Bp], writes=[B_q[bi]])
                        ps, Bp = PS.next()
                        mm8(ps, wv_, t0, n_, Bs1, bi, Bp)
                        fw.op("dve", lambda e: e.tensor_copy(out=vblk[:, 0:n_], in_=ps[:, 0:n_]), reads=[Bp], writes=[B_vblk])
                        ps2, Bp2 = PS.next()
                        psb = ps2.bitcast(BF16)
                        for j in range(nt_):
                            fw.op("pe", lambda e, j=j: e.transpose(psb[:, j * 128:(j + 1) * 128], vblk[:, j * 128:(j + 1) * 128], identb[:]),
                                  reads=[B_vblk, B_const], writes=[Bp2])
                        fw.op("act", lambda e: e.activation(out=vtok[:, g0:g0 + nt_, :],
                                                            in_=psb[:, 0:nt_ * 128].rearrange("p (c e) -> p c e", e=128), func=AF.Copy),
                              reads=[Bp2], writes=[B_vt[bi]])
                    for di in range(2):
                        fw.op("dve", lambda e, di=di: e.memset(Dd[di]["carry"][0][:], 0.0), writes=[Dd[di]["carry"][1]])
                    touched = [False] * 5

                    def front(item):
                        di, (t0, n_, v) = item
                        d = Dd[di]
                        bi = blk_index(t0)
                        nch = n_ // 64
                        nt_ = n_ // 128
                        g0 = t0 // 128
                        mid = 31 if di == 0 else 32
                        endc = 63 if di == 0 else 0
                        sig, B_sig = d["sig"]
                        lgf, B_lgf = d["lgf"]
                        bb, B_bb = d["bb"]
                        tmpf, B_tmpf = d["tmpf"]
                        E1, B_E1 = d["E1"]
                        qt_, B_qt = d["qt"]
                        kt_, B_kt = d["kt"]
                        kh_, B_kh = d["kh"]
                        khtok, B_khtok = d["khtok"]
                        ATm, B_AT = d["ATm"]
                        ebe, B_ebe = d["ebe"]
                        ps, Bp = PS.next()
                        mm8(ps, wz[di], t0, n_, Bs2, bi, Bp)
                        fw.op("act", lambda e: e.activation(out=sig[:, 0:n_], in_=ps[:, 0:n_], func=AF.Sigmoid, scale=-1.0), reads=[Bp], writes=[B_sig])
                        fw.op("act", lambda e: e.activation(out=lgf[:, 0:n_], in_=sig[:, 0:n_], func=AF.Ln,
                                                            scale=lbv[:, 2, di, m:m + 1], bias=1.0), reads=[B_sig, B_lb], writes=[B_lgf])
                        fw.op("dve", lambda e: e.tensor_tensor_scan(out=bb[:, 0:n_], data0=reset01[:, 0:n_], data1=lgf[:, 0:n_],
                                                                    initial=0.0, op0=ALU.mult, op1=ALU.add),
                              reads=[B_lgf, B_const], writes=[B_bb])
                        if di == 0:
                            bbv, B_bbv = bb, B_bb
                            E2, B_E2 = lgf, B_lgf
                        else:
                            fw.op("dve", lambda e: e.tensor_tensor(out=tmpf[:, 0:n_], in0=lgf[:, 0:n_], in1=bb[:, 0:n_], op=ALU.subtract),
                                  reads=[B_lgf, B_bb], writes=[B_tmpf])
                            bb3 = bb[:, 0:n_].rearrange("p (c s) -> p c s", s=64)
                            fw.op("dve", lambda e: e.tensor_tensor(
                                out=lgf[:, 0:n_].rearrange("p (c s) -> p c s", s=64),
                                in0=tmpf[:, 0:n_].rearrange("p (c s) -> p c s", s=64),
                                in1=bb3[:, :, 63:64].broadcast_to((128, nch, 64)), op=ALU.add),
                                reads=[B_tmpf, B_bb], writes=[B_lgf])
                            bbv, B_bbv = lgf, B_lgf
                            E2, B_E2 = bb, B_bb
                        bbv3 = bbv[:, 0:n_].rearrange("p (c s) -> p c s", s=64)
                        tm3 = tmpf[:, 0:n_].rearrange("p (c s) -> p c s", s=64)
                        fw.op("act", lambda e: e.activation(out=ebe[:, 0, 0:nch], in_=bbv3[:, :, endc], func=AF.Exp), reads=[B_bbv], writes=[B_ebe])
                        fw.op("act", lambda e: e.activation(out=ebe[:, 1, 0:nch], in_=bbv3[:, :, mid], func=AF.Exp), reads=[B_bbv], writes=[B_ebe])
                        fw.op("dve", lambda e: e.tensor_tensor(out=tm3, in0=bbv3, in1=bbv3[:, :, mid:mid + 1].broadcast_to((128, nch, 64)),
                                                               op=ALU.subtract), reads=[B_bbv], writes=[B_tmpf])
                        fw.op("act", lambda e: e.activation(out=E1[:, 0:n_], in_=tmpf[:, 0:n_], func=AF.Exp), reads=[B_tmpf], writes=[B_E1])
                        fw.op("act", lambda e: e.activation(out=E2[:, 0:n_], in_=tmpf[:, 0:n_], func=AF.Exp, scale=-1.0), reads=[B_tmpf], writes=[B_E2])
                        fw.op("dve", lambda e: e.tensor_tensor(out=qt_[:, 0:n_], in0=qT[:, t0:t0 + n_], in1=E1[:, 0:n_], op=ALU.mult),
                              reads=[B_q[bi], B_E1], writes=[B_qt])
                        fw.op("dve", lambda e: e.scalar_tensor_tensor(out=kt_[:, 0:n_], in0=sig[:, 0:n_], scalar=lbv[:, 1, di, m:m + 1],
                                                                      in1=E2[:, 0:n_], op0=ALU.mult, op1=ALU.mult),
                              reads=[B_sig, B_E2, B_lb], writes=[B_kt])
                        fw.op("dve", lambda e: e.tensor_tensor(out=tm3, in0=bbv3, in1=bbv3[:, :, endc:endc + 1].broadcast_to((128, nch, 64)),
                                                               op=ALU.subtract), reads=[B_bbv], writes=[B_tmpf])
                        fw.op("act", lambda e: e.activation(out=E1[:, 0:n_], in_=tmpf[:, 0:n_], func=AF.Exp, scale=-1.0), reads=[B_tmpf], writes=[B_E1])
                        fw.op("dve", lambda e: e.scalar_tensor_tensor(out=kh_[:, 0:n_], in0=sig[:, 0:n_], scalar=lbv[:, 1, di, m:m + 1],
                                                                      in1=E1[:, 0:n_], op0=ALU.mult, op1=ALU.mult),
                              reads=[B_sig, B_E1, B_lb], writes=[B_kh])
                        ps2, Bp2 = PS.next()
                        psb = ps2.bitcast(BF16)
                        for j in range(nt_):
                            fw.op("pe", lambda e, j=j: e.transpose(psb[:, j * 128:(j + 1) * 128], kh_[:, j * 128:(j + 1) * 128], identb[:]),
                                  reads=[B_kh, B_const], writes=[Bp2])
                        fw.op("act", lambda e: e.activation(out=khtok[:, 0:nt_, :],
                                                            in_=psb[:, 0:nt_ * 128].rearrange("p (c e) -> p c e", e=128), func=AF.Copy),
                              reads=[Bp2], writes=[B_khtok])
                        psa, Ba = PS.next()
                        for j in range(nt_):
                            fw.op("pe", lambda e, j=j: e.matmul(psa[:, j * 128:(j + 1) * 128], lhsT=kt_[:, j * 128:(j + 1) * 128],
                                                                rhs=qt_[:, j * 128:(j + 1) * 128], start=True, stop=True),
                                  reads=[B_kt, B_qt], writes=[Ba])
                        fw.op("dve", lambda e: e.tensor_tensor(
                            out=ATm[:, 0:nt_, :], in0=psa[:, 0:n_].rearrange("p (j t) -> p j t", t=128),
                            in1=mask128[:, di, :].unsqueeze(1).broadcast_to((128, nt_, 128)), op=ALU.mult),
                            reads=[Ba, B_const], writes=[B_AT])
                        for ch in range(nch):
                            j, hf = ch // 2, ch % 2
                            pss, Bss = d["pss"][hf]
                            fw.op("pe", lambda e, ch=ch, j=j, hf=hf, pss=pss: e.matmul(
                                pss[:, j * 128:(j + 1) * 128], lhsT=khtok[hf * 64:(hf + 1) * 64, j, :],
                                rhs=vtok[hf * 64:(hf + 1) * 64, g0 + j, :], start=True, stop=True),
                                reads=[B_khtok, B_vt[bi]], writes=[Bss])

                    def back(item):
                        di, (t0, n_, v) = item
                        d = Dd[di]
                        bi = blk_index(t0)
                        nch = n_ // 64
                        g0 = t0 // 128
                        qt_, B_qt = d["qt"]
                        ATm, B_AT = d["ATm"]
                        ebe, B_ebe = d["ebe"]
                        SS, B_SS = d["SS"]
                        Sbf, B_Sbf = d["Sbf"]
                        carry, B_carry = d["carry"]
                        chs = list(range(nch)) if di == 0 else list(range(nch - 1, -1, -1))
                        for idx, ch in enumerate(chs):
                            jin, jout = (ch, ch + 1) if di == 0 else (ch + 1, ch)
                            pss, Bss = d["pss"][ch % 2]
                            if idx == 0:
                                fw.op("dve", lambda e, jin=jin: e.tensor_copy(out=SS[:, jin, :], in_=carry[:]), reads=[B_carry], writes=[B_SS])
                            fw.op("dve", lambda e, ch=ch, jin=jin, jout=jout, pss=pss: e.scalar_tensor_tensor(
                                out=SS[:, jout, :], in0=SS[:, jin, :], scalar=ebe[:, 0, ch:ch + 1],
                                in1=pss[:, (ch // 2) * 128:(ch // 2 + 1) * 128], op0=ALU.mult, op1=ALU.add),
                                reads=[B_SS, B_ebe, Bss], writes=[B_SS])
                        jlast = nch if di == 0 else 0
                        fw.op("dve", lambda e: e.tensor_copy(out=carry[:], in_=SS[:, jlast, :]), reads=[B_SS], writes=[B_carry])
                        off = 0 if di == 0 else 1
                        fw.op("dve", lambda e: e.tensor_tensor(
                            out=Sbf[:, 0:nch, :], in0=SS[:, off:off + nch, :],
                            in1=ebe[:, 1, 0:nch].unsqueeze(2).broadcast_to((128, nch, 128)), op=ALU.mult),
                            reads=[B_SS, B_ebe], writes=[B_Sbf])
                        pso, Bo = PS.next()
                        for ch in range(nch):
                            j, hf = ch // 2, ch % 2
                            c_lo, c_hi = ch * 64, (ch + 1) * 64
                            fw.op("pe", lambda e: e.matmul(pso[:, c_lo:c_hi], lhsT=vtok[:, g0 + j, :], rhs=ATm[:, j, hf * 64:(hf + 1) * 64],
                                                           start=True, stop=False), reads=[B_vt[bi], B_AT], writes=[Bo])
                            fw.op("pe", lambda e: e.matmul(pso[:, c_lo:c_hi], lhsT=Sbf[:, ch, :], rhs=qt_[:, c_lo:c_hi],
                                                           start=False, stop=True), reads=[B_Sbf, B_qt], writes=[Bo])
                        if not touched[bi]:
                            touched[bi] = True
                            fw.op("act", lambda e: e.activation(out=oacc[:, t0:t0 + n_], in_=pso[:, 0:n_], func=AF.Copy), reads=[Bo], writes=[B_oa[bi]])
                        else:
                            fw.op("dve", lambda e: e.tensor_tensor(out=oacc[:, t0:t0 + n_], in0=oacc[:, t0:t0 + n_], in1=pso[:, 0:n_], op=ALU.add),
                                  reads=[Bo, B_oa[bi]], writes=[B_oa[bi]])

                    bw_order = [blks[0]] + blks[:0:-1]
                    seq = []
                    for i_ in range(len(blks)):
                        seq.append((0, blks[i_]))
                        seq.append((1, bw_order[i_]))
                    front(seq[0])
                    for k_ in range(len(seq)):
                        if k_ + 1 < len(seq):
                            front(seq[k_ + 1])
                        back(seq[k_])
                    for (t0, n_, v) in blks:
                        bi = blk_index(t0)
                        fw.op("act", lambda e: e.activation(out=osq[:, 0:n_], in_=oacc[:, t0:t0 + n_], func=AF.Square), reads=[B_oa[bi]], writes=[B_osq])
                        ps, Bp = PS.next()
                        fw.op("pe", lambda e: e.matmul(ps[:, 0:n_], lhsT=onesb[:], rhs=osq[:, 0:n_], start=True, stop=True), reads=[B_osq, B_const], writes=[Bp])
                        fw.op("act", lambda e: e.activation(out=rstd[:, 0:n_], in_=ps[:, 0:n_], func=AF.Ln, scale=1.0 / 128, bias=RMS_EPS), reads=[Bp], writes=[B_rstd])
                        fw.op("act", lambda e: e.activation(out=rstd[:, 0:n_], in_=rstd[:, 0:n_], func=AF.Exp, scale=-0.5), reads=[B_rstd], writes=[B_rstd])
                        fw.op("dve", lambda e: e.scalar_tensor_tensor(out=ontmp[:, 0:n_], in0=oacc[:, t0:t0 + n_], scalar=hgnw[:, m:m + 1], in1=rstd[:, 0:n_],
                                                                      op0=ALU.mult, op1=ALU.mult), reads=[B_oa[bi], B_rstd, B_const], writes=[B_ontmp])
                        ps, Bp = PS.next()
                        mm8(ps, wg, t0, n_, Bs3, bi, Bp)
                        fw.op("act", lambda e: e.activation(out=sgb[:, 0:n_], in_=ps[:, 0:n_], func=AF.Silu), reads=[Bp], writes=[B_sgb])
                        fw.op("dve", lambda e: e.tensor_tensor(out=onb[:, 0:n_], in0=ontmp[:, 0:n_], in1=sgb[:, 0:n_], op=ALU.mult),
                              reads=[B_ontmp, B_sgb], writes=[B_onb])
                        for mo in range(KC):
                            ps, Bp = PS.next()
                            fw.op("pe", lambda e: e.matmul(ps[:, 0:n_], lhsT=wo[:, mo * 128:(mo + 1) * 128], rhs=onb[:, 0:n_], start=True, stop=True),
                                  reads=[Bs3, B_onb], writes=[Bp])
                            fw.op("dve", add_branch(ps[:, 0:n_], l, 0, mo, t0, n_, v), reads=[Bp, B_modvs[l], B_x[mo][bi]], writes=[B_x[mo][bi]])
                PS.nrot = 7
                PS.i = 0

            def na_mixer(l, blks, ph):
                def t(name, shape, dt=F32):
                    return sb(name, shape, dt, stack=ph)
                NT = T // 128
                qT = t("naq", [128, T], BF16)
                kT = t("nak", [128, T], BF16)
                vblk, B_vblk = t("navb", [128, 512], BF16), Buf()
                vtok = t("navt", [128, NT, 2, 65], BF16)
                otok = t("naot", [128, NT, 128], BF16)
                oT = t("naoT", [128, T], BF16)
                mask, B_mask = t("namask", [128, NPAT, 128], BF16), Buf()
                EB = [(t("naEB%d" % i, [128, NPAT, 128]), Buf(), ch_ebs[i]) for i in range(2)]
                PT = [(t("naPT%d" % i, [128, 7 * 128], BF16), Buf()) for i in range(4)]
                rden = [(t("narden%d" % i, [128, 1]), Buf()) for i in range(4)]
                B_q = [Buf() for _ in range(NT)]
                B_k = [Buf() for _ in range(NT)]
                B_vt = [Buf() for _ in range(NT)]
                B_ot = [Buf() for _ in range(NT)]
                B_oT = [Buf() for _ in range(5)]
                fw.dma("sp", ch_in, mask[:], namask_d, writes=[B_mask])
                fw.op("dve", lambda e: e.memset(vtok[:, :, :, 64:65], 1.0), writes=B_vt)
                units = []
                for m in range(16):
                    if m < 2:
                        units.append((2 + m, [2, 3, 4, 5], 5 + 4 * m))
                    elif m >= 14:
                        units.append((2 + m, [14, 15, 16, 17], 5 + 4 * (m - 12)))
                    else:
                        units.append((2 + m, [m + i for i in range(5)], 0))
                units.append((0, [], 0))
                units.append((1, [], 0))

                def wview(s, h):
                    return s[:, h * 1024:(h + 1) * 1024].rearrange("p (k c) -> p k c", k=KC)

                def wsrc(col0):
                    return naqkv_d[:, col0:col0 + 128].rearrange("(k p) c -> p k c", p=128)

                def mm8(ps, wv, t0, n_, Bs, bi, Bp):
                    for k in range(KC):
                        fw.op("pe", lambda e, k=k: e.matmul(ps[:, 0:n_], lhsT=wv[:, k, :], rhs=hT[:, k, t0:t0 + n_],
                                                            start=(k == 0), stop=(k == KC - 1)),
                              reads=[Bs, B_h[k][bi]], writes=[Bp])
                nu = 0
                for mp in range(KC):
                    c0 = mp * 128
                    sl1, Bs1 = ring.get([(lambda s: wview(s, 0), wsrc(c0)), (lambda s: wview(s, 1), wsrc(D + c0))])
                    sl2, Bs2 = ring.get([(lambda s: wview(s, 0), wsrc(2 * D + c0)),
                                         (lambda s: s[:, 1024:2048], naout_d[c0:c0 + 128, :])])
                    wq, wk, wv_ = wview(sl1, 0), wview(sl1, 1), wview(sl2, 0)
                    wo = sl2[:, 1024:2048]
                    for (t0, n_, v) in blks:
                        bi = blk_index(t0)
                        nt_ = n_ // 128
                        g0 = t0 // 128
                        tb_ = list(range(g0, g0 + nt_))
                        ps, Bp = PS.next()
                        mm8(ps, wq, t0, n_, Bs1, bi, Bp)
                        fw.op("act", lambda e: e.activation(out=qT[:, t0:t0 + n_], in_=ps[:, 0:n_], func=AF.Copy), reads=[Bp], writes=[B_q[i] for i in tb_])
                        ps, Bp = PS.next()
                        mm8(ps, wk, t0, n_, Bs1, bi, Bp)
                        fw.op("dve", lambda e: e.tensor_copy(out=kT[:, t0:t0 + n_], in_=ps[:, 0:n_]), reads=[Bp], writes=[B_k[i] for i in tb_])
                        ps, Bp = PS.next()
                        mm8(ps, wv_, t0, n_, Bs2, bi, Bp)
                        fw.op("act", lambda e: e.activation(out=vblk[:, 0:n_], in_=ps[:, 0:n_], func=AF.Copy), reads=[Bp], writes=[B_vblk])
                        ps2, Bp2 = PS.next()
                        psb = ps2.bitcast(BF16)
                        for j in range(nt_):
                            fw.op("pe", lambda e, j=j: e.transpose(psb[:, j * 128:(j + 1) * 128], vblk[:, j * 128:(j + 1) * 128], identb[:]),
                                  reads=[B_vblk, B_const], writes=[Bp2])
                        for hh in range(2):
                            fw.op("dve", lambda e, hh=hh: e.tensor_copy(
                                out=vtok[:, g0:g0 + nt_, hh, 0:64],
                                in_=psb[:, 0:nt_ * 128].rearrange("p (j h e) -> p j h e", h=2, e=64)[:, :, hh, :]),
                                reads=[Bp2], writes=[B_vt[i] for i in tb_])
                    for hh in range(2):
                        h = 2 * mp + hh
                        pb = hh * 64
                        eb, B_eb, ch_eb = EB[h % 2]
                        fw.dma("sp", ch_eb, eb[:], nabias_d[h], writes=[B_eb])
                        fw.op("act", lambda e: e.activation(out=eb[:], in_=eb[:], func=AF.Exp), reads=[B_eb], writes=[B_eb])
                        fw.op("dve", lambda e: e.tensor_tensor(out=eb[:], in0=eb[:], in1=mask[:], op=ALU.mult), reads=[B_eb, B_mask], writes=[B_eb])
                        def front(u):
                            qt, ktiles, p0 = units[u]
                            nk = len(ktiles)
                            tiles = ktiles + [0, 1]
                            ntl = len(tiles)
                            pt, B_pt = PT[u % len(PT)]
                            banks = []
                            for j, kt_i in enumerate(tiles):
                                if j % 4 == 0:
                                    banks.append(PS.next())
                                psx, Bx = banks[-1]
                                jj = j % 4
                                fw.op("pe", lambda e, psx=psx, jj=jj, kt_i=kt_i: e.matmul(
                                    psx[:, jj * 128:(jj + 1) * 128], lhsT=kT[pb:pb + 64, kt_i * 128:(kt_i + 1) * 128],
                                    rhs=qT[pb:pb + 64, qt * 128:(qt + 1) * 128], start=True, stop=True),
                                    reads=[B_k[kt_i], B_q[qt]], writes=[Bx])
                            for bidx, (psx, Bx) in enumerate(banks):
                                w_ = min(4, ntl - 4 * bidx) * 128
                                fw.op("act", lambda e, psx=psx, w_=w_, bidx=bidx: e.activation(
                                    out=pt[:, bidx * 512:bidx * 512 + w_], in_=psx[:, 0:w_], func=AF.Exp, scale=0.125),
                                    reads=[Bx], writes=[B_pt])
                            if nk:
                                fw.op("dve", lambda e: e.tensor_tensor(
                                    out=pt[:, 0:nk * 128], in0=pt[:, 0:nk * 128],
                                    in1=eb[:, p0:p0 + nk, :].rearrange("p a b -> p (a b)"), op=ALU.mult),
                                    reads=[B_pt, B_eb], writes=[B_pt])

                        def back(u):
                            qt, ktiles, p0 = units[u]
                            tiles = ktiles + [0, 1]
                            ntl = len(tiles)
                            pt, B_pt = PT[u % len(PT)]
                            rd, B_rd = rden[u % len(rden)]
                            pso, Bo = PS.next()
                            for j, kt_i in enumerate(tiles):
                                fw.op("pe", lambda e, j=j, kt_i=kt_i: e.matmul(
                                    pso[:, 0:65], lhsT=pt[:, j * 128:(j + 1) * 128], rhs=vtok[:, kt_i, hh, :],
                                    start=(j == 0), stop=(j == ntl - 1)), reads=[B_pt, B_vt[kt_i]], writes=[Bo])
                            fw.op("dve", lambda e: e.reciprocal(out=rd[:], in_=pso[:, 64:65]), reads=[Bo], writes=[B_rd])
                            fw.op("act", lambda e: e.activation(out=otok[:, qt, pb:pb + 64], in_=pso[:, 0:64], func=AF.Identity, scale=rd[:, 0:1]),
                                  reads=[Bo, B_rd], writes=[B_ot[qt]])

                        NSK = 2
                        for u in range(min(NSK, len(units))):
                            front(u)
                        for u in range(len(units)):
                            if u + NSK < len(units):
                                front(u + NSK)
                            back(u)
                    for (t0, n_, v) in blks:
                        bi = blk_index(t0)
                        nt_ = n_ // 128
                        g0 = t0 // 128
                        ps2, Bp2 = PS.next()
                        psb = ps2.bitcast(BF16)
                        for j in range(nt_):
                            fw.op("pe", lambda e, j=j: e.transpose(psb[:, j * 128:(j + 1) * 128], otok[:, g0 + j, :], identb[:]),
                                  reads=[B_ot[g0 + j], B_const], writes=[Bp2])
                        fw.op("act", lambda e: e.activation(out=oT[:, t0:t0 + n_], in_=psb[:, 0:n_], func=AF.Copy), reads=[Bp2], writes=[B_oT[bi]])
                        for mo in range(KC):
                            ps, Bp = PS.next()
                            fw.op("pe", lambda e: e.matmul(ps[:, 0:n_], lhsT=wo[:, mo * 128:(mo + 1) * 128], rhs=oT[:, t0:t0 + n_], start=True, stop=True),
                                  reads=[Bs2, B_oT[bi]], writes=[Bp])
                            fw.op("dve", add_branch(ps[:, 0:n_], l, 0, mo, t0, n_, v), reads=[Bp, B_modvs[l], B_x[mo][bi]], writes=[B_x[mo][bi]])

            for l in range(nlayers):
                last = l == DEPTH - 1
                kind = l % 3
                blks = token_blocks(not last)
                if l == 0:
                    modulate(0, 0, blks)
                with contextlib.ExitStack() as ph:
                    if kind == 0:
                        pool_mixer(l, blks, ph)
                    elif kind == 1:
                        hgrn_mixer(l, blks, ph)
                    else:
                        na_mixer(l, blks, ph)
                    fw.barrier()
                with contextlib.ExitStack() as ph:
                    ybf = (sb("ybf", [128, KC, 512], BF16, stack=ph), Buf())
                    ysq = (sb("ysq", [128, KC, 512], BF16, stack=ph), Buf())
                    stt = (sb("lnst", [128, 4, 512], stack=ph), Buf())
                    act = (sb("act", [128, 6, T], BF16, stack=ph), [[Buf() for _ in range(5)] for _ in range(6)])
                    sg = [(sb("sg%d" % i, [128, 512], BF16, stack=ph), Buf()) for i in range(2)]
                    layer_norm(l, 0, blks, (ybf, ysq, stt), (l, 1))
                    ffn(l, blks, act, sg)
                    nxt = None if last else (l + 1, 0)
                    layer_norm(l, 1, blks, (ybf, ysq, stt), nxt)
                    fw.barrier()
            if dbg:
                fw.dma("sp", ch_dbg, dbg_d, xT[:], reads=all_x)
            with contextlib.ExitStack() as ph:
                xo = sb("xo", [128, 2, D], stack=ph)
                B_xo = [Buf(), Buf()]
                for i in range(SEQ // 128):
                    s = i % 2
                    tt = NCTX + i * 128
                    for half in range(2):
                        psb, Bp = PS.next()
                        for q in range(4):
                            c = half * 4 + q
                            fw.op("pe", lambda e, c=c, q=q, psb=psb, tt=tt: e.transpose(
                                psb[:, q * 128:(q + 1) * 128], xT[:, c, tt:tt + 128], ident[:]),
                                reads=[B_x[c][blk_index(tt)], B_const], writes=[Bp])
                        dst = xo[:, s, half * 512:(half + 1) * 512]
                        if half == 0:
                            fw.op("act", lambda e, dst=dst, psb=psb: e.activation(out=dst, in_=psb, func=AF.Copy), reads=[Bp], writes=[B_xo[s]])
                        else:
                            fw.op("dve", lambda e, dst=dst, psb=psb: e.tensor_copy(out=dst, in_=psb), reads=[Bp], writes=[B_xo[s]])
                    fw.dma("sp", ch_outs[s], out_d[i * 128:(i + 1) * 128, :], xo[:, s], reads=[B_xo[s]])
                if not fw.dry:
                    fw._wait_for(fw.E["sp"], [(c_[0], c_[1], {}, "dma") for c_ in ch_outs + [ch_dbg] if c_[1]])

        fw.dry = True
        gen()
        fw.dry = False
        PS.i = 0
        PS.nrot = 7
        gen()
    return nc


def na_patterns():
    pats = [(8, 4 + 2 * i) for i in range(5)]
    for qr0, krs in ((0, (0, 2, 4, 6)), (2, (0, 2, 4, 6)), (28, (24, 26, 28, 30)), (30, (24, 26, 28, 30))):
        for kr0 in krs:
            pats.append((qr0, kr0))
    kk = np.arange(128)
    kro, kc = (kk // 64)[:, None], (kk % 64)[:, None]
    qro, qc = (kk // 64)[None, :], (kk % 64)[None, :]
    drow = np.zeros((NPAT, 128, 128), np.int64)
    dcol = np.zeros((NPAT, 128, 128), np.int64)
    valid = np.zeros((NPAT, 128, 128), bool)
    for p, (qr0, kr0) in enumerate(pats):
        kr = kr0 + kro
        qr = qr0 + qro
        rs = np.clip(qr - 4, 0, 24)
        ws = np.clip(qc - 8, 0, 48)
        valid[p] = (kr >= rs) & (kr < rs + 8) & (kc >= ws) & (kc < ws + 16)
        drow[p] = np.clip(kr - qr + 7, 0, 14)
        dcol[p] = np.clip(kc - qc + 15, 0, 30)
    return drow, dcol, valid


def host_inputs(inputs, nlayers=DEPTH):
    f32 = np.float32
    drow, dcol, valid = na_patterns()
    rpb = np.asarray(inputs["na_rpb"], f32)[0]
    na_bias = rpb[:, drow, dcol].transpose(0, 2, 1, 3)
    na_mask = valid.transpose(1, 0, 2).astype(ml_dtypes.bfloat16)
    x = np.asarray(inputs["x"], f32)
    c = np.asarray(inputs["c"], f32)
    ctx = np.asarray(inputs["ctx"], f32)
    c_ctx = np.asarray(inputs["c_ctx"], f32)
    mod_b = np.asarray(inputs["mod_b"], f32)
    modb2 = np.repeat(mod_b.reshape(DEPTH, 48, 128).transpose(2, 0, 1)[..., None], 2, axis=-1)
    lnp = np.stack([np.asarray(inputs["ln_g"], f32), np.asarray(inputs["ln_b"], f32)], axis=2)
    lnp = lnp.reshape(DEPTH, 2, 2, KC, 128).transpose(4, 0, 1, 2, 3)
    poolsc = np.asarray(inputs["pool_scale"], f32).reshape(2, KC, 128).transpose(2, 0, 1)
    edge = np.ones((4, 16), f32)
    for wi, w in enumerate(POOL_WINDOWS):
        lo = w // 2
        hi = w - 1 - lo
        for t in range(8):
            cnt = min(t + hi + 1, 10 ** 6) - max(t - lo, 0)
            edge[wi, t] = f32(w) / f32(cnt)
            tr = 7 - t
            cnt = min(tr + lo + 1, w) if tr < hi else w
            edge[wi, 8 + t] = f32(w) / f32(cnt)
    edge = np.broadcast_to(edge[None], (128, 4, 16))
    common = {
        "mod_w": np.ascontiguousarray(inputs["mod_w"], f32),
        "mod_b2": np.ascontiguousarray(modb2),
        "lnp": np.ascontiguousarray(lnp),
        "ffn_w_in": np.ascontiguousarray(inputs["ffn_w_in"], f32),
        "ffn_w_out": np.ascontiguousarray(inputs["ffn_w_out"], f32),
        "pool_w": np.ascontiguousarray(inputs["pool_w"], f32),
        "pool_sc": np.ascontiguousarray(poolsc),
        "hgrn_w_in": np.ascontiguousarray(np.asarray(inputs["hgrn_w_in"], f32)[0]),
        "hgrn_w_out": np.ascontiguousarray(np.asarray(inputs["hgrn_w_out"], f32)[0]),
        "hgrn_nw": np.ascontiguousarray(np.asarray(inputs["hgrn_norm_w"], f32)[0].reshape(KC, 128).T),
        "hgrn_lbl": np.ascontiguousarray(np.asarray(inputs["hgrn_lb_logits"], f32).reshape(2, DEPTH, KC, 128).transpose(3, 0, 1, 2)),
        "mask128": np.ascontiguousarray(np.stack([np.kron(np.eye(2, dtype=f32), np.triu(np.ones((64, 64), f32))),
                                                  np.kron(np.eye(2, dtype=f32), np.tril(np.ones((64, 64), f32)))], axis=1)),
        "reset01": np.ascontiguousarray(np.broadcast_to((np.arange(512) % 64 != 0).astype(f32)[None], (128, 512))),
        "na_w_qkv": np.ascontiguousarray(np.asarray(inputs["na_w_qkv"], f32)[0]),
        "na_w_out": np.ascontiguousarray(np.asarray(inputs["na_w_out"], f32)[0]),
        "na_bias": np.ascontiguousarray(na_bias),
        "na_mask": np.ascontiguousarray(na_mask),
        "ident": np.eye(128, dtype=f32),
        "edgefac": np.ascontiguousarray(edge),
    }
    maps = []
    for b in range(x.shape[0]):
        cv = np.stack([c[b], c_ctx], axis=-1).reshape(KC, 128, 2).transpose(1, 0, 2)
        m = dict(common)
        m["x"] = np.ascontiguousarray(x[b])
        m["ctx"] = np.ascontiguousarray(ctx[b])
        m["cvec"] = np.ascontiguousarray(cv)
        maps.append(m)
    return maps


_NC_CACHE = {}


def kernel(**inputs):
    maps = host_inputs(inputs)
    if "nc" not in _NC_CACHE:
        _NC_CACHE["nc"] = build()
    nc = _NC_CACHE["nc"]
    res = run_bass_kernel_spmd(nc, maps, core_ids=list(range(len(maps))))
    return np.stack([r["out"] for r in res.results], axis=0).astype(np.float32)
```

```python
import contextlib
import numpy as np
import ml_dtypes
import concourse.bass as bass
import concourse.mybir as mybir
from concourse.bass_utils import run_bass_kernel_spmd

AF = mybir.ActivationFunctionType
ALU = mybir.AluOpType
F32 = mybir.dt.float32
BF16 = mybir.dt.bfloat16

D = 1024
KC = 8
NCTX = 256
SEQ = 2048
T = NCTX + SEQ
DEPTH = 4
FH = 2816
JC = FH // 128
ALPHA = (2 * DEPTH) ** 0.25
LN_EPS = 1e-6
RMS_EPS = 1e-6
EPS_LN = LN_EPS / (ALPHA * ALPHA)
POOL_WINDOWS = (2, 4, 8, 16)
NSLOT = 6
SLOTW = 2048
NPAT = 21


class Buf:
    __slots__ = ("w", "r", "name")

    def __init__(self, name=""):
        self.w = None
        self.r = {}
        self.name = name


class _Eng:
    def __init__(self, name, eng, sem):
        self.name = name
        self.eng = eng
        self.sem = sem
        self.count = 0
        self.known = {}


class FW:
    def __init__(self, nc, stack):
        self.nc = nc
        self.stack = stack
        self.dry = False
        self.E = {}
        for name, eng in (("pe", nc.tensor), ("act", nc.scalar), ("dve", nc.vector),
                          ("pool", nc.gpsimd), ("sp", nc.sync)):
            sem = stack.enter_context(nc.semaphore("sem_" + name))
            self.E[name] = _Eng(name, eng, sem)
        self.nwaits = 0
        self.nins = 0
        self.chans = []

    def new_chan(self, name):
        sem = self.stack.enter_context(self.nc.semaphore("ch_" + name))
        ch = [sem, 0]
        self.chans.append(ch)
        return ch

    def _wait_for(self, E, evs):
        need = {}
        for ev in evs:
            if ev is None:
                continue
            sem, val, clock, src = ev
            if src == "pe" and E.name == "pe":
                continue
            k = id(sem)
            if E.known.get(k, 0) >= val:
                continue
            if k not in need or need[k][1] < val:
                need[k] = (sem, val, clock)
        for k, (sem, val, clock) in need.items():
            if E.known.get(k, 0) >= val:
                continue
            E.eng.wait_ge(sem, val)
            self.nwaits += 1
            E.known[k] = val
            for kk, vv in clock.items():
                if E.known.get(kk, 0) < vv:
                    E.known[kk] = vv

    @staticmethod
    def _deps(reads, writes):
        evs = []
        for b in reads:
            evs.append(b.w)
        for b in writes:
            evs.append(b.w)
            evs.extend(b.r.values())
        return evs

    @staticmethod
    def _record(ev, reads, writes):
        k = id(ev[0])
        for b in reads:
            b.r[k] = ev
        for b in writes:
            b.w = ev
            b.r = {}

    def op(self, en, fn, reads=(), writes=()):
        if self.dry:
            return
        E = self.E[en]
        self._wait_for(E, self._deps(reads, writes))
        ins = fn(E.eng)
        E.count += 1
        ins.then_inc(E.sem, 1)
        self.nins += 1
        ev = (E.sem, E.count, dict(E.known), en)
        self._record(ev, reads, writes)

    def dma(self, en, chan, out, in_, reads=(), writes=()):
        if self.dry:
            return
        E = self.E[en]
        self._wait_for(E, self._deps(reads, writes))
        ins = E.eng.dma_start(out=out, in_=in_)
        chan[1] += 16
        ins.then_inc(chan[0], 16)
        self.nins += 1
        ev = (chan[0], chan[1], dict(E.known), "dma")
        self._record(ev, reads, writes)

    def barrier(self, engines=("pe", "act", "dve", "sp")):
        if self.dry:
            return
        evs = []
        for n in ("pe", "act", "dve", "sp"):
            E = self.E[n]
            if E.count:
                evs.append((E.sem, E.count, {}, "x"))
        for ch in self.chans:
            if ch[1] and not ch[2:]:
                evs.append((ch[0], ch[1], {}, "dma"))
        for n in engines:
            self._wait_for(self.E[n], evs)


class Ring:
    def __init__(self, fw, tile):
        self.fw = fw
        self.tile = tile
        self.bufs = [Buf("slot%d" % i) for i in range(NSLOT)]
        self.chs = [fw.new_chan("slot%d" % i) for i in range(NSLOT)]
        for ch in self.chs:
            ch.append("ring")
        self.reqs = []
        self.pos = 0
        self.issued = 0

    def reset(self):
        self.pos = 0
        self.issued = 0

    def get(self, loads):
        i = self.pos
        self.pos += 1
        if self.fw.dry:
            self.reqs.append(loads)
            return self.tile[:, 0, :], self.bufs[0]
        hi = min(len(self.reqs), i + NSLOT - 2)
        while self.issued < hi:
            r = self.issued
            s = r % NSLOT
            for dst_fn, src in self.reqs[r]:
                self.fw.dma("pool", self.chs[s], dst_fn(self.tile[:, s, :]), src, writes=[self.bufs[s]])
            self.issued += 1
        s = i % NSLOT
        return self.tile[:, s, :], self.bufs[s]


class PsumPool:
    def __init__(self, tile):
        self.tile = tile
        self.bufs = [Buf("ps%d" % i) for i in range(8)]
        self.i = 0
        self.nrot = 7

    def next(self):
        i = self.i
        self.i = (i + 1) % self.nrot
        return self.tile[:, i, :], self.bufs[i]

    def modbank(self):
        return self.tile[:, 7, :], self.bufs[7]


def token_blocks(with_ctx):
    blks = [(0, NCTX, 1)] if with_ctx else []
    for i in range(4):
        blks.append((NCTX + 512 * i, 512, 0))
    return blks


def build(nlayers=DEPTH, dbg=False):
    nc = bass.Bass("TRN2", target_bir_lowering=False)

    def din(name, shape, dt=F32):
        return nc.dram_tensor(name, list(shape), dt, kind="ExternalInput").ap()

    x_d = din("x", [SEQ, D])
    ctx_d = din("ctx", [NCTX, D])
    cvec_d = din("cvec", [128, KC, 2])
    modw_d = din("mod_w", [DEPTH, D, 6 * D])
    modb_d = din("mod_b2", [128, DEPTH, 48, 2])
    lnp_d = din("lnp", [128, DEPTH, 2, 2, KC])
    ffnin_d = din("ffn_w_in", [DEPTH, D, 2 * FH])
    ffnout_d = din("ffn_w_out", [DEPTH, FH, D])
    poolw_d = din("pool_w", [2, 4, 256, 256])
    poolsc_d = din("pool_sc", [128, 2, KC])
    ident_d = din("ident", [128, 128])
    edge_d = din("edgefac", [128, 4, 16])
    hgin_d = din("hgrn_w_in", [D, 5 * D])
    hgout_d = din("hgrn_w_out", [D, D])
    hgnw_d = din("hgrn_nw", [128, KC])
    hglb_d = din("hgrn_lbl", [128, 2, DEPTH, KC])
    mask128_d = din("mask128", [128, 2, 128])
    reset_d = din("reset01", [128, 512])
    naqkv_d = din("na_w_qkv", [D, 3 * D])
    naout_d = din("na_w_out", [D, D])
    nabias_d = din("na_bias", [16, 128, NPAT, 128])
    namask_d = din("na_mask", [128, NPAT, 128], BF16)
    out_d = nc.dram_tensor("out", [SEQ, D], F32, kind="ExternalOutput").ap()
    dbg_d = nc.dram_tensor("dbg", [128, KC, T], F32, kind="ExternalOutput").ap() if dbg else None

    with contextlib.ExitStack() as st:
        fw = FW(nc, st)

        uniq = [0]

        def sb(name, shape, dt=F32, stack=st):
            uniq[0] += 1
            return stack.enter_context(nc.sbuf_tensor("s%d_%s" % (uniq[0], name), list(shape), dt))

        xT = sb("xT", [128, KC, T])
        hT = sb("hT", [128, KC, T], BF16)
        ring_t = sb("ring", [128, NSLOT, SLOTW], BF16)
        modv = sb("modv", [128, DEPTH, 6, KC, 2])
        galpha = sb("galpha", [128, DEPTH, 2, KC, 2])
        lnp = sb("lnp", [128, DEPTH, 2, 2, KC])
        ident = sb("ident", [128, 128])
        identb = sb("identb", [128, 128], BF16)
        onesb = sb("onesb", [128, 128], BF16)
        edgef = sb("edgef", [128, 4, 16])
        poolsc = sb("poolsc", [128, 2, KC])
        cvec = sb("cvec", [128, KC, 2])
        sTb = sb("sTb", [128, KC, 2], BF16)
        modb = sb("modb", [128, DEPTH, 48, 2])
        hgnw = sb("hgnw", [128, KC])
        hglb = sb("hglb", [128, 2, DEPTH, KC])
        lbv = sb("lbv", [128, 4, 2, KC])
        mask128 = sb("mask128", [128, 2, 128])
        reset01 = sb("reset01", [128, 512])
        ps_t = st.enter_context(nc.psum_tensor("ps", [128, 8, 512], F32))
        PS = PsumPool(ps_t)
        ring = Ring(fw, ring_t)

        B_x = [[Buf("x%d_%d" % (c, b)) for b in range(5)] for c in range(KC)]
        B_h = [[Buf("h%d_%d" % (c, b)) for b in range(5)] for c in range(KC)]
        B_const = Buf("const")
        B_modvs = [Buf("modv%d" % i) for i in range(DEPTH)]
        ch_in = fw.new_chan("in")
        ch_outs = [fw.new_chan("out0"), fw.new_chan("out1")]
        ch_dbg = fw.new_chan("dbg")
        ch_mw = [fw.new_chan("mw0"), fw.new_chan("mw1")]
        ch_xin = [fw.new_chan("xin0"), fw.new_chan("xin1")]
        ch_ebs = [fw.new_chan("eb0"), fw.new_chan("eb1")]

        def blk_index(t0):
            return 0 if t0 < NCTX else 1 + (t0 - NCTX) // 512

        def xbufs(t0, chunks=range(KC)):
            return [B_x[c][blk_index(t0)] for c in chunks]

        def hbufs(t0, chunks=range(KC)):
            return [B_h[c][blk_index(t0)] for c in chunks]

        all_x = [b for row in B_x for b in row]
        all_h = [b for row in B_h for b in row]

        def gen():
            ring.reset()
            for dst, src in ((cvec, cvec_d), (lnp, lnp_d), (ident, ident_d), (edgef, edge_d), (poolsc, poolsc_d),
                             (hgnw, hgnw_d), (hglb, hglb_d), (mask128, mask128_d), (reset01, reset_d)):
                fw.dma("sp", ch_in, dst[:], src, writes=[B_const])
            fw.op("dve", lambda e: e.memset(onesb[:], 1.0), writes=[B_const])
            fw.op("act", lambda e: e.activation(out=identb[:], in_=ident[:], func=AF.Copy), reads=[B_const], writes=[B_const])
            fw.dma("sp", ch_in, modb[:], modb_d, writes=[B_const])
            fw.op("act", lambda e: e.activation(out=sTb[:], in_=cvec[:], func=AF.Silu), reads=[B_const], writes=[B_const])

            def mod_compute(l, part, nparts):
                if l >= nlayers:
                    return
                psb, Bp = PS.modbank()
                pv = psb[:, 0:96].rearrange("p (j v) -> p j v", v=2)
                per = 24 // nparts
                for sl in range(part * per, (part + 1) * per):
                    slot, Bs = ring.get([(lambda s_: s_.rearrange("p (k c) -> p k c", k=KC),
                                          modw_d[l, :, sl * 256:(sl + 1) * 256].rearrange("(kc p) c -> p kc c", p=128))])
                    wv = slot.rearrange("p (k c) -> p k c", k=KC)
                    for q in range(2):
                        oc = sl * 2 + q
                        for kc in range(KC):
                            fw.op("pe", lambda e, q=q, kc=kc, oc=oc: e.matmul(
                                pv[:, oc, :], lhsT=wv[:, kc, q * 128:(q + 1) * 128], rhs=sTb[:, kc, :],
                                start=(kc == 0), stop=(kc == KC - 1)),
                                reads=[Bs, B_const], writes=[Bp])
                if part != nparts - 1:
                    return
                Bm = B_modvs[l]
                fw.op("dve", lambda e: e.tensor_tensor(
                    out=modv[:, l].rearrange("p m c v -> p (m c) v"), in0=pv, in1=modb[:, l], op=ALU.add),
                    reads=[Bp, B_const], writes=[Bm])
                for m in (1, 4):
                    fw.op("dve", lambda e, m=m: e.tensor_scalar_add(out=modv[:, l, m], in0=modv[:, l, m], scalar1=1.0),
                          reads=[Bm], writes=[Bm])
                for w_, m in ((0, 2), (1, 5)):
                    fw.op("dve", lambda e, m=m, w_=w_: e.tensor_scalar_mul(
                        out=galpha[:, l, w_], in0=modv[:, l, m], scalar1=1.0 / ALPHA),
                        reads=[Bm], writes=[Bm])
                if l % 3 == 0:
                    j = l // 3
                    fw.op("dve", lambda e, j=j: e.tensor_tensor(
                        out=galpha[:, l, 0], in0=galpha[:, l, 0],
                        in1=poolsc[:, j].unsqueeze(2).broadcast_to((128, KC, 2)), op=ALU.mult),
                        reads=[Bm, B_const], writes=[Bm])

            mod_compute(0, 0, 1)
            with contextlib.ExitStack() as ph:
                xin = sb("xin", [128, 2, D], stack=ph)
                B_xin = [Buf("xin0"), Buf("xin1")]
                for i in range(T // 128):
                    s = i % 2
                    src = ctx_d[i * 128:(i + 1) * 128, :] if i < 2 else x_d[(i - 2) * 128:(i - 1) * 128, :]
                    fw.dma("sp", ch_xin[s], xin[:, s], src, writes=[B_xin[s]])
                    for half in range(2):
                        psb, Bp = PS.next()
                        for q in range(4):
                            c = half * 4 + q
                            fw.op("pe", lambda e, s=s, c=c, q=q, psb=psb: e.transpose(
                                psb[:, q * 128:(q + 1) * 128], xin[:, s, c * 128:(c + 1) * 128], ident[:]),
                                reads=[B_xin[s], B_const], writes=[Bp])
                        eng = "act" if half == 0 else "dve"
                        dst = xT[:, half * 4:half * 4 + 4, i * 128:(i + 1) * 128]
                        srcv = psb.rearrange("p (q t) -> p q t", q=4)
                        if eng == "act":
                            fw.op("act", lambda e, dst=dst, srcv=srcv: e.activation(out=dst, in_=srcv, func=AF.Copy),
                                  reads=[Bp], writes=xbufs(i * 128, range(half * 4, half * 4 + 4)))
                        else:
                            fw.op("dve", lambda e, dst=dst, srcv=srcv: e.tensor_copy(out=dst, in_=srcv),
                                  reads=[Bp], writes=xbufs(i * 128, range(half * 4, half * 4 + 4)))
                fw.barrier()

            def modulate(l, which, blks):
                for (t0, n_, v) in blks:
                    for c in range(KC):
                        fw.op("act", lambda e, c=c, t0=t0, n_=n_, v=v: e.activation(
                            out=hT[:, c, t0:t0 + n_], in_=xT[:, c, t0:t0 + n_], func=AF.Identity,
                            scale=modv[:, l, 3 * which + 1, c, v:v + 1], bias=modv[:, l, 3 * which, c, v:v + 1]),
                            reads=[B_x[c][blk_index(t0)], B_modvs[l]], writes=[B_h[c][blk_index(t0)]])

            def layer_norm(l, which, blks, lnt, next_mod):
                ybf, ysq, st_t = lnt
                for (t0, n_, v) in blks:
                    bi = blk_index(t0)
                    xb = [B_x[c][bi] for c in range(KC)]
                    hb = [B_h[c][bi] for c in range(KC)]
                    B_y, B_q, B_s = ybf[1], ysq[1], st_t[1]
                    fw.op("act", lambda e, t0=t0, n_=n_: e.activation(out=ybf[0][:, :, 0:n_], in_=xT[:, :, t0:t0 + n_], func=AF.Copy),
                          reads=xb, writes=[B_y])
                    fw.op("act", lambda e, t0=t0, n_=n_: e.activation(out=ysq[0][:, :, 0:n_], in_=xT[:, :, t0:t0 + n_], func=AF.Square),
                          reads=xb, writes=[B_q])
                    ps1, Bp1 = PS.next()
                    ps2, Bp2 = PS.next()
                    for c in range(KC):
                        fw.op("pe", lambda e, c=c, n_=n_, ps1=ps1: e.matmul(ps1[:, 0:n_], lhsT=onesb[:], rhs=ybf[0][:, c, 0:n_],
                                                                          start=(c == 0), stop=(c == KC - 1)),
                              reads=[B_y, B_const], writes=[Bp1])
                    for c in range(KC):
                        fw.op("pe", lambda e, c=c, n_=n_, ps2=ps2: e.matmul(ps2[:, 0:n_], lhsT=onesb[:], rhs=ysq[0][:, c, 0:n_],
                                                                          start=(c == 0), stop=(c == KC - 1)),
                              reads=[B_q, B_const], writes=[Bp2])
                    S = st_t[0]
                    mean, msq, rstd, mr = S[:, 0, 0:n_], S[:, 1, 0:n_], S[:, 2, 0:n_], S[:, 3, 0:n_]
                    fw.op("dve", lambda e: e.tensor_scalar_mul(out=mean, in0=ps1[:, 0:n_], scalar1=1.0 / D), reads=[Bp1], writes=[B_s])
                    fw.op("dve", lambda e: e.tensor_tensor(out=msq, in0=mean, in1=mean, op=ALU.mult), reads=[B_s], writes=[B_s])
                    fw.op("dve", lambda e: e.scalar_tensor_tensor(out=msq, in0=ps2[:, 0:n_], scalar=1.0 / D, in1=msq,
                                                                  op0=ALU.mult, op1=ALU.subtract), reads=[Bp2, B_s], writes=[B_s])
                    fw.op("act", lambda e: e.activation(out=rstd, in_=msq, func=AF.Ln, bias=EPS_LN), reads=[B_s], writes=[B_s])
                    fw.op("act", lambda e: e.activation(out=rstd, in_=rstd, func=AF.Exp, scale=-0.5), reads=[B_s], writes=[B_s])
                    fw.op("dve", lambda e: e.tensor_tensor(out=mr, in0=mean, in1=rstd, op=ALU.mult), reads=[B_s], writes=[B_s])
                    xv = xT[:, :, t0:t0 + n_]
                    fw.op("dve", lambda e: e.tensor_tensor(out=xv, in0=xv, in1=rstd.unsqueeze(1).broadcast_to((128, KC, n_)), op=ALU.mult),
                          reads=[B_s] + xb, writes=xb)
                    fw.op("dve", lambda e: e.tensor_tensor(out=xv, in0=xv, in1=mr.unsqueeze(1).broadcast_to((128, KC, n_)), op=ALU.subtract),
                          reads=[B_s] + xb, writes=xb)
                    for c in range(KC):
                        fw.op("dve", lambda e, c=c: e.tensor_scalar(
                            out=xT[:, c, t0:t0 + n_], in0=xT[:, c, t0:t0 + n_],
                            scalar1=lnp[:, l, which, 0, c:c + 1], scalar2=lnp[:, l, which, 1, c:c + 1],
                            op0=ALU.mult, op1=ALU.add), reads=[xb[c], B_const], writes=[xb[c]])
                        if next_mod is not None:
                            nl, nw = next_mod
                            fw.op("act", lambda e, c=c: e.activation(
                                out=hT[:, c, t0:t0 + n_], in_=xT[:, c, t0:t0 + n_], func=AF.Identity,
                                scale=modv[:, nl, 3 * nw + 1, c, v:v + 1], bias=modv[:, nl, 3 * nw, c, v:v + 1]),
                                reads=[xb[c], B_modvs[nl]], writes=[hb[c]])

            def add_branch(psv, l, w_, c, t0, n_, v):
                return lambda e: e.scalar_tensor_tensor(
                    out=xT[:, c, t0:t0 + n_], in0=psv, scalar=galpha[:, l, w_, c, v:v + 1], in1=xT[:, c, t0:t0 + n_],
                    op0=ALU.mult, op1=ALU.add)

            def ffn(l, blks, act_t, sg_t):
                act, B_act = act_t
                slices = [list(range(0, 6)), list(range(6, 12)), list(range(12, 17)), list(range(17, 22))]
                nsg = 0
                for si_, sl in enumerate(slices):
                    for jj, j in enumerate(sl):
                        slot, Bs = ring.get([
                            (lambda s: s[:, 0:1024].rearrange("p (k c) -> p k c", k=KC),
                             ffnin_d[l, :, j * 128:(j + 1) * 128].rearrange("(k p) c -> p k c", p=128)),
                            (lambda s: s[:, 1024:2048].rearrange("p (k c) -> p k c", k=KC),
                             ffnin_d[l, :, FH + j * 128:FH + (j + 1) * 128].rearrange("(k p) c -> p k c", p=128)),
                        ])
                        wg = slot[:, 0:1024].rearrange("p (k c) -> p k c", k=KC)
                        wu = slot[:, 1024:2048].rearrange("p (k c) -> p k c", k=KC)
                        for (t0, n_, v) in blks:
                            bi = blk_index(t0)
                            psg, Bg = PS.next()
                            psu, Bu = PS.next()
                            for k in range(KC):
                                fw.op("pe", lambda e, k=k, psg=psg: e.matmul(psg[:, 0:n_], lhsT=wg[:, k, :], rhs=hT[:, k, t0:t0 + n_],
                                                                          start=(k == 0), stop=(k == KC - 1)),
                                      reads=[Bs, B_h[k][bi]], writes=[Bg])
                            for k in range(KC):
                                fw.op("pe", lambda e, k=k, psu=psu: e.matmul(psu[:, 0:n_], lhsT=wu[:, k, :], rhs=hT[:, k, t0:t0 + n_],
                                                                          start=(k == 0), stop=(k == KC - 1)),
                                      reads=[Bs, B_h[k][bi]], writes=[Bu])
                            sg, Bsg = sg_t[nsg % 2]
                            nsg += 1
                            fw.op("act", lambda e, sg=sg, psg=psg: e.activation(out=sg[:, 0:n_], in_=psg[:, 0:n_], func=AF.Silu),
                                  reads=[Bg], writes=[Bsg])
                            fw.op("dve", lambda e, sg=sg, psu=psu, jj=jj: e.tensor_tensor(
                                out=act[:, jj, t0:t0 + n_], in0=sg[:, 0:n_], in1=psu[:, 0:n_], op=ALU.mult),
                                reads=[Bsg, Bu], writes=[B_act[jj][bi]])
                    wo = []
                    for p0 in range(0, len(sl), 2):
                        js = sl[p0:p0 + 2]
                        slot, Bs = ring.get([
                            (lambda s, q=q: s[:, q * 1024:(q + 1) * 1024], ffnout_d[l, j * 128:(j + 1) * 128, :])
                            for q, j in enumerate(js)])
                        for q in range(len(js)):
                            wo.append((slot[:, q * 1024:(q + 1) * 1024], Bs))
                    for (t0, n_, v) in blks:
                        bi = blk_index(t0)
                        for m in range(KC):
                            pso, Bo = PS.next()
                            for jj in range(len(sl)):
                                wv, Bs = wo[jj]
                                fw.op("pe", lambda e, jj=jj, wv=wv, pso=pso, m=m: e.matmul(
                                    pso[:, 0:n_], lhsT=wv[:, m * 128:(m + 1) * 128], rhs=act[:, jj, t0:t0 + n_],
                                    start=(jj == 0), stop=(jj == len(sl) - 1)),
                                    reads=[Bs, B_act[jj][bi]], writes=[Bo])
                            fw.op("dve", add_branch(pso[:, 0:n_], l, 1, m, t0, n_, v),
                                  reads=[Bo, B_modvs[l], B_x[m][bi]], writes=[B_x[m][bi]])
                    mod_compute(l + 1, si_, 4)

            def pool_mixer(l, blks, ph):
                j = l // 3
                pT = hT
                B_p = B_h
                pad = sb("pad", [128, 2, SEQ + 16], stack=ph)
                tmp = sb("ptmp", [128, 2, SEQ + 16], stack=ph)
                B_pad = [Buf(), Buf()]
                B_tmp = [Buf(), Buf()]
                segs = [(NCTX, SEQ)] + ([(0, NCTX)] if blks[0][2] == 1 else [])
                it = 0
                for c in range(KC):
                    wi = c // 2
                    w = POOL_WINDOWS[wi]
                    for (s0, L) in segs:
                        s = it % 2
                        it += 1
                        sb_ = [B_h[c][blk_index(t)] for t in range(s0, s0 + L, 512)] if L > NCTX else [B_h[c][0]]
                        pb_ = [B_p[c][blk_index(t)] for t in range(s0, s0 + L, 512)] if L > NCTX else [B_p[c][0]]
                        P_ = pad[:, s]
                        Q_ = tmp[:, s]
                        fw.op("dve", lambda e, P_=P_, L=L: e.memset(P_[:, 0:8], 0.0), writes=[B_pad[s]])
                        fw.op("dve", lambda e, P_=P_, L=L: e.memset(P_[:, 8 + L:16 + L], 0.0), writes=[B_pad[s]])
                        fw.op("dve", lambda e, P_=P_, L=L, c=c, s0=s0: e.tensor_copy(out=P_[:, 8:8 + L], in_=hT[:, c, s0:s0 + L]),
                              reads=sb_, writes=[B_pad[s]])
                        src, dst = P_, Q_
                        Bsrc, Bdst = B_pad[s], B_tmp[s]
                        length = L + 16
                        step = 1
                        while step < w:
                            nl_ = length - step
                            fw.op("dve", lambda e, src=src, dst=dst, nl_=nl_, step=step: e.tensor_tensor(
                                out=dst[:, 0:nl_], in0=src[:, 0:nl_], in1=src[:, step:step + nl_], op=ALU.add),
                                reads=[Bsrc], writes=[Bdst])
                            src, dst = dst, src
                            Bsrc, Bdst = Bdst, Bsrc
                            length = nl_
                            step *= 2
                        off = 8 - w // 2
                        fw.op("dve", lambda e, src=src, dst=dst, off=off, L=L, w=w: e.tensor_scalar_mul(
                            out=dst[:, 0:L], in0=src[:, off:off + L], scalar1=1.0 / w), reads=[Bsrc], writes=[Bdst])
                        fw.op("dve", lambda e, dst=dst, wi=wi: e.tensor_tensor(out=dst[:, 0:8], in0=dst[:, 0:8], in1=edgef[:, wi, 0:8], op=ALU.mult),
                              reads=[B_const], writes=[Bdst])
                        fw.op("dve", lambda e, dst=dst, wi=wi, L=L: e.tensor_tensor(out=dst[:, L - 8:L], in0=dst[:, L - 8:L], in1=edgef[:, wi, 8:16], op=ALU.mult),
                              reads=[B_const], writes=[Bdst])
                        fw.op("dve", lambda e, dst=dst, c=c, s0=s0, L=L: e.tensor_tensor(
                            out=pT[:, c, s0:s0 + L], in0=dst[:, 0:L], in1=hT[:, c, s0:s0 + L], op=ALU.subtract),
                            reads=[Bdst] + sb_, writes=pb_)
                slot, Bs = ring.get([
                    (lambda s: s.rearrange("p (g k c) -> p g k c", g=4, k=2),
                     poolw_d[j].rearrange("g (k p) c -> p g k c", p=128))])
                wgp = slot.rearrange("p (g k c) -> p g k c", g=4, k=2)
                for (t0, n_, v) in blks:
                    bi = blk_index(t0)
                    for g in range(4):
                        for mo in range(2):
                            pso, Bo = PS.next()
                            for k in range(2):
                                fw.op("pe", lambda e, g=g, mo=mo, k=k, pso=pso: e.matmul(
                                    pso[:, 0:n_], lhsT=wgp[:, g, k, mo * 128:(mo + 1) * 128], rhs=pT[:, 2 * g + k, t0:t0 + n_],
                                    start=(k == 0), stop=(k == 1)), reads=[Bs, B_p[2 * g + k][bi]], writes=[Bo])
                            c = 2 * g + mo
                            fw.op("dve", add_branch(pso[:, 0:n_], l, 0, c, t0, n_, v),
                                  reads=[Bo, B_modvs[l], B_x[c][bi]], writes=[B_x[c][bi]])

            def hgrn_mixer(l, blks, ph):
                B_lb = Buf()
                E_ = sb("lbE", [128, 2, DEPTH, KC], stack=ph)
                ssum = sb("lbS", [128, 2, KC], stack=ph)
                fw.op("act", lambda e: e.activation(out=E_[:], in_=hglb[:], func=AF.Exp), reads=[B_const], writes=[B_lb])
                fw.op("dve", lambda e: e.tensor_tensor(out=ssum[:], in0=E_[:, :, 0], in1=E_[:, :, 1], op=ALU.add), reads=[B_lb], writes=[B_lb])
                for k in (2, 3):
                    fw.op("dve", lambda e, k=k: e.tensor_tensor(out=ssum[:], in0=ssum[:], in1=E_[:, :, k], op=ALU.add), reads=[B_lb], writes=[B_lb])
                fw.op("dve", lambda e: e.reciprocal(out=ssum[:], in_=ssum[:]), reads=[B_lb], writes=[B_lb])
                fw.op("dve", lambda e: e.tensor_tensor(out=E_[:], in0=E_[:], in1=ssum[:].unsqueeze(2).broadcast_to((128, 2, DEPTH, KC)), op=ALU.mult),
                      reads=[B_lb], writes=[B_lb])
                fw.op("dve", lambda e: e.tensor_copy(out=lbv[:, 0], in_=E_[:, :, 0]), reads=[B_lb], writes=[B_lb])
                for k in range(1, l + 1):
                    fw.op("dve", lambda e, k=k: e.tensor_tensor(out=lbv[:, 0], in0=lbv[:, 0], in1=E_[:, :, k], op=ALU.add), reads=[B_lb], writes=[B_lb])
                fw.op("dve", lambda e: e.tensor_tensor(out=lbv[:, 0], in0=lbv[:, 0], in1=E_[:, :, 0], op=ALU.subtract), reads=[B_lb], writes=[B_lb])
                fw.op("dve", lambda e: e.tensor_scalar(out=lbv[:, 1], in0=lbv[:, 0], scalar1=-1.0, scalar2=1.0, op0=ALU.mult, op1=ALU.add),
                      reads=[B_lb], writes=[B_lb])
                fw.op("dve", lambda e: e.tensor_scalar_add(out=lbv[:, 2], in0=lbv[:, 0], scalar1=-1.0), reads=[B_lb], writes=[B_lb])

                def t(name, shape, dt=F32):
                    return sb(name, shape, dt, stack=ph)
                NT = T // 128
                qT = t("qT", [128, T], BF16)
                vtok = t("vtok", [128, NT, 128], BF16)
                oacc = t("oacc", [128, T])
                B_q = [Buf() for _ in range(5)]
                B_sg = [Buf() for _ in range(5)]
                B_vt = [Buf() for _ in range(5)]
                B_oa = [Buf() for _ in range(5)]
                vblk, B_vblk = t("vblk", [128, 512], BF16), Buf()
                Dd = []
                for di in range(2):
                    d = {}
                    for nm in ("sig", "lgf", "bb", "tmpf", "E1"):
                        d[nm] = (t("%s%d" % (nm, di), [128, 512]), Buf())
                    for nm in ("qt", "kt", "kh"):
                        d[nm] = (t("%s%d" % (nm, di), [128, 512], BF16), Buf())
                    d["khtok"] = (t("khtok%d" % di, [128, 4, 128], BF16), Buf())
                    d["ATm"] = (t("ATm%d" % di, [128, 4, 128], BF16), Buf())
                    d["ebe"] = (t("ebe%d" % di, [128, 2, 8]), Buf())
                    d["SS"] = (t("SS%d" % di, [128, 9, 128]), Buf())
                    d["Sbf"] = (t("Sbf%d" % di, [128, 8, 128], BF16), Buf())
                    d["carry"] = (t("carry%d" % di, [128, 128]), Buf())
                    d["pss"] = [(ps_t[:, 4 + 2 * di + i, :], PS.bufs[4 + 2 * di + i]) for i in range(2)]
                    Dd.append(d)
                rstd, B_rstd = Dd[0]["sig"]
                ontmp, B_ontmp = Dd[0]["lgf"]
                osq, B_osq = Dd[0]["qt"]
                onb, B_onb = Dd[0]["kt"]
                sgb, B_sgb = Dd[0]["kh"]
                PS.nrot = 4
                PS.i = 0

                def wview(s, h):
                    return s[:, h * 1024:(h + 1) * 1024].rearrange("p (k c) -> p k c", k=KC)

                def wsrc(col0):
                    return hgin_d[:, col0:col0 + 128].rearrange("(k p) c -> p k c", p=128)

                def mm8(ps, wv, t0, n_, Bs, bi, Bp):
                    for k in range(KC):
                        fw.op("pe", lambda e, k=k: e.matmul(ps[:, 0:n_], lhsT=wv[:, k, :], rhs=hT[:, k, t0:t0 + n_],
                                                            start=(k == 0), stop=(k == KC - 1)),
                              reads=[Bs, B_h[k][bi]], writes=[Bp])

                for m in range(KC):
                    c0 = m * 128
                    sl1, Bs1 = ring.get([(lambda s: wview(s, 0), wsrc(0 * D + c0)), (lambda s: wview(s, 1), wsrc(1 * D + c0))])
                    sl2, Bs2 = ring.get([(lambda s: wview(s, 0), wsrc(2 * D + c0)), (lambda s: wview(s, 1), wsrc(3 * D + c0))])
                    sl3, Bs3 = ring.get([(lambda s: wview(s, 0), wsrc(4 * D + c0)),
                                         (lambda s: s[:, 1024:2048], hgout_d[c0:c0 + 128, :])])
                    wq, wv_ = wview(sl1, 0), wview(sl1, 1)
                    wz = [wview(sl2, 0), wview(sl2, 1)]
                    wg = wview(sl3, 0)
                    wo = sl3[:, 1024:2048]
                    for (t0, n_, v) in blks:
                        bi = blk_index(t0)
                        nt_ = n_ // 128
                        g0 = t0 // 128
                        ps, Bp = PS.next()
                        mm8(ps, wq, t0, n_, Bs1, bi, Bp)
                        fw.op("act", lambda e: e.activation(out=qT[:, t0:t0 + n_], in_=ps[:, 0:n_], func=AF.Silu), reads=[Bp], writes=[B_q[bi]])
                        ps, Bp = PS.next()
                        mm8(ps, wv_, t0, n_, Bs1, bi, Bp)
                        fw.op("dve", lambda e: e.tensor_copy(out=vblk[:, 0:n_], in_=ps[:, 0:n_]), reads=[Bp], writes=[B_vblk])
                        ps2, Bp2 = PS.next()
                        psb = ps2.bitcast(BF16)
                        for j in range(nt_):
                            fw.op("pe", lambda e, j=j: e.transpose(psb[:, j * 128:(j + 1) * 128], vblk[:, j * 128:(j + 1) * 128], identb[:]),
                                  reads=[B_vblk, B_const], writes=[Bp2])
                        fw.op("act", lambda e: e.activation(out=vtok[:, g0:g0 + nt_, :],
                                                            in_=psb[:, 0:nt_ * 128].rearrange("p (c e) -> p c e", e=128), func=AF.Copy),
                              reads=[Bp2], writes=[B_vt[bi]])
                    for di in range(2):
                        fw.op("dve", lambda e, di=di: e.memset(Dd[di]["carry"][0][:], 0.0), writes=[Dd[di]["carry"][1]])
                    touched = [False] * 5

                    def front(item):
                        di, (t0, n_, v) = item
                        d = Dd[di]
                        bi = blk_index(t0)
                        nch = n_ // 64
                        nt_ = n_ // 128
                        g0 = t0 // 128
                        mid = 31 if di == 0 else 32
                        endc = 63 if di == 0 else 0
                        sig, B_sig = d["sig"]
                        lgf, B_lgf = d["lgf"]
                        bb, B_bb = d["bb"]
                        tmpf, B_tmpf = d["tmpf"]
                        E1, B_E1 = d["E1"]
                        qt_, B_qt = d["qt"]
                        kt_, B_kt = d["kt"]
                        kh_, B_kh = d["kh"]
                        khtok, B_khtok = d["khtok"]
                        ATm, B_AT = d["ATm"]
                        ebe, B_ebe = d["ebe"]
                        ps, Bp = PS.next()
                        mm8(ps, wz[di], t0, n_, Bs2, bi, Bp)
                        yield
                        fw.op("act", lambda e: e.activation(out=sig[:, 0:n_], in_=ps[:, 0:n_], func=AF.Sigmoid, scale=-1.0), reads=[Bp], writes=[B_sig])
                        yield
                        fw.op("act", lambda e: e.activation(out=lgf[:, 0:n_], in_=sig[:, 0:n_], func=AF.Ln,
                                                            scale=lbv[:, 2, di, m:m + 1], bias=1.0), reads=[B_sig, B_lb], writes=[B_lgf])
                        yield
                        fw.op("dve", lambda e: e.tensor_tensor_scan(out=bb[:, 0:n_], data0=reset01[:, 0:n_], data1=lgf[:, 0:n_],
                                                                    initial=0.0, op0=ALU.mult, op1=ALU.add),
                              reads=[B_lgf, B_const], writes=[B_bb])
                        yield
                        if di == 0:
                            bbv, B_bbv = bb, B_bb
                            E2, B_E2 = lgf, B_lgf
                        else:
                            fw.op("dve", lambda e: e.tensor_tensor(out=tmpf[:, 0:n_], in0=lgf[:, 0:n_], in1=bb[:, 0:n_], op=ALU.subtract),
                                  reads=[B_lgf, B_bb], writes=[B_tmpf])
                            yield
                            bb3 = bb[:, 0:n_].rearrange("p (c s) -> p c s", s=64)
                            fw.op("dve", lambda e: e.tensor_tensor(
                                out=lgf[:, 0:n_].rearrange("p (c s) -> p c s", s=64),
                                in0=tmpf[:, 0:n_].rearrange("p (c s) -> p c s", s=64),
                                in1=bb3[:, :, 63:64].broadcast_to((128, nch, 64)), op=ALU.add),
                                reads=[B_tmpf, B_bb], writes=[B_lgf])
                            yield
                            bbv, B_bbv = lgf, B_lgf
                            E2, B_E2 = bb, B_bb
                        bbv3 = bbv[:, 0:n_].rearrange("p (c s) -> p c s", s=64)
                        tm3 = tmpf[:, 0:n_].rearrange("p (c s) -> p c s", s=64)
                        fw.op("act", lambda e: e.activation(out=ebe[:, 0, 0:nch], in_=bbv3[:, :, endc], func=AF.Exp), reads=[B_bbv], writes=[B_ebe])
                        yield
                        fw.op("act", lambda e: e.activation(out=ebe[:, 1, 0:nch], in_=bbv3[:, :, mid], func=AF.Exp), reads=[B_bbv], writes=[B_ebe])
                        yield
                        fw.op("dve", lambda e: e.tensor_tensor(out=tm3, in0=bbv3, in1=bbv3[:, :, mid:mid + 1].broadcast_to((128, nch, 64)),
                                                               op=ALU.subtract), reads=[B_bbv], writes=[B_tmpf])
                        yield
                        fw.op("act", lambda e: e.activation(out=E1[:, 0:n_], in_=tmpf[:, 0:n_], func=AF.Exp), reads=[B_tmpf], writes=[B_E1])
                        yield
                        fw.op("act", lambda e: e.activation(out=E2[:, 0:n_], in_=tmpf[:, 0:n_], func=AF.Exp, scale=-1.0), reads=[B_tmpf], writes=[B_E2])
                        yield
                        fw.op("dve", lambda e: e.tensor_tensor(out=qt_[:, 0:n_], in0=qT[:, t0:t0 + n_], in1=E1[:, 0:n_], op=ALU.mult),
                              reads=[B_q[bi], B_E1], writes=[B_qt])
                        yield
                        fw.op("dve", lambda e: e.scalar_tensor_tensor(out=kt_[:, 0:n_], in0=sig[:, 0:n_], scalar=lbv[:, 1, di, m:m + 1],
                                                                      in1=E2[:, 0:n_], op0=ALU.mult, op1=ALU.mult),
                              reads=[B_sig, B_E2, B_lb], writes=[B_kt])
                        yield
                        fw.op("dve", lambda e: e.tensor_tensor(out=tm3, in0=bbv3, in1=bbv3[:, :, endc:endc + 1].broadcast_to((128, nch, 64)),
                                                               op=ALU.subtract), reads=[B_bbv], writes=[B_tmpf])
                        yield
                        fw.op("act", lambda e: e.activation(out=E1[:, 0:n_], in_=tmpf[:, 0:n_], func=AF.Exp, scale=-1.0), reads=[B_tmpf], writes=[B_E1])
                        yield
                        fw.op("dve", lambda e: e.scalar_tensor_tensor(out=kh_[:, 0:n_], in0=sig[:, 0:n_], scalar=lbv[:, 1, di, m:m + 1],
                                                                      in1=E1[:, 0:n_], op0=ALU.mult, op1=ALU.mult),
                              reads=[B_sig, B_E1, B_lb], writes=[B_kh])
                        yield
                        ps2, Bp2 = PS.next()
                        psb = ps2.bitcast(BF16)
                        for j in range(nt_):
                            fw.op("pe", lambda e, j=j: e.transpose(psb[:, j * 128:(j + 1) * 128], kh_[:, j * 128:(j + 1) * 128], identb[:]),
                                  reads=[B_kh, B_const], writes=[Bp2])
                            yield
                        fw.op("act", lambda e: e.activation(out=khtok[:, 0:nt_, :],
                                                            in_=psb[:, 0:nt_ * 128].rearrange("p (c e) -> p c e", e=128), func=AF.Copy),
                              reads=[Bp2], writes=[B_khtok])
                        yield
                        psa, Ba = PS.next()
                        for j in range(nt_):
                            fw.op("pe", lambda e, j=j: e.matmul(psa[:, j * 128:(j + 1) * 128], lhsT=kt_[:, j * 128:(j + 1) * 128],
                                                                rhs=qt_[:, j * 128:(j + 1) * 128], start=True, stop=True),
                                  reads=[B_kt, B_qt], writes=[Ba])
                            yield
                        fw.op("dve", lambda e: e.tensor_tensor(
                            out=ATm[:, 0:nt_, :], in0=psa[:, 0:n_].rearrange("p (j t) -> p j t", t=128),
                            in1=mask128[:, di, :].unsqueeze(1).broadcast_to((128, nt_, 128)), op=ALU.mult),
                            reads=[Ba, B_const], writes=[B_AT])
                        yield
                        for ch in range(nch):
                            j, hf = ch // 2, ch % 2
                            pss, Bss = d["pss"][hf]
                            fw.op("pe", lambda e, ch=ch, j=j, hf=hf, pss=pss: e.matmul(
                                pss[:, j * 128:(j + 1) * 128], lhsT=khtok[hf * 64:(hf + 1) * 64, j, :],
                                rhs=vtok[hf * 64:(hf + 1) * 64, g0 + j, :], start=True, stop=True),
                                reads=[B_khtok, B_vt[bi]], writes=[Bss])
                            yield

                    def back(item):
                        di, (t0, n_, v) = item
                        d = Dd[di]
                        bi = blk_index(t0)
                        nch = n_ // 64
                        g0 = t0 // 128
                        qt_, B_qt = d["qt"]
                        ATm, B_AT = d["ATm"]
                        ebe, B_ebe = d["ebe"]
                        SS, B_SS = d["SS"]
                        Sbf, B_Sbf = d["Sbf"]
                        carry, B_carry = d["carry"]
                        chs = list(range(nch)) if di == 0 else list(range(nch - 1, -1, -1))
                        for idx, ch in enumerate(chs):
                            jin, jout = (ch, ch + 1) if di == 0 else (ch + 1, ch)
                            pss, Bss = d["pss"][ch % 2]
                            if idx == 0:
                                fw.op("dve", lambda e, jin=jin: e.tensor_copy(out=SS[:, jin, :], in_=carry[:]), reads=[B_carry], writes=[B_SS])
                                yield
                            fw.op("dve", lambda e, ch=ch, jin=jin, jout=jout, pss=pss: e.scalar_tensor_tensor(
                                out=SS[:, jout, :], in0=SS[:, jin, :], scalar=ebe[:, 0, ch:ch + 1],
                                in1=pss[:, (ch // 2) * 128:(ch // 2 + 1) * 128], op0=ALU.mult, op1=ALU.add),
                                reads=[B_SS, B_ebe, Bss], writes=[B_SS])
                            yield
                        jlast = nch if di == 0 else 0
                        fw.op("dve", lambda e: e.tensor_copy(out=carry[:], in_=SS[:, jlast, :]), reads=[B_SS], writes=[B_carry])
                        yield
                        off = 0 if di == 0 else 1
                        fw.op("dve", lambda e: e.tensor_tensor(
                            out=Sbf[:, 0:nch, :], in0=SS[:, off:off + nch, :],
                            in1=ebe[:, 1, 0:nch].unsqueeze(2).broadcast_to((128, nch, 128)), op=ALU.mult),
                            reads=[B_SS, B_ebe], writes=[B_Sbf])
                        yield
                        pso, Bo = PS.next()
                        for ch in range(nch):
                            j, hf = ch // 2, ch % 2
                            c_lo, c_hi = ch * 64, (ch + 1) * 64
                            fw.op("pe", lambda e: e.matmul(pso[:, c_lo:c_hi], lhsT=vtok[:, g0 + j, :], rhs=ATm[:, j, hf * 64:(hf + 1) * 64],
                                                           start=True, stop=False), reads=[B_vt[bi], B_AT], writes=[Bo])
                            yield
                            fw.op("pe", lambda e: e.matmul(pso[:, c_lo:c_hi], lhsT=Sbf[:, ch, :], rhs=qt_[:, c_lo:c_hi],
                                                           start=False, stop=True), reads=[B_Sbf, B_qt], writes=[Bo])
                            yield
                        if not touched[bi]:
                            touched[bi] = True
                            fw.op("act", lambda e: e.activation(out=oacc[:, t0:t0 + n_], in_=pso[:, 0:n_], func=AF.Copy), reads=[Bo], writes=[B_oa[bi]])
                            yield
                        else:
                            fw.op("dve", lambda e: e.tensor_tensor(out=oacc[:, t0:t0 + n_], in0=oacc[:, t0:t0 + n_], in1=pso[:, 0:n_], op=ALU.add),
                                  reads=[Bo, B_oa[bi]], writes=[B_oa[bi]])
                            yield

                    bw_order = [blks[0]] + blks[:0:-1]
                    seq = []
                    for i_ in range(len(blks)):
                        seq.append((0, blks[i_]))
                        seq.append((1, bw_order[i_]))
                    def run_il(*gens):
                        gens = [g_ for g_ in gens if g_ is not None]
                        while gens:
                            for g_ in list(gens):
                                try:
                                    next(g_)
                                except StopIteration:
                                    gens.remove(g_)

                    run_il(front(seq[0]))
                    for k_ in range(len(seq)):
                        run_il(back(seq[k_]), front(seq[k_ + 1]) if k_ + 1 < len(seq) else None)
                    for (t0, n_, v) in blks:
                        bi = blk_index(t0)
                        fw.op("act", lambda e: e.activation(out=osq[:, 0:n_], in_=oacc[:, t0:t0 + n_], func=AF.Square), reads=[B_oa[bi]], writes=[B_osq])
                        ps, Bp = PS.next()
                        fw.op("pe", lambda e: e.matmul(ps[:, 0:n_], lhsT=onesb[:], rhs=osq[:, 0:n_], start=True, stop=True), reads=[B_osq, B_const], writes=[Bp])
                        fw.op("act", lambda e: e.activation(out=rstd[:, 0:n_], in_=ps[:, 0:n_], func=AF.Ln, scale=1.0 / 128, bias=RMS_EPS), reads=[Bp], writes=[B_rstd])
                        fw.op("act", lambda e: e.activation(out=rstd[:, 0:n_], in_=rstd[:, 0:n_], func=AF.Exp, scale=-0.5), reads=[B_rstd], writes=[B_rstd])
                        fw.op("dve", lambda e: e.scalar_tensor_tensor(out=ontmp[:, 0:n_], in0=oacc[:, t0:t0 + n_], scalar=hgnw[:, m:m + 1], in1=rstd[:, 0:n_],
                                                                      op0=ALU.mult, op1=ALU.mult), reads=[B_oa[bi], B_rstd, B_const], writes=[B_ontmp])
                        ps, Bp = PS.next()
                        mm8(ps, wg, t0, n_, Bs3, bi, Bp)
                        fw.op("act", lambda e: e.activation(out=sgb[:, 0:n_], in_=ps[:, 0:n_], func=AF.Silu), reads=[Bp], writes=[B_sgb])
                        fw.op("dve", lambda e: e.tensor_tensor(out=onb[:, 0:n_], in0=ontmp[:, 0:n_], in1=sgb[:, 0:n_], op=ALU.mult),
                              reads=[B_ontmp, B_sgb], writes=[B_onb])
                        for mo in range(KC):
                            ps, Bp = PS.next()
                            fw.op("pe", lambda e: e.matmul(ps[:, 0:n_], lhsT=wo[:, mo * 128:(mo + 1) * 128], rhs=onb[:, 0:n_], start=True, stop=True),
                                  reads=[Bs3, B_onb], writes=[Bp])
                            fw.op("dve", add_branch(ps[:, 0:n_], l, 0, mo, t0, n_, v), reads=[Bp, B_modvs[l], B_x[mo][bi]], writes=[B_x[mo][bi]])
                PS.nrot = 7
                PS.i = 0

            def na_mixer(l, blks, ph):
                def t(name, shape, dt=F32):
                    return sb(name, shape, dt, stack=ph)
                NT = T // 128
                qT = t("naq", [128, T], BF16)
                kT = t("nak", [128, T], BF16)
                vblk, B_vblk = t("navb", [128, 512], BF16), Buf()
                vtok = t("navt", [128, NT, 2, 65], BF16)
                otok = t("naot", [128, NT, 128], BF16)
                oT = t("naoT", [128, T], BF16)
                mask, B_mask = t("namask", [128, NPAT, 128], BF16), Buf()
                EB = [(t("naEB%d" % i, [128, NPAT, 128]), Buf(), ch_ebs[i]) for i in range(2)]
                PT = [(t("naPT%d" % i, [128, 7 * 128], BF16), Buf()) for i in range(4)]
                rden = [(t("narden%d" % i, [128, 1]), Buf()) for i in range(4)]
                B_q = [Buf() for _ in range(NT)]
                B_k = [Buf() for _ in range(NT)]
                B_vt = [Buf() for _ in range(NT)]
                B_ot = [Buf() for _ in range(NT)]
                B_oT = [Buf() for _ in range(5)]
                fw.dma("sp", ch_in, mask[:], namask_d, writes=[B_mask])
                fw.op("dve", lambda e: e.memset(vtok[:, :, :, 64:65], 1.0), writes=B_vt)
                units = []
                for m in range(16):
                    if m < 2:
                        units.append((2 + m, [2, 3, 4, 5], 5 + 4 * m))
                    elif m >= 14:
                        units.append((2 + m, [14, 15, 16, 17], 5 + 4 * (m - 12)))
                    else:
                        units.append((2 + m, [m + i for i in range(5)], 0))
                units.append((0, [], 0))
                units.append((1, [], 0))

                def wview(s, h):
                    return s[:, h * 1024:(h + 1) * 1024].rearrange("p (k c) -> p k c", k=KC)

                def wsrc(col0):
                    return naqkv_d[:, col0:col0 + 128].rearrange("(k p) c -> p k c", p=128)

                def mm8(ps, wv, t0, n_, Bs, bi, Bp):
                    for k in range(KC):
                        fw.op("pe", lambda e, k=k: e.matmul(ps[:, 0:n_], lhsT=wv[:, k, :], rhs=hT[:, k, t0:t0 + n_],
                                                            start=(k == 0), stop=(k == KC - 1)),
                              reads=[Bs, B_h[k][bi]], writes=[Bp])
                nu = 0
                for mp in range(KC):
                    c0 = mp * 128
                    sl1, Bs1 = ring.get([(lambda s: wview(s, 0), wsrc(c0)), (lambda s: wview(s, 1), wsrc(D + c0))])
                    sl2, Bs2 = ring.get([(lambda s: wview(s, 0), wsrc(2 * D + c0)),
                                         (lambda s: s[:, 1024:2048], naout_d[c0:c0 + 128, :])])
                    wq, wk, wv_ = wview(sl1, 0), wview(sl1, 1), wview(sl2, 0)
                    wo = sl2[:, 1024:2048]
                    for (t0, n_, v) in blks:
                        bi = blk_index(t0)
                        nt_ = n_ // 128
                        g0 = t0 // 128
                        tb_ = list(range(g0, g0 + nt_))
                        ps, Bp = PS.next()
                        mm8(ps, wq, t0, n_, Bs1, bi, Bp)
                        fw.op("act", lambda e: e.activation(out=qT[:, t0:t0 + n_], in_=ps[:, 0:n_], func=AF.Copy), reads=[Bp], writes=[B_q[i] for i in tb_])
                        ps, Bp = PS.next()
                        mm8(ps, wk, t0, n_, Bs1, bi, Bp)
                        fw.op("dve", lambda e: e.tensor_copy(out=kT[:, t0:t0 + n_], in_=ps[:, 0:n_]), reads=[Bp], writes=[B_k[i] for i in tb_])
                        ps, Bp = PS.next()
                        mm8(ps, wv_, t0, n_, Bs2, bi, Bp)
                        fw.op("act", lambda e: e.activation(out=vblk[:, 0:n_], in_=ps[:, 0:n_], func=AF.Copy), reads=[Bp], writes=[B_vblk])
                        ps2, Bp2 = PS.next()
                        psb = ps2.bitcast(BF16)
                        for j in range(nt_):
                            fw.op("pe", lambda e, j=j: e.transpose(psb[:, j * 128:(j + 1) * 128], vblk[:, j * 128:(j + 1) * 128], identb[:]),
                                  reads=[B_vblk, B_const], writes=[Bp2])
                        for hh in range(2):
                            fw.op("dve", lambda e, hh=hh: e.tensor_copy(
                                out=vtok[:, g0:g0 + nt_, hh, 0:64],
                                in_=psb[:, 0:nt_ * 128].rearrange("p (j h e) -> p j h e", h=2, e=64)[:, :, hh, :]),
                                reads=[Bp2], writes=[B_vt[i] for i in tb_])
                    for hh in range(2):
                        h = 2 * mp + hh
                        pb = hh * 64
                        eb, B_eb, ch_eb = EB[h % 2]
                        fw.dma("sp", ch_eb, eb[:], nabias_d[h], writes=[B_eb])
                        fw.op("act", lambda e: e.activation(out=eb[:], in_=eb[:], func=AF.Exp), reads=[B_eb], writes=[B_eb])
                        fw.op("dve", lambda e: e.tensor_tensor(out=eb[:], in0=eb[:], in1=mask[:], op=ALU.mult), reads=[B_eb, B_mask], writes=[B_eb])
                        def front(u):
                            qt, ktiles, p0 = units[u]
                            nk = len(ktiles)
                            tiles = ktiles + [0, 1]
                            ntl = len(tiles)
                            pt, B_pt = PT[u % len(PT)]
                            banks = []
                            for j, kt_i in enumerate(tiles):
                                if j % 4 == 0:
                                    banks.append(PS.next())
                                psx, Bx = banks[-1]
                                jj = j % 4
                                fw.op("pe", lambda e, psx=psx, jj=jj, kt_i=kt_i: e.matmul(
                                    psx[:, jj * 128:(jj + 1) * 128], lhsT=kT[pb:pb + 64, kt_i * 128:(kt_i + 1) * 128],
                                    rhs=qT[pb:pb + 64, qt * 128:(qt + 1) * 128], start=True, stop=True),
                                    reads=[B_k[kt_i], B_q[qt]], writes=[Bx])
                            for bidx, (psx, Bx) in enumerate(banks):
                                w_ = min(4, ntl - 4 * bidx) * 128
                                fw.op("act", lambda e, psx=psx, w_=w_, bidx=bidx: e.activation(
                                    out=pt[:, bidx * 512:bidx * 512 + w_], in_=psx[:, 0:w_], func=AF.Exp, scale=0.125),
                                    reads=[Bx], writes=[B_pt])
                            if nk:
                                fw.op("dve", lambda e: e.tensor_tensor(
                                    out=pt[:, 0:nk * 128], in0=pt[:, 0:nk * 128],
                                    in1=eb[:, p0:p0 + nk, :].rearrange("p a b -> p (a b)"), op=ALU.mult),
                                    reads=[B_pt, B_eb], writes=[B_pt])

                        def back(u):
                            qt, ktiles, p0 = units[u]
                            tiles = ktiles + [0, 1]
                            ntl = len(tiles)
                            pt, B_pt = PT[u % len(PT)]
                            rd, B_rd = rden[u % len(rden)]
                            pso, Bo = PS.next()
                            for j, kt_i in enumerate(tiles):
                                fw.op("pe", lambda e, j=j, kt_i=kt_i: e.matmul(
                                    pso[:, 0:65], lhsT=pt[:, j * 128:(j + 1) * 128], rhs=vtok[:, kt_i, hh, :],
                                    start=(j == 0), stop=(j == ntl - 1)), reads=[B_pt, B_vt[kt_i]], writes=[Bo])
                            fw.op("dve", lambda e: e.reciprocal(out=rd[:], in_=pso[:, 64:65]), reads=[Bo], writes=[B_rd])
                            fw.op("act", lambda e: e.activation(out=otok[:, qt, pb:pb + 64], in_=pso[:, 0:64], func=AF.Identity, scale=rd[:, 0:1]),
                                  reads=[Bo, B_rd], writes=[B_ot[qt]])

                        NSK = 2
                        for u in range(min(NSK, len(units))):
                            front(u)
                        for u in range(len(units)):
                            if u + NSK < len(units):
                                front(u + NSK)
                            back(u)
                    for (t0, n_, v) in blks:
                        bi = blk_index(t0)
                        nt_ = n_ // 128
                        g0 = t0 // 128
                        ps2, Bp2 = PS.next()
                        psb = ps2.bitcast(BF16)
                        for j in range(nt_):
                            fw.op("pe", lambda e, j=j: e.transpose(psb[:, j * 128:(j + 1) * 128], otok[:, g0 + j, :], identb[:]),
                                  reads=[B_ot[g0 + j], B_const], writes=[Bp2])
                        fw.op("act", lambda e: e.activation(out=oT[:, t0:t0 + n_], in_=psb[:, 0:n_], func=AF.Copy), reads=[Bp2], writes=[B_oT[bi]])
                        for mo in range(KC):
                            ps, Bp = PS.next()
                            fw.op("pe", lambda e: e.matmul(ps[:, 0:n_], lhsT=wo[:, mo * 128:(mo + 1) * 128], rhs=oT[:, t0:t0 + n_], start=True, stop=True),
                                  reads=[Bs2, B_oT[bi]], writes=[Bp])
                            fw.op("dve", add_branch(ps[:, 0:n_], l, 0, mo, t0, n_, v), reads=[Bp, B_modvs[l], B_x[mo][bi]], writes=[B_x[mo][bi]])

            for l in range(nlayers):
                last = l == DEPTH - 1
                kind = l % 3
                blks = token_blocks(not last)
                if l == 0:
                    modulate(0, 0, blks)
                with contextlib.ExitStack() as ph:
                    if kind == 0:
                        pool_mixer(l, blks, ph)
                    elif kind == 1:
                        hgrn_mixer(l, blks, ph)
                    else:
                        na_mixer(l, blks, ph)
                    fw.barrier()
                with contextlib.ExitStack() as ph:
                    ybf = (sb("ybf", [128, KC, 512], BF16, stack=ph), Buf())
                    ysq = (sb("ysq", [128, KC, 512], BF16, stack=ph), Buf())
                    stt = (sb("lnst", [128, 4, 512], stack=ph), Buf())
                    act = (sb("act", [128, 6, T], BF16, stack=ph), [[Buf() for _ in range(5)] for _ in range(6)])
                    sg = [(sb("sg%d" % i, [128, 512], BF16, stack=ph), Buf()) for i in range(2)]
                    layer_norm(l, 0, blks, (ybf, ysq, stt), (l, 1))
                    ffn(l, blks, act, sg)
                    nxt = None if last else (l + 1, 0)
                    layer_norm(l, 1, blks, (ybf, ysq, stt), nxt)
                    fw.barrier()
            if dbg:
                fw.dma("sp", ch_dbg, dbg_d, xT[:], reads=all_x)
            with contextlib.ExitStack() as ph:
                xo = sb("xo", [128, 2, D], stack=ph)
                B_xo = [Buf(), Buf()]
                for i in range(SEQ // 128):
                    s = i % 2
                    tt = NCTX + i * 128
                    for half in range(2):
                        psb, Bp = PS.next()
                        for q in range(4):
                            c = half * 4 + q
                            fw.op("pe", lambda e, c=c, q=q, psb=psb, tt=tt: e.transpose(
                                psb[:, q * 128:(q + 1) * 128], xT[:, c, tt:tt + 128], ident[:]),
                                reads=[B_x[c][blk_index(tt)], B_const], writes=[Bp])
                        dst = xo[:, s, half * 512:(half + 1) * 512]
                        if half == 0:
                            fw.op("act", lambda e, dst=dst, psb=psb: e.activation(out=dst, in_=psb, func=AF.Copy), reads=[Bp], writes=[B_xo[s]])
                        else:
                            fw.op("dve", lambda e, dst=dst, psb=psb: e.tensor_copy(out=dst, in_=psb), reads=[Bp], writes=[B_xo[s]])
                    fw.dma("sp", ch_outs[s], out_d[i * 128:(i + 1) * 128, :], xo[:, s], reads=[B_xo[s]])
                if not fw.dry:
                    fw._wait_for(fw.E["sp"], [(c_[0], c_[1], {}, "dma") for c_ in ch_outs + [ch_dbg] if c_[1]])

        fw.dry = True
        gen()
        fw.dry = False
        PS.i = 0
        PS.nrot = 7
        gen()
    return nc


def na_patterns():
    pats = [(8, 4 + 2 * i) for i in range(5)]
    for qr0, krs in ((0, (0, 2, 4, 6)), (2, (0, 2, 4, 6)), (28, (24, 26, 28, 30)), (30, (24, 26, 28, 30))):
        for kr0 in krs:
            pats.append((qr0, kr0))
    kk = np.arange(128)
    kro, kc = (kk // 64)[:, None], (kk % 64)[:, None]
    qro, qc = (kk // 64)[None, :], (kk % 64)[None, :]
    drow = np.zeros((NPAT, 128, 128), np.int64)
    dcol = np.zeros((NPAT, 128, 128), np.int64)
    valid = np.zeros((NPAT, 128, 128), bool)
    for p, (qr0, kr0) in enumerate(pats):
        kr = kr0 + kro
        qr = qr0 + qro
        rs = np.clip(qr - 4, 0, 24)
        ws = np.clip(qc - 8, 0, 48)
        valid[p] = (kr >= rs) & (kr < rs + 8) & (kc >= ws) & (kc < ws + 16)
        drow[p] = np.clip(kr - qr + 7, 0, 14)
        dcol[p] = np.clip(kc - qc + 15, 0, 30)
    return drow, dcol, valid


def host_inputs(inputs, nlayers=DEPTH):
    f32 = np.float32
    drow, dcol, valid = na_patterns()
    rpb = np.asarray(inputs["na_rpb"], f32)[0]
    na_bias = rpb[:, drow, dcol].transpose(0, 2, 1, 3)
    na_mask = valid.transpose(1, 0, 2).astype(ml_dtypes.bfloat16)
    x = np.asarray(inputs["x"], f32)
    c = np.asarray(inputs["c"], f32)
    ctx = np.asarray(inputs["ctx"], f32)
    c_ctx = np.asarray(inputs["c_ctx"], f32)
    mod_b = np.asarray(inputs["mod_b"], f32)
    modb2 = np.repeat(mod_b.reshape(DEPTH, 48, 128).transpose(2, 0, 1)[..., None], 2, axis=-1)
    lnp = np.stack([np.asarray(inputs["ln_g"], f32), np.asarray(inputs["ln_b"], f32)], axis=2)
    lnp = lnp.reshape(DEPTH, 2, 2, KC, 128).transpose(4, 0, 1, 2, 3)
    poolsc = np.asarray(inputs["pool_scale"], f32).reshape(2, KC, 128).transpose(2, 0, 1)
    edge = np.ones((4, 16), f32)
    for wi, w in enumerate(POOL_WINDOWS):
        lo = w // 2
        hi = w - 1 - lo
        for t in range(8):
            cnt = min(t + hi + 1, 10 ** 6) - max(t - lo, 0)
            edge[wi, t] = f32(w) / f32(cnt)
            tr = 7 - t
            cnt = min(tr + lo + 1, w) if tr < hi else w
            edge[wi, 8 + t] = f32(w) / f32(cnt)
    edge = np.broadcast_to(edge[None], (128, 4, 16))
    common = {
        "mod_w": np.ascontiguousarray(inputs["mod_w"], f32),
        "mod_b2": np.ascontiguousarray(modb2),
        "lnp": np.ascontiguousarray(lnp),
        "ffn_w_in": np.ascontiguousarray(inputs["ffn_w_in"], f32),
        "ffn_w_out": np.ascontiguousarray(inputs["ffn_w_out"], f32),
        "pool_w": np.ascontiguousarray(inputs["pool_w"], f32),
        "pool_sc": np.ascontiguousarray(poolsc),
        "hgrn_w_in": np.ascontiguousarray(np.asarray(inputs["hgrn_w_in"], f32)[0]),
        "hgrn_w_out": np.ascontiguousarray(np.asarray(inputs["hgrn_w_out"], f32)[0]),
        "hgrn_nw": np.ascontiguousarray(np.asarray(inputs["hgrn_norm_w"], f32)[0].reshape(KC, 128).T),
        "hgrn_lbl": np.ascontiguousarray(np.asarray(inputs["hgrn_lb_logits"], f32).reshape(2, DEPTH, KC, 128).transpose(3, 0, 1, 2)),
        "mask128": np.ascontiguousarray(np.stack([np.kron(np.eye(2, dtype=f32), np.triu(np.ones((64, 64), f32))),
                                                  np.kron(np.eye(2, dtype=f32), np.tril(np.ones((64, 64), f32)))], axis=1)),
        "reset01": np.ascontiguousarray(np.broadcast_to((np.arange(512) % 64 != 0).astype(f32)[None], (128, 512))),
        "na_w_qkv": np.ascontiguousarray(np.asarray(inputs["na_w_qkv"], f32)[0]),
        "na_w_out": np.ascontiguousarray(np.asarray(inputs["na_w_out"], f32)[0]),
        "na_bias": np.ascontiguousarray(na_bias),
        "na_mask": np.ascontiguousarray(na_mask),
        "ident": np.eye(128, dtype=f32),
        "edgefac": np.ascontiguousarray(edge),
    }
    maps = []
    for b in range(x.shape[0]):
        cv = np.stack([c[b], c_ctx], axis=-1).reshape(KC, 128, 2).transpose(1, 0, 2)
        m = dict(common)
        m["x"] = np.ascontiguousarray(x[b])
        m["ctx"] = np.ascontiguousarray(ctx[b])
        m["cvec"] = np.ascontiguousarray(cv)
        maps.append(m)
    return maps


_NC_CACHE = {}


def kernel(**inputs):
    maps = host_inputs(inputs)
    if "nc" not in _NC_CACHE:
        _NC_CACHE["nc"] = build()
    nc = _NC_CACHE["nc"]
    res = run_bass_kernel_spmd(nc, maps, core_ids=list(range(len(maps))))
    return np.stack([r["out"] for r in res.results], axis=0).astype(np.float32)
```
